# Optimizing a Trainium2 kernel written in Bass

```python
import math
import jax
import jax.numpy as jnp
from jax import lax
import numpy as np

D_MODEL = 1024
BATCH = 8
SEQ = 8192
DEPTH = 2

GRID_W = 64
CTX_LEN = 256
EPS = 1e-6

SSM_WIDTH = D_MODEL // 2
SSM_GROUP = 16
SSM_GROUPS = SSM_WIDTH // SSM_GROUP
SSM_STATE = 64

DN_HEADS = 4
DN_DK = 128
DN_DV = 128
DN_CONV = 5
DN_CHUNK = 64

AT_HEADS = 8
AT_KV = 2
AT_HD = 64
WINDOW = 128
AT_BLOCK = 128
ROPE_BASE = 10000.0

D_FF = 4 * D_MODEL
N_BRANCH = 3
N_MOD = 6

IN_SPLITS = (SSM_WIDTH, DN_HEADS * DN_DK, DN_HEADS * DN_DK, DN_HEADS * DN_DV, DN_HEADS * DN_DV,
             2 * DN_HEADS, 2 * DN_HEADS, AT_HEADS * AT_HD, AT_KV * AT_HD, AT_KV * AT_HD, N_BRANCH * D_MODEL)
D_IN = sum(IN_SPLITS)

kernel_name = "hybrid_s5_deltanet_swa_dit_block"


def rmsnorm(x, g):
    xf = x.astype(jnp.float32)
    y = xf * lax.rsqrt(jnp.mean(xf * xf, axis=-1, keepdims=True) + EPS)
    return (y * g.astype(jnp.float32)).astype(x.dtype)


def l2norm(x):
    return x * lax.rsqrt(jnp.sum(x * x, axis=-1, keepdims=True) + EPS)


def split_cols(t, sizes):
    idx = np.cumsum(np.array(sizes))[:-1].tolist()
    return jnp.split(t, idx, axis=-1)


def adaln_modulation(cond, w_mod, b_mod):
    m = jax.nn.silu(cond) @ w_mod + b_mod
    return jnp.split(m[..., None, :], N_MOD, axis=-1)


def s5_discretize(lam_re, lam_im, log_dt, b_re, b_im):
    lam_re = lam_re.astype(jnp.float32)
    lam_im = lam_im.astype(jnp.float32)
    b_re = b_re.astype(jnp.float32)
    b_im = b_im.astype(jnp.float32)
    dt = jnp.exp(log_dt.astype(jnp.float32))[:, None]
    mag = jnp.exp(lam_re * dt)
    a_re = mag * jnp.cos(lam_im * dt)
    a_im = mag * jnp.sin(lam_im * dt)
    den = lam_re * lam_re + lam_im * lam_im
    f_re = ((a_re - 1.0) * lam_re + a_im * lam_im) / den
    f_im = (a_im * lam_re - (a_re - 1.0) * lam_im) / den
    bb_re = f_re[..., None] * b_re - f_im[..., None] * b_im
    bb_im = f_re[..., None] * b_im + f_im[..., None] * b_re
    return a_re, a_im, bb_re, bb_im


def complex_affine_combine(e1, e2):
    a1r, a1i, b1r, b1i = e1
    a2r, a2i, b2r, b2i = e2
    return (a2r * a1r - a2i * a1i,
            a2r * a1i + a2i * a1r,
            a2r * b1r - a2i * b1i + b2r,
            a2r * b1i + a2i * b1r + b2i)


def s5_scan(a_re, a_im, bb_re, bb_im, u, s0_re, s0_im, reverse):
    bu_re = jnp.einsum('blgh,gph->blgp', u, bb_re)
    bu_im = jnp.einsum('blgh,gph->blgp', u, bb_im)
    first = -1 if reverse else 0
    bu_re = bu_re.at[:, first].add(a_re * s0_re - a_im * s0_im)
    bu_im = bu_im.at[:, first].add(a_re * s0_im + a_im * s0_re)
    L = u.shape[1]
    ar = jnp.broadcast_to(a_re, (1, L) + a_re.shape)
    ai = jnp.broadcast_to(a_im, (1, L) + a_im.shape)
    _, _, s_re, s_im = lax.associative_scan(complex_affine_combine, (ar, ai, bu_re, bu_im),
                                            reverse=reverse, axis=1)
    return s_re, s_im


def s5_readout(s_re, s_im, c_re, c_im):
    return jnp.einsum('blgp,ghp->blgh', s_re, c_re) - jnp.einsum('blgp,ghp->blgh', s_im, c_im)


def s5_glu(y, w_glu):
    z = jax.nn.gelu(y)
    return z * jax.nn.sigmoid(z @ w_glu.astype(jnp.float32))


def s5_branch(u, u_c, lam_re, lam_im, log_dt, b_re, b_im, c_re, c_im, d_skip, w_glu, with_ctx_out):
    Bn, L, _ = u.shape
    Lc = u_c.shape[1]
    ug = u.astype(jnp.float32).reshape(Bn, L, SSM_GROUPS, SSM_GROUP)
    ucg = u_c.astype(jnp.float32).reshape(Bn, Lc, SSM_GROUPS, SSM_GROUP)
    d = d_skip.astype(jnp.float32).reshape(SSM_GROUPS, SSM_GROUP)
    y = ug * d
    y_c = ucg * d if with_ctx_out else None
    zero = jnp.zeros((Bn, SSM_GROUPS, SSM_STATE), jnp.float32)
    for direction in range(2):
        rev = direction == 1
        a_re, a_im, bb_re, bb_im = s5_discretize(lam_re[direction], lam_im[direction], log_dt[direction],
                                                 b_re[direction], b_im[direction])
        cr = c_re[direction].astype(jnp.float32)
        ci = c_im[direction].astype(jnp.float32)
        sc_re, sc_im = s5_scan(a_re, a_im, bb_re, bb_im, ucg, zero, zero, rev)
        end = 0 if rev else -1
        s_re, s_im = s5_scan(a_re, a_im, bb_re, bb_im, ug, sc_re[:, end], sc_im[:, end], rev)
        y = y + s5_readout(s_re, s_im, cr, ci)
        if with_ctx_out:
            y_c = y_c + s5_readout(sc_re, sc_im, cr, ci)
    out = s5_glu(y.reshape(Bn, L, SSM_WIDTH), w_glu).astype(u.dtype)
    out_c = s5_glu(y_c.reshape(Bn, Lc, SSM_WIDTH), w_glu).astype(u.dtype) if with_ctx_out else None
    return out, out_c


def dn_prepare(q, k, v, b, a, conv_w, a_log, dt_bias):
    Bn, L, _ = q.shape
    qkv = jnp.concatenate([q, k, v], axis=-1)
    qkv = lax.conv_general_dilated(qkv, conv_w[:, None, :].astype(qkv.dtype), window_strides=(1,),
                                   padding=[(DN_CONV // 2, DN_CONV // 2)],
                                   dimension_numbers=('NWC', 'WIO', 'NWC'),
                                   feature_group_count=qkv.shape[-1])
    qkv = jax.nn.silu(qkv.astype(jnp.float32))
    qf, kf, vf = jnp.split(qkv, [DN_HEADS * DN_DK, 2 * DN_HEADS * DN_DK], axis=-1)
    qf = l2norm(qf.reshape(Bn, L, DN_HEADS, DN_DK)) * (DN_DK ** -0.5)
    kf = l2norm(kf.reshape(Bn, L, DN_HEADS, DN_DK))
    vf = vf.reshape(Bn, L, DN_HEADS, DN_DV)
    beta = jax.nn.sigmoid(b.astype(jnp.float32)).reshape(Bn, L, 2, DN_HEADS)
    g = -jnp.exp(a_log.astype(jnp.float32)) * jax.nn.softplus(
        a.astype(jnp.float32).reshape(Bn, L, 2, DN_HEADS) + dt_bias.astype(jnp.float32))
    return qf, kf, vf, beta, g


def gated_delta_chunked(q, k, v, beta, g, s0):
    Bn, L, H, dk = q.shape
    dv = v.shape[-1]
    n = L // DN_CHUNK
    C = DN_CHUNK

    def chunks(t):
        return t.reshape(Bn, n, C, H, t.shape[-1]).transpose(1, 0, 3, 2, 4)

    qc, kc, vc = chunks(q), chunks(k), chunks(v)
    bc = beta.reshape(Bn, n, C, H).transpose(1, 0, 3, 2)
    gc = jnp.cumsum(g.reshape(Bn, n, C, H).transpose(1, 0, 3, 2), axis=-1)
    tri = jnp.tril(jnp.ones((C, C), dtype=bool))
    strict = jnp.tril(jnp.ones((C, C), dtype=bool), -1)
    decay = jnp.exp(jnp.where(tri, gc[..., :, None] - gc[..., None, :], -jnp.inf))
    kb = kc * bc[..., None]
    eye = jnp.eye(C, dtype=jnp.float32)
    a_mat = eye + jnp.where(strict, jnp.einsum('nbhik,nbhjk->nbhij', kb, kc) * decay, 0.0)
    rhs = jnp.concatenate([vc * bc[..., None], kb * jnp.exp(gc)[..., None]], axis=-1)
    sol = lax.linalg.triangular_solve(a_mat, rhs, left_side=True, lower=True, unit_diagonal=True)
    u_c, w_c = sol[..., :dv], sol[..., dv:]
    qk = jnp.einsum('nbhik,nbhjk->nbhij', qc, kc) * decay

    def step(S, inp):
        q_i, k_i, u_i, w_i, qk_i, g_i = inp
        g_end = g_i[..., -1]
        v_new = u_i - jnp.einsum('bhck,bhkv->bhcv', w_i, S)
        o = (jnp.einsum('bhck,bhkv->bhcv', q_i * jnp.exp(g_i)[..., None], S)
             + jnp.einsum('bhcs,bhsv->bhcv', qk_i, v_new))
        k_dec = k_i * jnp.exp(g_end[..., None] - g_i)[..., None]
        S = S * jnp.exp(g_end)[..., None, None] + jnp.einsum('bhck,bhcv->bhkv', k_dec, v_new)
        return S, o

    S, o = lax.scan(step, s0, (qc, kc, u_c, w_c, qk, gc))
    o = o.transpose(1, 0, 3, 2, 4).reshape(Bn, L, H, dv)
    return o, S


def orient(t, reverse):
    return jnp.flip(t, axis=1) if reverse else t


def dn_output(o, z, norm_g):
    Bn, L = o.shape[:2]
    zf = z.astype(jnp.float32).reshape(Bn, L, DN_HEADS, DN_DV)
    y = rmsnorm(o, norm_g) * jax.nn.silu(zf)
    return y.reshape(Bn, L, DN_HEADS * DN_DV).astype(z.dtype)


def gated_deltanet_branch(q, k, v, z, b, a, q_c, k_c, v_c, z_c, b_c, a_c,
                          conv_w, a_log, dt_bias, norm_g, with_ctx_out):
    Bn = q.shape[0]
    lq, lk, lv, lbeta, lg = dn_prepare(q, k, v, b, a, conv_w, a_log, dt_bias)
    cq, ck, cv, cbeta, cg = dn_prepare(q_c, k_c, v_c, b_c, a_c, conv_w, a_log, dt_bias)
    zero = jnp.zeros((Bn, DN_HEADS, DN_DK, DN_DV), jnp.float32)
    o = jnp.zeros(lv.shape, jnp.float32)
    o_c = jnp.zeros(cv.shape, jnp.float32)
    for direction in range(2):
        rev = direction == 1
        oc_d, s_ctx = gated_delta_chunked(orient(cq, rev), orient(ck, rev), orient(cv, rev),
                                          orient(cbeta[:, :, direction], rev),
                                          orient(cg[:, :, direction], rev), zero)
        ol_d, _ = gated_delta_chunked(orient(lq, rev), orient(lk, rev), orient(lv, rev),
                                      orient(lbeta[:, :, direction], rev),
                                      orient(lg[:, :, direction], rev), s_ctx)
        o = o + orient(ol_d, rev)
        if with_ctx_out:
            o_c = o_c + orient(oc_d, rev)
    out = dn_output(o, z, norm_g)
    out_c = dn_output(o_c, z_c, norm_g) if with_ctx_out else None
    return out, out_c


def rope_1d(x, pos):
    n = x.shape[-1] // 2
    inv_freq = ROPE_BASE ** (-jnp.arange(n, dtype=jnp.float32) / n)
    ang = pos.astype(jnp.float32)[:, None] * inv_freq[None, :]
    cos = jnp.cos(ang)[None, :, None, :]
    sin = jnp.sin(ang)[None, :, None, :]
    xf = x.astype(jnp.float32)
    x1, x2 = xf[..., :n], xf[..., n:]
    return jnp.concatenate([x1 * cos - x2 * sin, x2 * cos + x1 * sin], axis=-1).astype(x.dtype)


def axial_rope(x, rows, cols):
    half = x.shape[-1] // 2
    return jnp.concatenate([rope_1d(x[..., :half], rows), rope_1d(x[..., half:], cols)], axis=-1)


def banded_window_attention(q, k, v, k_ctx, v_ctx, sink):
    Bn, L, H, hd = q.shape
    KV = k.shape[2]
    G = H // KV
    Lc = k_ctx.shape[1]
    nb = L // AT_BLOCK
    W3 = 3 * AT_BLOCK
    scale = hd ** -0.5
    qb = q.reshape(Bn, nb, AT_BLOCK, KV, G, hd)

    def band(t):
        tp = jnp.pad(t, ((0, 0), (AT_BLOCK, AT_BLOCK), (0, 0), (0, 0)))
        tp = tp.reshape(Bn, nb + 2, AT_BLOCK, KV, hd)
        return jnp.concatenate([tp[:, :-2], tp[:, 1:-1], tp[:, 2:]], axis=2)

    k_band, v_band = band(k), band(v)
    s_loc = jnp.einsum('bnqhgd,bnkhd->bnhgqk', qb, k_band, preferred_element_type=jnp.float32) * scale
    q_pos = jnp.arange(nb)[:, None] * AT_BLOCK + jnp.arange(AT_BLOCK)[None, :]
    k_pos = (jnp.arange(nb)[:, None] - 1) * AT_BLOCK + jnp.arange(W3)[None, :]
    kp = k_pos[:, None, :]
    valid = (jnp.abs(q_pos[:, :, None] - kp) <= WINDOW) & (kp >= 0) & (kp < L)
    s_loc = jnp.where(valid[None, :, None, None], s_loc, -jnp.inf)
    s_ctx = jnp.einsum('bnqhgd,bkhd->bnhgqk', qb, k_ctx, preferred_element_type=jnp.float32) * scale
    s_sink = jnp.broadcast_to(sink.astype(jnp.float32).reshape(KV, G)[None, None, :, :, None, None],
                              s_loc.shape[:-1] + (1,))
    p = jax.nn.softmax(jnp.concatenate([s_loc, s_ctx, s_sink], axis=-1), axis=-1).astype(v.dtype)
    o = (jnp.einsum('bnhgqk,bnkhd->bnqhgd', p[..., :W3], v_band)
         + jnp.einsum('bnhgqk,bkhd->bnqhgd', p[..., W3:W3 + Lc], v_ctx))
    return o.reshape(Bn, L, H * hd)


def context_attention(q, k, v, sink):
    Bn, Lc, H, hd = q.shape
    KV = k.shape[2]
    G = H // KV
    qg = q.reshape(Bn, Lc, KV, G, hd)
    s = jnp.einsum('bqhgd,bkhd->bhgqk', qg, k, preferred_element_type=jnp.float32) * (hd ** -0.5)
    s_sink = jnp.broadcast_to(sink.astype(jnp.float32).reshape(KV, G)[None, :, :, None, None],
                              s.shape[:-1] + (1,))
    p = jax.nn.softmax(jnp.concatenate([s, s_sink], axis=-1), axis=-1)[..., :Lc].astype(v.dtype)
    o = jnp.einsum('bhgqk,bkhd->bqhgd', p, v)
    return o.reshape(Bn, Lc, H * hd)


def window_attention_branch(q, k, v, q_c, k_c, v_c, sink, rows, cols, with_ctx_out):
    Bn, L, _ = q.shape
    Lc = k_c.shape[1]
    qh = axial_rope(q.reshape(Bn, L, AT_HEADS, AT_HD), rows, cols)
    kh = axial_rope(k.reshape(Bn, L, AT_KV, AT_HD), rows, cols)
    vh = v.reshape(Bn, L, AT_KV, AT_HD)
    kch = k_c.reshape(Bn, Lc, AT_KV, AT_HD)
    vch = v_c.reshape(Bn, Lc, AT_KV, AT_HD)
    out = banded_window_attention(qh, kh, vh, kch, vch, sink)
    out_c = context_attention(q_c.reshape(Bn, Lc, AT_HEADS, AT_HD), kch, vch, sink) if with_ctx_out else None
    return out, out_c


def merge_branches(ya, yb, yc, gate_logits, w_ba, w_bb, w_bc, w_out):
    ga, gb, gc = jnp.split(jax.nn.sigmoid(gate_logits), N_BRANCH, axis=-1)
    m = ga * (ya @ w_ba) + gb * (yb @ w_bb) + gc * (yc @ w_bc)
    return m @ w_out


def sq_relu_mlp(h, w1, w2):
    return jnp.square(jax.nn.relu(h @ w1)) @ w2


def hybrid_mixer(h, h_c, w_in, lam_re, lam_im, log_dt, b_re, b_im, c_re, c_im, d_skip, w_glu,
                 conv_w, a_log, dt_bias, dn_norm_g, sink, w_ba, w_bb, w_bc, w_out,
                 rows, cols, with_ctx_out):
    (u, dq, dk, dv, dz, db, da, aq, ak, av, gates) = split_cols(h @ w_in, IN_SPLITS)
    (u_c, dq_c, dk_c, dv_c, dz_c, db_c, da_c, aq_c, ak_c, av_c, gates_c) = split_cols(h_c @ w_in, IN_SPLITS)
    ya, ya_c = s5_branch(u, u_c, lam_re, lam_im, log_dt, b_re, b_im, c_re, c_im, d_skip, w_glu, with_ctx_out)
    yb, yb_c = gated_deltanet_branch(dq, dk, dv, dz, db, da, dq_c, dk_c, dv_c, dz_c, db_c, da_c,
                                     conv_w, a_log, dt_bias, dn_norm_g, with_ctx_out)
    yc, yc_c = window_attention_branch(aq, ak, av, aq_c, ak_c, av_c, sink, rows, cols, with_ctx_out)
    out = merge_branches(ya, yb, yc, gates, w_ba, w_bb, w_bc, w_out)
    out_c = merge_branches(ya_c, yb_c, yc_c, gates_c, w_ba, w_bb, w_bc, w_out) if with_ctx_out else None
    return out, out_c


def setup_inputs(seed: int = 0) -> dict:
    key = jax.random.key(seed)
    keys = jax.random.split(key, 40)
    counter = iter(range(40))

    def nk():
        return keys[next(counter)]

    def nrm(shape, scale):
        return jax.random.normal(nk(), shape, jnp.float32) * scale

    def unif(shape, lo, hi):
        return jax.random.uniform(nk(), shape, jnp.float32, lo, hi)

    G, P, H = SSM_GROUPS, SSM_STATE, SSM_GROUP
    dn_dt = jnp.exp(unif((DEPTH, 2, DN_HEADS), math.log(1e-3), math.log(1e-1)))
    return {
        'x': nrm((BATCH, SEQ, D_MODEL), 1.0),
        'c': nrm((BATCH, D_MODEL), 1.0),
        'ctx': nrm((BATCH, CTX_LEN, D_MODEL), 1.0),
        'c_ctx': nrm((D_MODEL,), 1.0),
        'norm1_g': 1.0 + nrm((DEPTH, D_MODEL), 0.02),
        'norm2_g': 1.0 + nrm((DEPTH, D_MODEL), 0.02),
        'w_mod': nrm((DEPTH, D_MODEL, N_MOD * D_MODEL), 0.5 * D_MODEL ** -0.5),
        'b_mod': nrm((DEPTH, N_MOD * D_MODEL), 0.02),
        'w_in': nrm((DEPTH, D_MODEL, D_IN), D_MODEL ** -0.5),
        'ssm_lam_re': -0.5 + nrm((DEPTH, 2, G, P), 0.01),
        'ssm_lam_im': math.pi * jnp.arange(P, dtype=jnp.float32) + nrm((DEPTH, 2, G, P), 0.01),
        'ssm_log_dt': unif((DEPTH, 2, G), math.log(1e-3), math.log(1e-1)),
        'ssm_b_re': nrm((DEPTH, 2, G, P, H), (2 * H) ** -0.5),
        'ssm_b_im': nrm((DEPTH, 2, G, P, H), (2 * H) ** -0.5),
        'ssm_c_re': nrm((DEPTH, 2, G, H, P), (2 * P) ** -0.5),
        'ssm_c_im': nrm((DEPTH, 2, G, H, P), (2 * P) ** -0.5),
        'ssm_d': nrm((DEPTH, SSM_WIDTH), 1.0),
        'ssm_w_glu': nrm((DEPTH, SSM_WIDTH, SSM_WIDTH), SSM_WIDTH ** -0.5),
        'dn_conv_w': nrm((DEPTH, DN_CONV, 2 * DN_HEADS * DN_DK + DN_HEADS * DN_DV), DN_CONV ** -0.5),
        'dn_a_log': jnp.log(unif((DEPTH, 2, DN_HEADS), 1.0, 16.0)),
        'dn_dt_bias': dn_dt + jnp.log(-jnp.expm1(-dn_dt)),
        'dn_norm_g': 1.0 + nrm((DEPTH, DN_DV), 0.02),
        'attn_sink': nrm((DEPTH, AT_HEADS), 0.5),
        'w_branch_a': nrm((DEPTH, SSM_WIDTH, D_MODEL), SSM_WIDTH ** -0.5),
        'w_branch_b': nrm((DEPTH, DN_HEADS * DN_DV, D_MODEL), (DN_HEADS * DN_DV) ** -0.5),
        'w_branch_c': nrm((DEPTH, AT_HEADS * AT_HD, D_MODEL), (AT_HEADS * AT_HD) ** -0.5),
        'w_out': nrm((DEPTH, D_MODEL, D_MODEL), D_MODEL ** -0.5),
        'w_ff1': nrm((DEPTH, D_MODEL, D_FF), D_MODEL ** -0.5),
        'w_ff2': nrm((DEPTH, D_FF, D_MODEL), D_FF ** -0.5),
        'final_norm_g': 1.0 + nrm((D_MODEL,), 0.02),
    }


def reference(x, c, ctx, c_ctx, norm1_g, norm2_g, w_mod, b_mod, w_in,
              ssm_lam_re, ssm_lam_im, ssm_log_dt, ssm_b_re, ssm_b_im, ssm_c_re, ssm_c_im, ssm_d, ssm_w_glu,
              dn_conv_w, dn_a_log, dn_dt_bias, dn_norm_g, attn_sink,
              w_branch_a, w_branch_b, w_branch_c, w_out, w_ff1, w_ff2, final_norm_g):
    L = x.shape[1]
    ROWS = L // GRID_W
    rows = jnp.repeat(jnp.arange(ROWS, dtype=jnp.int32), GRID_W)
    cols = jnp.tile(jnp.arange(GRID_W, dtype=jnp.int32), ROWS)
    for layer in range(DEPTH):
        with_ctx_out = layer < DEPTH - 1
        sh1, sc1, g1, sh2, sc2, g2 = adaln_modulation(c, w_mod[layer], b_mod[layer])
        csh1, csc1, cg1, csh2, csc2, cg2 = adaln_modulation(c_ctx, w_mod[layer], b_mod[layer])
        h = rmsnorm(x, norm1_g[layer]) * (1.0 + sc1) + sh1
        h_c = rmsnorm(ctx, norm1_g[layer]) * (1.0 + csc1) + csh1
        mix, mix_c = hybrid_mixer(h, h_c, w_in[layer],
                                  ssm_lam_re[layer], ssm_lam_im[layer], ssm_log_dt[layer],
                                  ssm_b_re[layer], ssm_b_im[layer], ssm_c_re[layer], ssm_c_im[layer],
                                  ssm_d[layer], ssm_w_glu[layer],
                                  dn_conv_w[layer], dn_a_log[layer], dn_dt_bias[layer], dn_norm_g[layer],
                                  attn_sink[layer],
                                  w_branch_a[layer], w_branch_b[layer], w_branch_c[layer], w_out[layer],
                                  rows, cols, with_ctx_out)
        x = x + g1 * mix
        x = x + g2 * sq_relu_mlp(rmsnorm(x, norm2_g[layer]) * (1.0 + sc2) + sh2, w_ff1[layer], w_ff2[layer])
        if with_ctx_out:
            ctx = ctx + cg1 * mix_c
            ctx = ctx + cg2 * sq_relu_mlp(rmsnorm(ctx, norm2_g[layer]) * (1.0 + csc2) + csh2,
                                          w_ff1[layer], w_ff2[layer])
    return rmsnorm(x, final_norm_g)
```

```python
import contextlib
import numpy as np
import concourse.bass as bass
import concourse.mybir as mybir
from concourse.bass_utils import run_bass_kernel_spmd

F32 = mybir.dt.float32
BF16 = mybir.dt.bfloat16
I32 = mybir.dt.int32
AF = mybir.ActivationFunctionType
ALU = mybir.AluOpType
AX = mybir.AxisListType

D = 1024
LC = 256
DEPTH = 2
EPS = 1e-6
D_IN = 6416
ENG = ("pe", "act", "dve", "pool", "sp")
NDMA = 24


class Buf:
    __slots__ = ("w", "r", "ap", "excl")

    def __init__(self, ap=None, excl=False):
        self.w = None
        self.r = []
        self.ap = ap
        self.excl = excl

    def __getitem__(self, k):
        return self.ap[k]


class _Rec:
    def __getattr__(self, name):
        def f(*a, **k):
            return (name, a, k)
        return f


_REC = _Rec()


class Prog:
    def __init__(self, nc):
        self.nc = nc
        self.q = {e: [] for e in ENG}
        self.cnt = {e: 0 for e in ENG}
        self.seen = {e: {} for e in ENG}
        self.dma_q = {"sp": 0, "act": 0, "pool": 0}
        self.limit = None
        self.dma_n = [0] * NDMA

    def _need(self, eng, waits, tok):
        if tok is None:
            return
        key, val = tok
        if key == eng and eng == "pe":
            return
        if self.seen[eng].get(key, 0) >= val:
            return
        if waits.get(key, 0) < val:
            waits[key] = val

    def _deps(self, eng, reads, writes):
        waits = {}
        for b in reads:
            self._need(eng, waits, b.w)
            if b.excl:
                for t in b.r:
                    if t[0] != eng:
                        self._need(eng, waits, t)
        for b in writes:
            self._need(eng, waits, b.w)
            for t in b.r:
                self._need(eng, waits, t)
        for k, v in waits.items():
            self.seen[eng][k] = v
        return waits

    def _mark(self, tok, reads, writes):
        for b in reads:
            b.r.append(tok)
            if len(b.r) > 64:
                b.r = b.r[-48:]
        for b in writes:
            b.w = tok
            b.r = []

    def op(self, eng, fn, reads=(), writes=()):
        if self.limit is not None:
            self.limit -= 1
            if self.limit < 0:
                return None
        waits = self._deps(eng, reads, writes)
        self.cnt[eng] += 1
        tok = (eng, self.cnt[eng])
        self.q[eng].append((list(waits.items()), fn(_REC), (eng, 1)))
        self._mark(tok, reads, writes)
        return tok

    def dma(self, qeng, out_ap, in_ap, reads=(), writes=(), **kw):
        if self.limit is not None:
            self.limit -= 1
            if self.limit < 0:
                return None
        lo, n = {"sp": (0, 12), "act": (12, 6), "pool": (18, 6)}[qeng]
        i = lo + self.dma_q[qeng] % n
        self.dma_q[qeng] += 1
        waits = {}
        if self.dma_n[i] > 0:
            self._need(qeng, waits, (("d", i), 16 * self.dma_n[i]))
        for k, v in waits.items():
            self.seen[qeng][k] = v
        w2 = self._deps(qeng, reads, writes)
        waits.update(w2)
        self.dma_n[i] += 1
        tok = (("d", i), 16 * self.dma_n[i])
        kw2 = dict(kw)
        kw2["out"] = out_ap
        kw2["in_"] = in_ap
        self.q[qeng].append((list(waits.items()), ("dma_start", (), kw2), (("d", i), 16)))
        self._mark(tok, reads, writes)
        return tok

    def barrier(self):
        toks = [(e, self.cnt[e]) for e in ENG if self.cnt[e] > 0]
        toks += [(("d", i), 16 * self.dma_n[i]) for i in range(NDMA) if self.dma_n[i] > 0]
        for e in ENG:
            waits = {}
            for t in toks:
                if t[0] == e:
                    continue
                self._need(e, waits, t)
            for k, v in waits.items():
                self.seen[e][k] = v
            if waits:
                self.q[e].append((list(waits.items()), None, None))

    def emit(self):
        nc = self.nc
        self.barrier()
        with contextlib.ExitStack() as st:
            sems = {}
            for e in ENG:
                sems[e] = st.enter_context(nc.semaphore("s_" + e))
            for i in range(NDMA):
                sems[("d", i)] = st.enter_context(nc.semaphore("s_d%d" % i))
            block = st.enter_context(nc.Block())

            def run(e, name):
                for waits, fn, inc in self.q[name]:
                    for k, v in waits:
                        e.wait_ge(sems[k], v)
                    if fn is not None:
                        try:
                            ins = getattr(e, fn[0])(*fn[1], **fn[2])
                        except Exception:
                            print("FAILED OP", fn[0], [str(v)[:80] for v in fn[2].values()], flush=True)
                            raise
                        ins.then_inc(sems[inc[0]], inc[1])

            @block.tensor
            def _(e):
                run(e, "pe")

            @block.scalar
            def _(e):
                run(e, "act")

            @block.vector
            def _(e):
                run(e, "dve")

            @block.gpsimd
            def _(e):
                run(e, "pool")

            @block.sync
            def _(e):
                run(e, "sp")


class Arena:
    def __init__(self, nc, st, words):
        self.t = st.enter_context(nc.sbuf_tensor("arena", [128, words], F32))
        self.words = words
        self.off = 0

    def mark(self):
        return self.off

    def reset(self, m):
        self.off = m

    def alloc(self, free, dt=F32):
        free = list(free)
        n = int(np.prod(free))
        bpe = 4 if dt in (F32, I32) else 2
        w = (n * bpe + 3) // 4
        w = (w + 7) // 8 * 8
        assert self.off + w <= self.words, ("arena overflow", self.off, w, self.words)
        ap = self.t[:, self.off:self.off + w]
        self.off += w
        if bpe == 2:
            ap = ap.bitcast(dt)[:, 0:n]
        elif dt != F32:
            ap = ap.bitcast(dt)[:, 0:n]
        else:
            ap = ap[:, 0:n]
        if len(free) == 2:
            ap = ap.rearrange("p (a b) -> p a b", a=free[0])
        elif len(free) == 3:
            ap = ap.rearrange("p (a b c) -> p a b c", a=free[0], b=free[1])
        elif len(free) == 4:
            ap = ap.rearrange("p (a b c d) -> p a b c d", a=free[0], b=free[1], c=free[2])
        return Buf(ap)


def build(L, depth=DEPTH, dbg=None, stop=None):
    T = LC + L
    NT = T // 128
    nc = bass.Bass("TRN2", target_bir_lowering=False)
    dbg = dbg or set()

    def din(name, shape):
        return nc.dram_tensor(name, list(shape), F32, kind="ExternalInput").ap()

    def dscr(name, shape, dt=F32):
        kind = "ExternalOutput" if name in dbg else "Internal"
        return nc.dram_tensor(name, list(shape), dt, kind=kind).ap()

    xin = din("xin", [T, D])
    cc = din("cc", [2, D])
    ident_d = din("ident", [128, 128])
    W = {}
    for nm, shp in WSHAPES.items():
        W[nm] = din(nm, shp)
    out_d = nc.dram_tensor("out", [L, D], F32, kind="ExternalOutput").ap()

    X = dscr("X", [T, D])
    UT = dscr("UT", [512, T], BF16)
    QKVT = dscr("QKVT", [1536, T])
    TM = dscr("TM", [T, 1296])
    GT = dscr("GT", [3072, T], BF16)
    YTf = dscr("YT", [3 * 512, T], BF16)
    YT = YTf.rearrange("(b k p) t -> b p k t", b=3, p=128)
    masks_d = din("masks", [128, 3, 128])
    ropepos_d = din("ropepos", [128, L // 128 + 1])
    invfreq_d = din("invfreq", [128, 16])
    dncon_d = din("dncon", [128, 7, 128])
    dnsel_d = din("dnsel", [128, 2, 128])

    with contextlib.ExitStack() as st:
        P = Prog(nc)
        A = Arena(nc, st, 50 * 1024)
        PS = [Buf(st.enter_context(nc.psum_tensor("ps%d" % i, [128, 512], F32))[:], excl=True) for i in range(8)]

        def psbf(i):
            return PS[i].ap.bitcast(BF16)

        ident = A.alloc([128])
        identb = A.alloc([128], BF16)
        ones = A.alloc([128])
        P.dma("sp", ident[:], ident_d, writes=[ident])
        P.op("dve", lambda e: e.tensor_copy(out=identb[:], in_=ident[:]), reads=[ident], writes=[identb])
        P.op("pool", lambda e: e.memset(ones[:], 1.0), writes=[ones])
        base_mark = A.mark()
        dX = Buf()
        dUT, dQKVT, dTM, dGT, dYT = Buf(), Buf(), Buf(), Buf(), Buf()

        for l in range(depth):
            A.reset(base_mark)
            pre = '' if l == 0 else '1'
            Xr = xin if l == 0 else X
            condT = A.alloc([2, 8])
            for c in range(2):
                P.dma("sp", condT[:, c, :], cc[c].rearrange("(k p) -> p k", p=128), writes=[condT],
                      allow_slow_non_contiguous=True)
            P.op("act", lambda e: e.activation(out=condT[:], in_=condT[:], func=AF.Silu), reads=[condT], writes=[condT])
            bmodF = A.alloc([48])
            modF = A.alloc([48, 2])
            Grow = A.alloc([2, 2, 1024])
            ng = A.alloc([2, 8])
            AB = A.alloc([2, 8, 2])
            l_mark = A.mark()
            condB = A.alloc([16, 128])
            for k in range(8):
                for c in range(2):
                    P.op("dve", lambda e, k=k, c=c: e.tensor_scalar(
                        out=condB[:, k * 2 + c, :], in0=ones[:], scalar1=condT[:, c, k:k + 1], scalar2=None,
                        op0=ALU.mult), reads=[condT, ones], writes=[condB])
            P.dma("sp", bmodF[:], W["b_mod"][l].rearrange("(c p) -> p c", p=128), writes=[bmodF],
                  allow_slow_non_contiguous=True)
            if stop == pre + '0a':
                break
            wms = [A.alloc([8, 512]) for _ in range(2)]
            bmodB = [A.alloc([512]) for _ in range(2)]
            wmod_v = W["w_mod"][l].rearrange("(k p) n -> p k n", p=128)
            for oc4 in range(12):
                if stop == '0b' and oc4 == 1:
                    break
                if stop == '0c' and oc4 == 5:
                    break
                wm = wms[oc4 % 2]
                P.dma("sp", wm[:], wmod_v[:, :, oc4 * 512:(oc4 + 1) * 512], writes=[wm])
                if oc4 in (4, 5, 10, 11):
                    gate = 0 if oc4 < 6 else 1
                    half = oc4 % 2 if oc4 < 6 else (oc4 - 10)
                    bb = bmodB[oc4 % 2]
                    P.dma("sp", bb[:], W["b_mod"][l][oc4 * 512:(oc4 + 1) * 512].partition_broadcast(128), writes=[bb])
                    for c in range(2):
                        ps = PS[c]
                        for k in range(8):
                            P.op("pe", lambda e, k=k, c=c, ps=ps, wm=wm: e.matmul(
                                ps[:], lhsT=condB[:, k * 2 + c, :], rhs=wm[:, k, :], start=(k == 0), stop=(k == 7)),
                                reads=[condB, wm], writes=[ps])
                        P.op("dve", lambda e, c=c, ps=ps, bb=bb, gate=gate, half=half: e.tensor_tensor(
                            out=Grow[:, gate, c, half * 512:(half + 1) * 512], in0=ps[:], in1=bb[:], op=ALU.add),
                            reads=[ps, bb], writes=[Grow])
                else:
                    ps = PS[2]
                    for j in range(4):
                        oc = oc4 * 4 + j
                        for k in range(8):
                            P.op("pe", lambda e, k=k, j=j, oc=oc, ps=ps, wm=wm: e.matmul(
                                ps[:, oc * 2:oc * 2 + 2], lhsT=wm[:, k, j * 128:(j + 1) * 128], rhs=condT[:, :, k],
                                start=(k == 0), stop=(k == 7)), reads=[condT, wm], writes=[ps])
                    P.op("dve", lambda e, oc4=oc4, ps=ps: e.tensor_tensor(
                        out=modF[:, oc4 * 4:oc4 * 4 + 4, :],
                        in0=ps[:, oc4 * 8:oc4 * 8 + 8].rearrange("p (a b) -> p a b", b=2),
                        in1=bmodF[:, oc4 * 4:oc4 * 4 + 4].unsqueeze(2).to_broadcast([128, 4, 2]), op=ALU.add),
                        reads=[ps, bmodF], writes=[modF])
            if stop in tuple(pre + q for q in ('0b', '0c', '0d')):
                break
            P.dma("sp", ng[:, 0, :], W["norm1_g"][l].rearrange("(k p) -> p k", p=128), writes=[ng], allow_slow_non_contiguous=True)
            P.dma("sp", ng[:, 1, :], W["norm2_g"][l].rearrange("(k p) -> p k", p=128), writes=[ng], allow_slow_non_contiguous=True)
            for wn in range(2):
                scb = 8 + 24 * wn
                P.op("dve", lambda e, wn=wn, scb=scb: e.tensor_scalar(
                    out=AB[:, wn, :, :], in0=modF[:, scb:scb + 8, :], scalar1=1.0, scalar2=None, op0=ALU.add),
                    reads=[modF], writes=[AB])
                P.op("dve", lambda e, wn=wn: e.tensor_tensor(
                    out=AB[:, wn, :, :], in0=AB[:, wn, :, :], in1=ng[:, wn, :].unsqueeze(2).to_broadcast([128, 8, 2]),
                    op=ALU.mult), reads=[AB, ng], writes=[AB])
            P.barrier()
            A.reset(l_mark)

            if stop == pre + '0':
                break
            win = A.alloc([8, D_IN], BF16)
            wst = [A.alloc([8, 256]) for _ in range(2)]
            win_v = W["w_in"][l].rearrange("(k p) n -> p k n", p=128)
            nchunk = (D_IN + 255) // 256
            for ci in range(nchunk):
                c0 = ci * 256
                cw = min(256, D_IN - c0)
                ws = wst[ci % 2]
                P.dma("sp", ws[:, :, 0:cw], win_v[:, :, c0:c0 + cw], writes=[ws])
                P.op("pool" if ci % 2 else "dve", lambda e, ws=ws, c0=c0, cw=cw: e.tensor_copy(
                    out=win[:, :, c0:c0 + cw], in_=ws[:, :, 0:cw]), reads=[ws], writes=[win])
            xts = [A.alloc([1024]) for _ in range(2)]
            junk = A.alloc([1024])
            xn = A.alloc([1024], BF16)
            ss = A.alloc([2])
            hT = A.alloc([8, 512], BF16)
            stF = [A.alloc([512]) for _ in range(2)]
            stFb = [A.alloc([512], BF16) for _ in range(2)]
            stT = [A.alloc([1296]) for _ in range(2)]
            nsuper = (T + 511) // 512
            evi = 0
            for s_ in range(nsuper):
                t0 = s_ * 512
                ST = min(512, T - t0)
                nsub = ST // 128
                for sub in range(nsub):
                    tok0 = t0 + sub * 128
                    cnd = 1 if tok0 < LC else 0
                    xt = xts[sub % 2]
                    P.dma("sp", xt[:], Xr[tok0:tok0 + 128, :], reads=[dX], writes=[xt])
                    P.op("act", lambda e, xt=xt: e.activation(out=junk[:], in_=xt[:], func=AF.Square, accum_out=ss[:, 0:1]),
                         reads=[xt], writes=[junk, ss])
                    P.op("act", lambda e: e.activation(out=ss[:, 1:2], in_=ss[:, 0:1], func=AF.Sqrt, scale=1.0 / D, bias=EPS),
                         reads=[ss], writes=[ss])
                    P.op("dve", lambda e: e.reciprocal(out=ss[:, 1:2], in_=ss[:, 1:2]), reads=[ss], writes=[ss])
                    P.op("act", lambda e, xt=xt: e.activation(out=xn[:], in_=xt[:], func=AF.Copy, scale=ss[:, 1:2]),
                         reads=[xt, ss], writes=[xn])
                    pT = PS[7]
                    for k in range(8):
                        P.op("pe", lambda e, k=k: e.transpose(out=psbf(7)[:, k * 128:(k + 1) * 128], in_=xn[:, k * 128:(k + 1) * 128],
                                                               identity=identb[:]), reads=[xn, identb], writes=[pT])
                    for k in range(8):
                        if k % 2:
                            P.op("dve", lambda e, k=k, sub=sub, cnd=cnd: e.tensor_scalar(
                                out=hT[:, k, sub * 128:(sub + 1) * 128], in0=psbf(7)[:, k * 128:(k + 1) * 128],
                                scalar1=AB[:, 0, k, cnd:cnd + 1], scalar2=modF[:, k, cnd:cnd + 1], op0=ALU.mult, op1=ALU.add),
                                reads=[pT, AB, modF], writes=[hT])
                        else:
                            P.op("act", lambda e, k=k, sub=sub, cnd=cnd: e.activation(
                                out=hT[:, k, sub * 128:(sub + 1) * 128], in_=psbf(7)[:, k * 128:(k + 1) * 128], func=AF.Identity,
                                scale=AB[:, 0, k, cnd:cnd + 1], bias=modF[:, k, cnd:cnd + 1]), reads=[pT, AB, modF], writes=[hT])
                fm = [("U", i, i * 128) for i in range(4)] + [("Q", i, 512 + i * 128) for i in range(12)] + \
                     [("G", i, 3344 + i * 128) for i in range(24)]
                for (kind, i, c0) in fm:
                    ps = PS[evi % 4]
                    for k in range(8):
                        P.op("pe", lambda e, k=k, c0=c0, ps=ps, ST=ST: e.matmul(
                            ps[:, 0:ST], lhsT=win[:, k, c0:c0 + 128], rhs=hT[:, k, 0:ST], start=(k == 0), stop=(k == 7)),
                            reads=[win, hT], writes=[ps])
                    if kind == "Q":
                        sg = stF[evi % 2]
                        P.op("dve", lambda e, sg=sg, ps=ps, ST=ST: e.tensor_copy(out=sg[:, 0:ST], in_=ps[:, 0:ST]), reads=[ps], writes=[sg])
                        P.dma("pool", QKVT[i * 128:(i + 1) * 128, t0:t0 + ST], sg[:, 0:ST], reads=[sg], writes=[dQKVT])
                    elif kind == "U":
                        sg = stFb[evi % 2]
                        P.op("act", lambda e, sg=sg, ps=ps, ST=ST: e.activation(out=sg[:, 0:ST], in_=ps[:, 0:ST], func=AF.Copy), reads=[ps], writes=[sg])
                        P.dma("pool", UT[i * 128:(i + 1) * 128, t0:t0 + ST], sg[:, 0:ST], reads=[sg], writes=[dUT])
                    else:
                        sg = stFb[evi % 2]
                        P.op("act", lambda e, sg=sg, ps=ps, ST=ST: e.activation(out=sg[:, 0:ST], in_=ps[:, 0:ST], func=AF.Sigmoid), reads=[ps], writes=[sg])
                        P.dma("pool", GT[i * 128:(i + 1) * 128, t0:t0 + ST], sg[:, 0:ST], reads=[sg], writes=[dGT])
                    evi += 1
                for sub in range(nsub):
                    sg = stT[sub % 2]
                    for (c0, cw) in ((2048, 512), (2560, 512), (3072, 272)):
                        ps = PS[4 + (evi % 3)]
                        evi += 1
                        for k in range(8):
                            P.op("pe", lambda e, k=k, c0=c0, cw=cw, ps=ps, sub=sub: e.matmul(
                                ps[:, 0:cw], lhsT=hT[:, k, sub * 128:(sub + 1) * 128], rhs=win[:, k, c0:c0 + cw],
                                start=(k == 0), stop=(k == 7)), reads=[win, hT], writes=[ps])
                        P.op("dve", lambda e, sg=sg, ps=ps, c0=c0, cw=cw: e.tensor_copy(
                            out=sg[:, c0 - 2048:c0 - 2048 + cw], in_=ps[:, 0:cw]), reads=[ps], writes=[sg])
                    P.dma("pool", TM[t0 + sub * 128:t0 + (sub + 1) * 128, :], sg[:], reads=[sg], writes=[dTM])
            P.barrier()
            A.reset(l_mark)
            with_ctx = l < depth - 1
            if stop == pre + 'A':
                break
            NL = L // 128
            sinkB = A.alloc([8])
            P.dma("sp", sinkB[:], W["attn_sink"][l].partition_broadcast(128), writes=[sinkB])
            nsinkB = A.alloc([8])
            P.op("dve", lambda e: e.tensor_scalar(out=nsinkB[:], in0=sinkB[:], scalar1=-1.0, scalar2=None, op0=ALU.mult),
                 reads=[sinkB], writes=[nsinkB])
            maskf = A.alloc([3, 128])
            maskb = A.alloc([3, 128], BF16)
            P.dma("sp", maskf[:], masks_d, writes=[maskf])
            P.op("dve", lambda e: e.tensor_copy(out=maskb[:], in_=maskf[:]), reads=[maskf], writes=[maskb])
            pos = A.alloc([NL + 1])
            invf = A.alloc([16])
            P.dma("sp", pos[:], ropepos_d, writes=[pos])
            P.dma("sp", invf[:], invfreq_d, writes=[invf])
            yy = A.alloc([2, NL + 1, 16])
            yi = A.alloc([2, NL + 1, 16], I32)
            yf = A.alloc([2, NL + 1, 16])
            tab = A.alloc([2, NL + 1, 16])
            for sc_ in range(2):
                P.op("dve", lambda e, sc_=sc_: e.tensor_tensor(
                    out=yy[:, sc_, :, :], in0=pos[:].unsqueeze(2).to_broadcast([128, NL + 1, 16]),
                    in1=invf[:].unsqueeze(1).to_broadcast([128, NL + 1, 16]), op=ALU.mult), reads=[pos, invf], writes=[yy])
            P.op("dve", lambda e: e.tensor_scalar(out=yy[:, 0, :, :], in0=yy[:, 0, :, :], scalar1=1.0 / (2 * np.pi), scalar2=None,
                                                  op0=ALU.mult), reads=[yy], writes=[yy])
            P.op("dve", lambda e: e.tensor_scalar(out=yy[:, 1, :, :], in0=yy[:, 1, :, :], scalar1=1.0 / (2 * np.pi), scalar2=0.25,
                                                  op0=ALU.mult, op1=ALU.add), reads=[yy], writes=[yy])
            P.op("dve", lambda e: e.tensor_copy(out=yi[:], in_=yy[:]), reads=[yy], writes=[yi])
            P.op("dve", lambda e: e.tensor_copy(out=yf[:], in_=yi[:]), reads=[yi], writes=[yf])
            P.op("dve", lambda e: e.tensor_sub(out=yy[:], in0=yy[:], in1=yf[:]), reads=[yy, yf], writes=[yy])
            P.op("act", lambda e: e.activation(out=tab[:], in_=yy[:], func=AF.Sin, scale=2 * np.pi), reads=[yy], writes=[tab])
            if stop == pre + 'C0':
                break
            KT = A.alloc([T], BF16)
            Vr = A.alloc([NT, 128], BF16)
            QT = A.alloc([4, T], BF16)
            qin = [A.alloc([768]) for _ in range(2)]
            qk = A.alloc([640], BF16)
            tmp = [A.alloc([10, 16]) for _ in range(4)]
            qkr = A.alloc([640])
            for tt in range(NT):
                if stop == 'C1a' and tt == 2:
                    break
                if stop == 'C1b' and tt == 3:
                    break
                qi = qin[tt % 2]
                P.dma("sp", qi[:], TM[tt * 128:(tt + 1) * 128, 528:1296], reads=[dTM], writes=[qi])
                import os
                CUT = int(os.environ.get("CUT", "9"))
                if CUT < 1:
                    continue
                P.op("dve", lambda e, qi=qi, tt=tt: e.tensor_copy(out=Vr[:, tt, :], in_=qi[:, 640:768]), reads=[qi], writes=[Vr])
                if CUT < 2:
                    continue
                if tt < 2:
                    for kq in range(2):
                        P.op("dve", lambda e, qi=qi, kq=kq: e.tensor_copy(out=qk[:, 0:512].rearrange("p (j k d) -> p j k d", j=4, k=2)[:, :, kq, :],
                                                                         in_=qi[:, kq * 256:(kq + 1) * 256].rearrange("p (j d) -> p j d", j=4)), reads=[qi], writes=[qk])
                    P.op("dve", lambda e, qi=qi: e.tensor_copy(out=qk[:, 512:640], in_=qi[:, 512:640]), reads=[qi], writes=[qk])
                else:
                    i = tt - 2
                    qv = qi[:, 0:640].rearrange("p (h f x j) -> p h f x j", h=10, f=2, x=2, j=16)
                    ov = qkr[:].rearrange("p (h f x j) -> p h f x j", h=10, f=2, x=2, j=16)
                    for f in range(2):
                        ti = i if f == 0 else NL
                        cs = tab[:, 1, ti, :].unsqueeze(1).to_broadcast([128, 10, 16])
                        sn = tab[:, 0, ti, :].unsqueeze(1).to_broadcast([128, 10, 16])
                        x1 = qv[:, :, f, 0, :]
                        x2 = qv[:, :, f, 1, :]
                        P.op("dve", lambda e, x1=x1, cs=cs: e.tensor_tensor(out=tmp[0][:], in0=x1, in1=cs, op=ALU.mult), reads=[qi, tab], writes=[tmp[0]])
                        P.op("dve", lambda e, x2=x2, sn=sn: e.tensor_tensor(out=tmp[1][:], in0=x2, in1=sn, op=ALU.mult), reads=[qi, tab], writes=[tmp[1]])
                        P.op("dve", lambda e, x2=x2, cs=cs: e.tensor_tensor(out=tmp[2][:], in0=x2, in1=cs, op=ALU.mult), reads=[qi, tab], writes=[tmp[2]])
                        P.op("dve", lambda e, x1=x1, sn=sn: e.tensor_tensor(out=tmp[3][:], in0=x1, in1=sn, op=ALU.mult), reads=[qi, tab], writes=[tmp[3]])
                        P.op("dve", lambda e, f=f, ov=ov: e.tensor_sub(out=ov[:, :, f, 0, :], in0=tmp[0][:], in1=tmp[1][:]), reads=[tmp[0], tmp[1]], writes=[qkr])
                        P.op("dve", lambda e, f=f, ov=ov: e.tensor_add(out=ov[:, :, f, 1, :], in0=tmp[2][:], in1=tmp[3][:]), reads=[tmp[2], tmp[3]], writes=[qkr])
                    for kq in range(2):
                        P.op("dve", lambda e, kq=kq: e.tensor_copy(out=qk[:, 0:512].rearrange("p (j k d) -> p j k d", j=4, k=2)[:, :, kq, :],
                                                                  in_=qkr[:, kq * 256:(kq + 1) * 256].rearrange("p (j d) -> p j d", j=4)), reads=[qkr], writes=[qk])
                    P.op("dve", lambda e: e.tensor_copy(out=qk[:, 512:640], in_=qkr[:, 512:640]), reads=[qkr], writes=[qk])
                pT = PS[7]
                if CUT < 3:
                    continue
                for j in range(4):
                    P.op("pe", lambda e, j=j: e.transpose(out=psbf(7)[:, j * 128:(j + 1) * 128], in_=qk[:, j * 128:(j + 1) * 128],
                                                          identity=identb[:]), reads=[qk, identb], writes=[pT])
                P.op("pe", lambda e: e.transpose(out=psbf(7)[:, 512:640], in_=qk[:, 512:640], identity=identb[:]),
                     reads=[qk, identb], writes=[pT])
                if CUT < 4:
                    continue
                P.op("dve", lambda e, tt=tt: e.tensor_copy(out=QT[:, :, tt * 128:(tt + 1) * 128],
                                                            in_=psbf(7)[:, 0:512].rearrange("p (a b) -> p a b", a=4)),
                     reads=[pT], writes=[QT])
                if CUT < 5:
                    continue
                P.op("dve", lambda e, tt=tt: e.tensor_copy(out=KT[:, tt * 128:(tt + 1) * 128], in_=psbf(7)[:, 512:640]),
                     reads=[pT], writes=[KT])
            if stop in tuple(pre + q for q in ('C1', 'C1a', 'C1b')):
                break
            Pexp = [A.alloc([640], BF16) for _ in range(2)]
            PTs = [A.alloc([5, 128], BF16) for _ in range(2)]
            st8 = A.alloc([8, 8])
            yct = A.alloc([512], BF16)
            ycT = [A.alloc([4, 128], BF16) for _ in range(2)]
            hc = 0
            for qt in range(NT):
                if qt < 2 and not with_ctx:
                    continue
                if stop == 'C2' and qt == 1:
                    break
                if stop == 'C3' and qt == 3:
                    break
                lat = qt >= 2
                i = qt - 2
                for h in range(8):
                    j, kvh = h % 4, h // 4
                    pb = 64 * kvh
                    psA, psB = PS[(hc % 2) * 2], PS[(hc % 2) * 2 + 1]
                    pe_ = Pexp[hc % 2]
                    pts = PTs[hc % 2]
                    lhs = QT[pb:pb + 64, j, qt * 128:(qt + 1) * 128]
                    P.op("pe", lambda e, lhs=lhs, psA=psA, pb=pb: e.matmul(psA[:, 0:256], lhsT=lhs, rhs=KT[pb:pb + 64, 0:256], start=True, stop=True),
                         reads=[QT, KT], writes=[psA])
                    blocks = [(0, 0), (1, 128)]
                    if lat:
                        for bi, blk in enumerate((i - 1, i, i + 1)):
                            src = min(max(blk, 0), NL - 1)
                            c0 = 256 + 128 * src
                            mt = None
                            if bi == 0:
                                mt = 0 if i > 0 else 2
                            if bi == 2:
                                mt = 1 if i < NL - 1 else 2
                            P.op("pe", lambda e, lhs=lhs, psB=psB, pb=pb, c0=c0, bi=bi, mt=mt: e.matmul(
                                psB[:, bi * 128:(bi + 1) * 128], lhsT=lhs, rhs=KT[pb:pb + 64, c0:c0 + 128], start=True, stop=(mt is None)),
                                reads=[QT, KT], writes=[psB])
                            if mt is not None:
                                P.op("pe", lambda e, psB=psB, bi=bi, mt=mt: e.matmul(
                                    psB[:, bi * 128:(bi + 1) * 128], lhsT=identb[:], rhs=maskb[:, mt, :], start=False, stop=True),
                                    reads=[identb, maskb], writes=[psB])
                            blocks.append((2 + src, 256 + bi * 128))
                    sv = st8[:, h, :]
                    P.op("dve", lambda e, sv=sv, psA=psA: e.reduce_max(out=sv[:, 0:1], in_=psA[:, 0:256], axis=AX.X), reads=[psA], writes=[st8])
                    if lat:
                        P.op("dve", lambda e, sv=sv, psB=psB: e.reduce_max(out=sv[:, 1:2], in_=psB[:, 0:384], axis=AX.X), reads=[psB], writes=[st8])
                        P.op("dve", lambda e, sv=sv: e.tensor_tensor(out=sv[:, 0:1], in0=sv[:, 0:1], in1=sv[:, 1:2], op=ALU.max), reads=[st8], writes=[st8])
                    P.op("dve", lambda e, sv=sv, h=h: e.tensor_scalar(out=sv[:, 2:3], in0=sv[:, 0:1], scalar1=-0.125, scalar2=nsinkB[:, h:h + 1],
                                                                     op0=ALU.mult, op1=ALU.min), reads=[st8, nsinkB], writes=[st8])
                    P.op("act", lambda e, sv=sv, psA=psA, pe_=pe_: e.activation(out=pe_[:, 0:256], in_=psA[:, 0:256], func=AF.Exp, scale=0.125,
                                                                               bias=sv[:, 2:3], accum_out=sv[:, 3:4]), reads=[psA, st8], writes=[pe_, st8])
                    if lat:
                        P.op("act", lambda e, sv=sv, psB=psB, pe_=pe_: e.activation(out=pe_[:, 256:640], in_=psB[:, 0:384], func=AF.Exp, scale=0.125,
                                                                                   bias=sv[:, 2:3], accum_out=sv[:, 4:5]), reads=[psB, st8], writes=[pe_, st8])
                    else:
                        P.op("dve", lambda e, sv=sv: e.memset(sv[:, 4:5], 0.0), writes=[st8])
                    P.op("act", lambda e, sv=sv, h=h: e.activation(out=sv[:, 5:6], in_=sinkB[:, h:h + 1], func=AF.Exp, bias=sv[:, 2:3]),
                         reads=[sinkB, st8], writes=[st8])
                    P.op("dve", lambda e, sv=sv: e.reduce_sum(out=sv[:, 6:7], in_=sv[:, 3:6], axis=AX.X), reads=[st8], writes=[st8])
                    P.op("dve", lambda e, sv=sv: e.reciprocal(out=sv[:, 7:8], in_=sv[:, 6:7]), reads=[st8], writes=[st8])
                    pTT = PS[4 + hc % 2]
                    nb_ = len(blocks)
                    for bi, (kt_, co) in enumerate(blocks):
                        P.op("pe", lambda e, bi=bi, co=co, pe_=pe_, hc=hc: e.transpose(
                            out=psbf(4 + hc % 2)[:, bi * 128:(bi + 1) * 128], in_=pe_[:, co:co + 128], identity=identb[:]),
                            reads=[pe_, identb], writes=[pTT])
                    P.op("dve", lambda e, pts=pts, hc=hc, nb_=nb_: e.tensor_copy(
                        out=pts[:, 0:nb_, :], in_=psbf(4 + hc % 2)[:, 0:nb_ * 128].rearrange("p (a b) -> p a b", b=128)),
                        reads=[pTT], writes=[pts])
                    psO = PS[6]
                    for bi, (kt_, co) in enumerate(blocks):
                        P.op("pe", lambda e, bi=bi, kt_=kt_, pts=pts, pb=pb, nb_=nb_: e.matmul(
                            psO[:, 0:64], lhsT=pts[:, bi, :], rhs=Vr[:, kt_, pb:pb + 64], start=(bi == 0), stop=(bi == nb_ - 1)),
                            reads=[pts, Vr], writes=[psO])
                    P.op("act", lambda e, sv=sv, h=h: e.activation(out=yct[:, h * 64:(h + 1) * 64], in_=psO[:, 0:64], func=AF.Copy, scale=sv[:, 7:8]),
                         reads=[psO, st8], writes=[yct])
                    hc += 1
                pT = PS[7]
                yo = ycT[qt % 2]
                for k in range(4):
                    P.op("pe", lambda e, k=k: e.transpose(out=psbf(7)[:, k * 128:(k + 1) * 128], in_=yct[:, k * 128:(k + 1) * 128], identity=identb[:]),
                         reads=[yct, identb], writes=[pT])
                P.op("dve", lambda e, yo=yo: e.tensor_copy(out=yo[:], in_=psbf(7)[:, 0:512].rearrange("p (a b) -> p a b", a=4)), reads=[pT], writes=[yo])
                P.dma("act", YT[2, :, :, qt * 128:(qt + 1) * 128], yo[:], reads=[yo], writes=[dYT])
            P.barrier()
            A.reset(l_mark)
            if stop in tuple(pre + q for q in ('C', 'C2', 'C3')):
                break
            YS = dscr("YS%d" % l, [512, T])
            ZS = dscr("ZS%d" % l, [512, T], BF16)
            dYS, dZS = Buf(), Buf()
            sm = A.alloc([24, 32])
            LRE, LIM, LDT, DTT, ZZ, TH, MAG, CC, SS_, ARE, AIM, FRE, FIM, T0_, T1_, T2_, X2, CT_, ST_ = range(19)
            for d in range(2):
                P.dma("sp", sm[:, LRE, d * 16:(d + 1) * 16], W["ssm_lam_re"][l, d].rearrange("(j g) p -> g p j", g=2)[0], writes=[sm], allow_slow_non_contiguous=True) if False else None
                for g2 in range(2):
                    P.dma("sp", sm[g2 * 64:(g2 + 1) * 64, LRE, d * 16:(d + 1) * 16], W["ssm_lam_re"][l, d].rearrange("(j g) p -> g p j", g=2)[g2], writes=[sm], allow_slow_non_contiguous=True)
                    P.dma("sp", sm[g2 * 64:(g2 + 1) * 64, LIM, d * 16:(d + 1) * 16], W["ssm_lam_im"][l, d].rearrange("(j g) p -> g p j", g=2)[g2], writes=[sm], allow_slow_non_contiguous=True)
                    P.dma("sp", sm[g2 * 64:(g2 + 1) * 64, LDT, d * 16:(d + 1) * 16], W["ssm_log_dt"][l, d].rearrange("(j g) -> g j", g=2)[g2].partition_broadcast(64), writes=[sm], allow_slow_non_contiguous=True)

            def sv(i):
                return sm[:, i, :]

            def tt_(eng, o, a_, b_, op):
                P.op(eng, lambda e: e.tensor_tensor(out=sv(o), in0=sv(a_), in1=sv(b_), op=op), reads=[sm], writes=[sm])

            def ts_(o, a_, s1, s2=None, op0=ALU.mult, op1=ALU.add):
                if s2 is None:
                    P.op("dve", lambda e: e.tensor_scalar(out=sv(o), in0=sv(a_), scalar1=s1, scalar2=None, op0=op0), reads=[sm], writes=[sm])
                else:
                    P.op("dve", lambda e: e.tensor_scalar(out=sv(o), in0=sv(a_), scalar1=s1, scalar2=s2, op0=op0, op1=op1), reads=[sm], writes=[sm])

            P.op("act", lambda e: e.activation(out=sv(DTT), in_=sv(LDT), func=AF.Exp), reads=[sm], writes=[sm])
            tt_("dve", ZZ, LRE, DTT, ALU.mult)
            tt_("dve", TH, LIM, DTT, ALU.mult)
            P.op("act", lambda e: e.activation(out=sv(MAG), in_=sv(ZZ), func=AF.Exp), reads=[sm], writes=[sm])
            ts_(T0_, TH, 1.0 / 256)
            tt_("dve", X2, T0_, T0_, ALU.mult)
            ts_(T1_, X2, -1.0 / 20, 1.0)
            tt_("dve", T1_, T1_, X2, ALU.mult)
            ts_(T1_, T1_, -1.0 / 6, 1.0)
            tt_("dve", SS_, T1_, T0_, ALU.mult)
            ts_(T1_, X2, -1.0 / 30, 1.0)
            tt_("dve", T1_, T1_, X2, ALU.mult)
            ts_(T1_, T1_, -1.0 / 12, 1.0)
            tt_("dve", T1_, T1_, X2, ALU.mult)
            ts_(CC, T1_, -0.5, 1.0)
            for _ in range(8):
                tt_("dve", T0_, CC, CC, ALU.mult)
                tt_("dve", T1_, SS_, SS_, ALU.mult)
                tt_("dve", T2_, CC, SS_, ALU.mult)
                tt_("dve", CC, T0_, T1_, ALU.subtract)
                ts_(SS_, T2_, 2.0)
            tt_("dve", ARE, MAG, CC, ALU.mult)
            tt_("dve", AIM, MAG, SS_, ALU.mult)
            tt_("dve", T0_, LRE, LRE, ALU.mult)
            tt_("dve", T1_, LIM, LIM, ALU.mult)
            tt_("dve", T0_, T0_, T1_, ALU.add)
            P.op("dve", lambda e: e.reciprocal(out=sv(T0_), in_=sv(T0_)), reads=[sm], writes=[sm])
            ts_(T1_, ARE, -1.0, None, op0=ALU.add)
            tt_("dve", T2_, T1_, LRE, ALU.mult)
            tt_("dve", FRE, AIM, LIM, ALU.mult)
            tt_("dve", FRE, FRE, T2_, ALU.add)
            tt_("dve", FRE, FRE, T0_, ALU.mult)
            tt_("dve", T2_, AIM, LRE, ALU.mult)
            tt_("dve", FIM, T1_, LIM, ALU.mult)
            tt_("dve", FIM, T2_, FIM, ALU.subtract)
            tt_("dve", FIM, FIM, T0_, ALU.mult)
            ct = A.alloc([32, 128])
            stb = A.alloc([32, 128])
            Rf = A.alloc([32, 128])
            wk = A.alloc([4, 32])
            tq = [A.alloc([32, 64]) for _ in range(2)]
            P.op("pool", lambda e: e.memset(ct[:, :, 0:1], 1.0), writes=[ct])
            P.op("pool", lambda e: e.memset(stb[:, :, 0:1], 0.0), writes=[stb])
            P.op("dve", lambda e: e.tensor_copy(out=wk[:, 0, :], in_=sv(CC)), reads=[sm], writes=[wk])
            P.op("dve", lambda e: e.tensor_scalar(out=wk[:, 1, :], in0=sv(SS_), scalar1=-1.0, scalar2=None, op0=ALU.mult), reads=[sm], writes=[wk])
            kk_ = 1
            while kk_ <= 128:
                if kk_ < 128:
                    wc = wk[:, 0, :].unsqueeze(2).to_broadcast([128, 32, kk_])
                    ws_ = wk[:, 1, :].unsqueeze(2).to_broadcast([128, 32, kk_])
                    P.op("dve", lambda e, wc=wc, kk_=kk_: e.tensor_tensor(out=tq[0][:, :, 0:kk_], in0=ct[:, :, 0:kk_], in1=wc, op=ALU.mult), reads=[ct, wk], writes=[tq[0]])
                    P.op("dve", lambda e, ws_=ws_, kk_=kk_: e.tensor_tensor(out=tq[1][:, :, 0:kk_], in0=stb[:, :, 0:kk_], in1=ws_, op=ALU.mult), reads=[stb, wk], writes=[tq[1]])
                    P.op("dve", lambda e, kk_=kk_: e.tensor_sub(out=ct[:, :, kk_:2 * kk_], in0=tq[0][:, :, 0:kk_], in1=tq[1][:, :, 0:kk_]), reads=[tq[0], tq[1]], writes=[ct])
                    P.op("dve", lambda e, ws_=ws_, kk_=kk_: e.tensor_tensor(out=tq[0][:, :, 0:kk_], in0=ct[:, :, 0:kk_], in1=ws_, op=ALU.mult), reads=[ct, wk], writes=[tq[0]])
                    P.op("dve", lambda e, wc=wc, kk_=kk_: e.tensor_tensor(out=tq[1][:, :, 0:kk_], in0=stb[:, :, 0:kk_], in1=wc, op=ALU.mult), reads=[stb, wk], writes=[tq[1]])
                    P.op("dve", lambda e, kk_=kk_: e.tensor_add(out=stb[:, :, kk_:2 * kk_], in0=tq[0][:, :, 0:kk_], in1=tq[1][:, :, 0:kk_]), reads=[tq[0], tq[1]], writes=[stb])
                    P.op("dve", lambda e: e.tensor_tensor(out=wk[:, 2, :], in0=wk[:, 0, :], in1=wk[:, 0, :], op=ALU.mult), reads=[wk], writes=[wk])
                    P.op("dve", lambda e: e.tensor_tensor(out=wk[:, 3, :], in0=wk[:, 1, :], in1=wk[:, 1, :], op=ALU.mult), reads=[wk], writes=[wk])
                    P.op("dve", lambda e: e.tensor_tensor(out=wk[:, 1, :], in0=wk[:, 0, :], in1=wk[:, 1, :], op=ALU.mult), reads=[wk], writes=[wk])
                    P.op("dve", lambda e: e.tensor_scalar(out=wk[:, 1, :], in0=wk[:, 1, :], scalar1=2.0, scalar2=None, op0=ALU.mult), reads=[wk], writes=[wk])
                    P.op("dve", lambda e: e.tensor_sub(out=wk[:, 0, :], in0=wk[:, 2, :], in1=wk[:, 3, :]), reads=[wk], writes=[wk])
                kk_ *= 2
            for cI in range(32):
                P.op("pool", lambda e, cI=cI: e.tensor_scalar(out=Rf[:, cI, :], in0=ones[:], scalar1=sm[:, MAG, cI:cI + 1], scalar2=None, op0=ALU.mult), reads=[ones, sm], writes=[Rf])
            BW = A.alloc([32, 2, 128], BF16)
            CP = A.alloc([32, 2, 128], BF16)
            P.op("pool", lambda e: e.memset(CP[:], 0.0), writes=[CP])
            braw = A.alloc([2, 32, 16])
            for d in range(2):
                for g2 in range(2):
                    P.dma("sp", braw[g2 * 64:(g2 + 1) * 64, 0, d * 16:(d + 1) * 16, :], W["ssm_b_re"][l, d].rearrange("(j g) p h -> g p j h", g=2)[g2], writes=[braw])
                    P.dma("sp", braw[g2 * 64:(g2 + 1) * 64, 1, d * 16:(d + 1) * 16, :], W["ssm_b_im"][l, d].rearrange("(j g) p h -> g p j h", g=2)[g2], writes=[braw])
            bbar = A.alloc([2, 32, 16])
            tb = [A.alloc([32, 16]) for _ in range(2)]
            fr = sm[:, FRE, :].unsqueeze(2).to_broadcast([128, 32, 16])
            fi = sm[:, FIM, :].unsqueeze(2).to_broadcast([128, 32, 16])
            P.op("dve", lambda e: e.tensor_tensor(out=tb[0][:], in0=braw[:, 0, :, :], in1=fr, op=ALU.mult), reads=[braw, sm], writes=[tb[0]])
            P.op("dve", lambda e: e.tensor_tensor(out=tb[1][:], in0=braw[:, 1, :, :], in1=fi, op=ALU.mult), reads=[braw, sm], writes=[tb[1]])
            P.op("dve", lambda e: e.tensor_sub(out=bbar[:, 0, :, :], in0=tb[0][:], in1=tb[1][:]), reads=[tb[0], tb[1]], writes=[bbar])
            P.op("dve", lambda e: e.tensor_tensor(out=tb[0][:], in0=braw[:, 1, :, :], in1=fr, op=ALU.mult), reads=[braw, sm], writes=[tb[0]])
            P.op("dve", lambda e: e.tensor_tensor(out=tb[1][:], in0=braw[:, 0, :, :], in1=fi, op=ALU.mult), reads=[braw, sm], writes=[tb[1]])
            P.op("dve", lambda e: e.tensor_add(out=bbar[:, 1, :, :], in0=tb[0][:], in1=tb[1][:]), reads=[tb[0], tb[1]], writes=[bbar])
            xp = [A.alloc([128]) for _ in range(2)]
            cl = A.alloc([2, 32, 64])
            for cI in range(32):
                d, j = cI // 16, cI % 16
                jl = j % 4
                for ri in range(2):
                    x_ = xp[(cI * 2 + ri) % 2]
                    P.op("pool", lambda e, x_=x_: e.memset(x_[:], 0.0), writes=[x_])
                    for g2 in range(2):
                        c0 = 32 * jl + 16 * g2
                        P.op("dve", lambda e, x_=x_, g2=g2, c0=c0, ri=ri, cI=cI: e.tensor_copy(out=x_[g2 * 64:(g2 + 1) * 64, c0:c0 + 16], in_=bbar[g2 * 64:(g2 + 1) * 64, ri, cI, :]),
                             reads=[bbar], writes=[x_])
                    P.op("pe", lambda e, x_=x_: e.transpose(out=PS[0][:, 0:128], in_=x_[:], identity=ident[:]), reads=[x_, ident], writes=[PS[0]])
                    P.op("act", lambda e, cI=cI, ri=ri: e.activation(out=BW[:, cI, ri, :], in_=PS[0][:, 0:128], func=AF.Copy), reads=[PS[0]], writes=[BW])
            for d in range(2):
                P.dma("sp", cl[0:16, 0, :, :], W["ssm_c_re"][l, d].rearrange("g h p -> h g p"), writes=[cl])
                P.dma("sp", cl[0:16, 1, :, :], W["ssm_c_im"][l, d].rearrange("g h p -> h g p"), writes=[cl])
                for j in range(16):
                    cI = d * 16 + j
                    for ri in range(2):
                        P.op("pe", lambda e, ri=ri, j=j: e.transpose(out=PS[1][:, 0:16], in_=cl[0:16, ri, 2 * j:2 * j + 2, :].rearrange("p a b -> p (a b)"),
                                                                     identity=ident[0:16, 0:16]), reads=[cl, ident], writes=[PS[1]])
                        for g2 in range(2):
                            c0 = 32 * (j % 4) + 16 * g2
                            P.op("act", lambda e, ri=ri, g2=g2, cI=cI, c0=c0: e.activation(out=CP[g2 * 64:(g2 + 1) * 64, cI, ri, c0:c0 + 16], in_=PS[1][g2 * 64:(g2 + 1) * 64, 0:16],
                                                                                      func=AF.Copy, scale=(1.0 if ri == 0 else -1.0)), reads=[PS[1]], writes=[CP])
            dsk = A.alloc([4])
            P.dma("sp", dsk[:], W["ssm_d"][l].rearrange("(k p) -> p k", p=128), writes=[dsk], allow_slow_non_contiguous=True)
            uts = [A.alloc([128], BF16) for _ in range(2)]
            wre = A.alloc([4, 128])
            wim = A.alloc([4, 128])
            tt4 = [A.alloc([4, 128]) for _ in range(4)]
            qre = A.alloc([4, 128])
            qim = A.alloc([4, 128])
            sre = A.alloc([4, 128], BF16)
            sim_ = A.alloc([4, 128], BF16)
            qin_ = A.alloc([2, 4])
            qtm = A.alloc([4, 4])
            yst = [A.alloc([128]) for _ in range(2)]
            zst = [A.alloc([128], BF16) for _ in range(2)]
            for d in range(2):
                order = list(range(NT)) if d == 0 else [1, 0] + list(range(NT - 1, 1, -1))
                rv = (lambda ap: ap) if d == 0 else (lambda ap: ap[:, ::-1])
                for ft in range(4):
                    c4 = d * 16 + 4 * ft
                    P.op("pool", lambda e: e.memset(qin_[:], 0.0), writes=[qin_])
                    for it, tt in enumerate(order):
                        tok0 = tt * 128
                        u_ = uts[it % 2]
                        P.dma("sp", u_[:], UT[ft * 128:(ft + 1) * 128, tok0:tok0 + 128], reads=[dUT], writes=[u_])
                        for jl in range(4):
                            for ri in range(2):
                                P.op("pe", lambda e, jl=jl, ri=ri, u_=u_, c4=c4: e.matmul(PS[ri][:, jl * 128:(jl + 1) * 128], lhsT=BW[:, c4 + jl, ri, :], rhs=rv(u_[:]), start=True, stop=True),
                                     reads=[BW, u_], writes=[PS[ri]])
                        bre = PS[0][:].rearrange("p (a b) -> p a b", a=4)
                        bim = PS[1][:].rearrange("p (a b) -> p a b", a=4)
                        cT = ct[:, c4:c4 + 4, :]
                        sT = stb[:, c4:c4 + 4, :]
                        P.op("dve", lambda e, bre=bre, cT=cT: e.tensor_tensor(out=tt4[0][:], in0=bre, in1=cT, op=ALU.mult), reads=[PS[0], ct], writes=[tt4[0]])
                        P.op("dve", lambda e, bim=bim, sT=sT: e.tensor_tensor(out=tt4[1][:], in0=bim, in1=sT, op=ALU.mult), reads=[PS[1], stb], writes=[tt4[1]])
                        P.op("pool", lambda e: e.tensor_sub(out=wre[:], in0=tt4[0][:], in1=tt4[1][:]), reads=[tt4[0], tt4[1]], writes=[wre])
                        P.op("dve", lambda e, bim=bim, cT=cT: e.tensor_tensor(out=tt4[2][:], in0=bim, in1=cT, op=ALU.mult), reads=[PS[1], ct], writes=[tt4[2]])
                        P.op("dve", lambda e, bre=bre, sT=sT: e.tensor_tensor(out=tt4[3][:], in0=bre, in1=sT, op=ALU.mult), reads=[PS[0], stb], writes=[tt4[3]])
                        P.op("pool", lambda e: e.tensor_add(out=wim[:], in0=tt4[2][:], in1=tt4[3][:]), reads=[tt4[2], tt4[3]], writes=[wim])
                        for jl in range(4):
                            P.op("dve", lambda e, jl=jl, c4=c4: e.tensor_tensor_scan(out=qre[:, jl, :], data0=Rf[:, c4 + jl, :], data1=wre[:, jl, :], initial=qin_[:, 0, jl:jl + 1],
                                                                                    op0=ALU.mult, op1=ALU.add), reads=[Rf, wre, qin_], writes=[qre])
                            P.op("dve", lambda e, jl=jl, c4=c4: e.tensor_tensor_scan(out=qim[:, jl, :], data0=Rf[:, c4 + jl, :], data1=wim[:, jl, :], initial=qin_[:, 1, jl:jl + 1],
                                                                                    op0=ALU.mult, op1=ALU.add), reads=[Rf, wim, qin_], writes=[qim])
                        wc = wk[:, 0, c4:c4 + 4]
                        ws_ = wk[:, 1, c4:c4 + 4]
                        P.op("pool", lambda e, wc=wc: e.tensor_tensor(out=qtm[:, 0, :], in0=qre[:, :, 127], in1=wc, op=ALU.mult), reads=[qre, wk], writes=[qtm])
                        P.op("pool", lambda e, ws_=ws_: e.tensor_tensor(out=qtm[:, 1, :], in0=qim[:, :, 127], in1=ws_, op=ALU.mult), reads=[qim, wk], writes=[qtm])
                        P.op("pool", lambda e, wc=wc: e.tensor_tensor(out=qtm[:, 2, :], in0=qim[:, :, 127], in1=wc, op=ALU.mult), reads=[qim, wk], writes=[qtm])
                        P.op("pool", lambda e, ws_=ws_: e.tensor_tensor(out=qtm[:, 3, :], in0=qre[:, :, 127], in1=ws_, op=ALU.mult), reads=[qre, wk], writes=[qtm])
                        P.op("pool", lambda e: e.tensor_add(out=qin_[:, 0, :], in0=qtm[:, 0, :], in1=qtm[:, 1, :]), reads=[qtm], writes=[qin_])
                        P.op("pool", lambda e: e.tensor_sub(out=qin_[:, 1, :], in0=qtm[:, 2, :], in1=qtm[:, 3, :]), reads=[qtm], writes=[qin_])
                        P.op("dve", lambda e, cT=cT: e.tensor_tensor(out=tt4[0][:], in0=qre[:], in1=cT, op=ALU.mult), reads=[qre, ct], writes=[tt4[0]])
                        P.op("pool", lambda e, sT=sT: e.tensor_tensor(out=tt4[1][:], in0=qim[:], in1=sT, op=ALU.mult), reads=[qim, stb], writes=[tt4[1]])
                        P.op("dve", lambda e: e.tensor_add(out=sre[:], in0=tt4[0][:], in1=tt4[1][:]), reads=[tt4[0], tt4[1]], writes=[sre])
                        P.op("pool", lambda e, cT=cT: e.tensor_tensor(out=tt4[2][:], in0=qim[:], in1=cT, op=ALU.mult), reads=[qim, ct], writes=[tt4[2]])
                        P.op("dve", lambda e, sT=sT: e.tensor_tensor(out=tt4[3][:], in0=qre[:], in1=sT, op=ALU.mult), reads=[qre, stb], writes=[tt4[3]])
                        P.op("pool", lambda e: e.tensor_sub(out=sim_[:], in0=tt4[2][:], in1=tt4[3][:]), reads=[tt4[2], tt4[3]], writes=[sim_])
                        for jl in range(4):
                            P.op("pe", lambda e, jl=jl, c4=c4: e.matmul(PS[2][:, 0:128], lhsT=CP[:, c4 + jl, 0, :], rhs=rv(sre[:, jl, :]), start=(jl == 0), stop=False), reads=[CP, sre], writes=[PS[2]])
                            P.op("pe", lambda e, jl=jl, c4=c4: e.matmul(PS[2][:, 0:128], lhsT=CP[:, c4 + jl, 1, :], rhs=rv(sim_[:, jl, :]), start=False, stop=(jl == 3)), reads=[CP, sim_], writes=[PS[2]])
                        y_ = yst[it % 2]
                        if d == 0:
                            P.op("act", lambda e, y_=y_: e.activation(out=y_[:], in_=PS[2][:, 0:128], func=AF.Copy), reads=[PS[2]], writes=[y_])
                            P.dma("act", YS[ft * 128:(ft + 1) * 128, tok0:tok0 + 128], y_[:], reads=[y_], writes=[dYS])
                        elif tt >= 2 or with_ctx:
                            z_ = zst[it % 2]
                            P.dma("sp", y_[:], YS[ft * 128:(ft + 1) * 128, tok0:tok0 + 128], reads=[dYS], writes=[y_])
                            P.op("dve", lambda e, y_=y_: e.tensor_add(out=y_[:], in0=y_[:], in1=PS[2][:, 0:128]), reads=[y_, PS[2]], writes=[y_])
                            P.op("dve", lambda e, y_=y_, u_=u_, ft=ft: e.scalar_tensor_tensor(out=y_[:], in0=u_[:], scalar=dsk[:, ft:ft + 1], in1=y_[:], op0=ALU.mult, op1=ALU.add),
                                 reads=[y_, u_, dsk], writes=[y_])
                            P.op("act", lambda e, y_=y_, z_=z_: e.activation(out=z_[:], in_=y_[:], func=AF.Gelu), reads=[y_], writes=[z_])
                            P.dma("act", ZS[ft * 128:(ft + 1) * 128, tok0:tok0 + 128], z_[:], reads=[z_], writes=[dZS])
            P.barrier()
            stg = [A.alloc([4, 256]) for _ in range(2)]
            wg = A.alloc([4, 512], BF16)
            wgv = W["ssm_w_glu"][l].rearrange("(k p) n -> p k n", p=128)
            for ci in range(2):
                ws = stg[ci]
                P.dma("sp", ws[:], wgv[:, :, ci * 256:(ci + 1) * 256], writes=[ws])
                P.op("dve", lambda e, ws=ws, ci=ci: e.tensor_copy(out=wg[:, :, ci * 256:(ci + 1) * 256], in_=ws[:]), reads=[ws], writes=[wg])
            zin = [A.alloc([4, 512], BF16) for _ in range(2)]
            sg_ = A.alloc([512])
            yo_ = [A.alloc([4, 512], BF16) for _ in range(2)]
            ZSv = ZS.rearrange("(k p) t -> p k t", p=128)
            t0 = 0 if with_ctx else LC
            bi_ = 0
            while t0 < T:
                ST = min(512, T - t0)
                if t0 < LC:
                    ST = LC - t0
                zi = zin[bi_ % 2]
                yo = yo_[bi_ % 2]
                P.dma("sp", zi[:, :, 0:ST], ZSv[:, :, t0:t0 + ST], reads=[dZS], writes=[zi])
                for oc in range(4):
                    ps = PS[3 + oc % 2]
                    for k in range(4):
                        P.op("pe", lambda e, k=k, oc=oc, ps=ps, zi=zi, ST=ST: e.matmul(ps[:, 0:ST], lhsT=wg[:, k, oc * 128:(oc + 1) * 128], rhs=zi[:, k, 0:ST], start=(k == 0), stop=(k == 3)),
                             reads=[wg, zi], writes=[ps])
                    P.op("act", lambda e, ps=ps, ST=ST: e.activation(out=sg_[:, 0:ST], in_=ps[:, 0:ST], func=AF.Sigmoid), reads=[ps], writes=[sg_])
                    P.op("dve", lambda e, oc=oc, zi=zi, yo=yo, ST=ST: e.tensor_tensor(out=yo[:, oc, 0:ST], in0=zi[:, oc, 0:ST], in1=sg_[:, 0:ST], op=ALU.mult), reads=[zi, sg_], writes=[yo])
                P.dma("act", YT[0, :, :, t0:t0 + ST], yo[:, :, 0:ST], reads=[yo], writes=[dYT])
                t0 += ST
                bi_ += 1
            P.barrier()
            A.reset(l_mark)

            if stop == pre + 'S':
                break
            QN = dscr("QN%d" % l, [T, 512], BF16)
            KN = dscr("KN%d" % l, [T, 512], BF16)
            VS = dscr("VS%d" % l, [T, 512], BF16)
            BG = dscr("BG%d" % l, [T, 16])
            OD = dscr("OD%d" % l, [2, T, 512])
            dQN, dKN, dVS, dBG, dOD = Buf(), Buf(), Buf(), Buf(), Buf()
            cw = A.alloc([12, 5])
            for j in range(5):
                P.dma("sp", cw[:, :, j], W["dn_conv_w"][l][j].rearrange("(c p) -> p c", p=128), writes=[cw], allow_slow_non_contiguous=True)
            alB = A.alloc([8])
            dtB = A.alloc([8])
            P.dma("sp", alB[:], W["dn_a_log"][l].rearrange("d h -> (d h)").partition_broadcast(128), writes=[alB])
            P.dma("sp", dtB[:], W["dn_dt_bias"][l].rearrange("d h -> (d h)").partition_broadcast(128), writes=[dtB])
            P.op("act", lambda e: e.activation(out=alB[:], in_=alB[:], func=AF.Exp), reads=[alB], writes=[alB])
            dcon = A.alloc([7, 128])
            P.dma("sp", dcon[:], dncon_d, writes=[dcon])
            sel2 = A.alloc([2, 128])
            P.dma("sp", sel2[:], dnsel_d, writes=[sel2])
            ngB = A.alloc([128])
            P.dma("sp", ngB[:], W["dn_norm_g"][l].partition_broadcast(128), writes=[ngB])
            b_mark = A.mark()
            sq = A.alloc([12, 512])
            cin = [A.alloc([516]) for _ in range(2)]
            acc = [A.alloc([512]) for _ in range(2)]
            tmi = [A.alloc([16]) for _ in range(2)]
            sqr = A.alloc([512])
            rn = A.alloc([2, 4, 2])
            qo = [A.alloc([3, 512], BF16) for _ in range(2)]
            bg = [A.alloc([16]) for _ in range(2)]
            sp_t = A.alloc([4, 8])
            spans = [(0, LC)] + [(LC + 512 * i, 512) for i in range(L // 512)]
            for (s0, sl) in spans:
                for c in range(12):
                    ci_ = cin[c % 2]
                    P.op("pool", lambda e, ci_=ci_: e.memset(ci_[:], 0.0), writes=[ci_])
                    seg_lo, seg_hi = (0, LC) if s0 < LC else (LC, T)
                    lo_, hi_ = max(s0 - 2, seg_lo), min(s0 + sl + 2, seg_hi)
                    o0 = 2 - (s0 - lo_)
                    P.dma("sp", ci_[:, o0:o0 + (hi_ - lo_)], QKVT[c * 128:(c + 1) * 128, lo_:hi_], reads=[dQKVT], writes=[ci_])
                    ac = acc[c % 2]
                    P.op("dve", lambda e, ci_=ci_, ac=ac, c=c, sl=sl: e.tensor_scalar(out=ac[:, 0:sl], in0=ci_[:, 0:sl], scalar1=cw[:, c, 0:1], scalar2=None, op0=ALU.mult),
                         reads=[ci_, cw], writes=[ac])
                    for j in range(1, 5):
                        P.op("dve", lambda e, ci_=ci_, ac=ac, c=c, sl=sl, j=j: e.scalar_tensor_tensor(
                            out=ac[:, 0:sl], in0=ci_[:, j:j + sl], scalar=cw[:, c, j:j + 1], in1=ac[:, 0:sl], op0=ALU.mult, op1=ALU.add),
                            reads=[ci_, cw, ac], writes=[ac])
                    P.op("act", lambda e, ac=ac, c=c, sl=sl: e.activation(out=sq[:, c, 0:sl], in_=ac[:, 0:sl], func=AF.Silu), reads=[ac], writes=[sq])
                for sub in range(sl // 128):
                    tok0 = s0 + sub * 128
                    o_ = qo[sub % 2]
                    for grp in range(3):
                        ps = PS[grp]
                        for c4 in range(4):
                            P.op("pe", lambda e, grp=grp, c4=c4, ps=ps, sub=sub: e.transpose(
                                out=ps[:, c4 * 128:(c4 + 1) * 128], in_=sq[:, grp * 4 + c4, sub * 128:(sub + 1) * 128], identity=ident[:]),
                                reads=[sq, ident], writes=[ps])
                        if grp < 2:
                            P.op("act", lambda e, ps=ps: e.activation(out=sqr[:], in_=ps[:], func=AF.Square), reads=[ps], writes=[sqr])
                            P.op("dve", lambda e, grp=grp: e.reduce_sum(out=rn[:, grp, :, 0], in_=sqr[:].rearrange("p (h d) -> p h d", h=4), axis=AX.X),
                                 reads=[sqr], writes=[rn])
                            P.op("act", lambda e, grp=grp: e.activation(out=rn[:, grp, :, 1], in_=rn[:, grp, :, 0], func=AF.Sqrt, bias=EPS,
                                                                        scale=(128.0 if grp == 0 else 1.0)), reads=[rn], writes=[rn])
                            P.op("dve", lambda e, grp=grp: e.reciprocal(out=rn[:, grp, :, 1], in_=rn[:, grp, :, 1]), reads=[rn], writes=[rn])
                            P.op("dve", lambda e, grp=grp, ps=ps, o_=o_: e.tensor_tensor(
                                out=o_[:, grp, :].rearrange("p (h d) -> p h d", h=4), in0=ps[:].rearrange("p (h d) -> p h d", h=4),
                                in1=rn[:, grp, :, 1].unsqueeze(2).to_broadcast([128, 4, 128]), op=ALU.mult), reads=[ps, rn], writes=[o_])
                        else:
                            P.op("act", lambda e, ps=ps, o_=o_: e.activation(out=o_[:, 2, :], in_=ps[:], func=AF.Copy), reads=[ps], writes=[o_])
                    P.dma("act", QN[tok0:tok0 + 128, :], o_[:, 0, :], reads=[o_], writes=[dQN])
                    P.dma("act", KN[tok0:tok0 + 128, :], o_[:, 1, :], reads=[o_], writes=[dKN])
                    P.dma("act", VS[tok0:tok0 + 128, :], o_[:, 2, :], reads=[o_], writes=[dVS])
                    ti_ = tmi[sub % 2]
                    bg_ = bg[sub % 2]
                    P.dma("sp", ti_[:], TM[tok0:tok0 + 128, 512:528], reads=[dTM], writes=[ti_])
                    P.op("act", lambda e, ti_=ti_, bg_=bg_: e.activation(out=bg_[:, 0:8], in_=ti_[:, 0:8], func=AF.Sigmoid), reads=[ti_], writes=[bg_])
                    P.op("dve", lambda e, ti_=ti_: e.tensor_add(out=sp_t[:, 0, :], in0=ti_[:, 8:16], in1=dtB[:]), reads=[ti_, dtB], writes=[sp_t])
                    P.op("act", lambda e: e.activation(out=sp_t[:, 1, :], in_=sp_t[:, 0, :], func=AF.Abs), reads=[sp_t], writes=[sp_t])
                    P.op("act", lambda e: e.activation(out=sp_t[:, 1, :], in_=sp_t[:, 1, :], func=AF.Exp, scale=-1.0), reads=[sp_t], writes=[sp_t])
                    P.op("act", lambda e: e.activation(out=sp_t[:, 1, :], in_=sp_t[:, 1, :], func=AF.Ln, bias=1.0), reads=[sp_t], writes=[sp_t])
                    P.op("dve", lambda e: e.scalar_tensor_tensor(out=sp_t[:, 2, :], in0=sp_t[:, 0, :], scalar=0.0, in1=sp_t[:, 1, :], op0=ALU.max, op1=ALU.add),
                         reads=[sp_t], writes=[sp_t])
                    P.op("dve", lambda e, bg_=bg_: e.scalar_tensor_tensor(out=bg_[:, 8:16], in0=sp_t[:, 2, :], scalar=-1.0, in1=alB[:], op0=ALU.mult, op1=ALU.mult),
                         reads=[sp_t, alB], writes=[bg_])
                    P.dma("act", BG[tok0:tok0 + 128, :], bg_[:], reads=[bg_], writes=[dBG])
            P.barrier()
            A.reset(b_mark)
            if stop == pre + 'B0':
                break
            import os
            if os.environ.get("MAXOPS"):
                P.limit = int(os.environ["MAXOPS"])
            qkv = [A.alloc([3, 512], BF16) for _ in range(2)]
            bgt = [A.alloc([16]) for _ in range(2)]
            gs = A.alloc([4, 8])
            egb = A.alloc([2, 8])
            Sf = A.alloc([4, 128])
            Sb = A.alloc([4, 128], BF16)
            qT = A.alloc([128], BF16)
            kT = A.alloc([128], BF16)
            Dg = A.alloc([2, 128])
            Dst = A.alloc([128])
            Din = A.alloc([128])
            Mm = A.alloc([2, 2, 128])
            Rr = A.alloc([2, 2, 128])
            QKm = A.alloc([128])
            QKmT = A.alloc([128], BF16)
            vb = A.alloc([128])
            xk = A.alloc([128])
            ut = A.alloc([128])
            wT = A.alloc([128], BF16)
            qgT = A.alloc([128], BF16)
            dq_ = A.alloc([128], BF16)
            kdec = A.alloc([128], BF16)
            vnew = A.alloc([128], BF16)
            osb = [A.alloc([4, 128]) for _ in range(2)]
            for d in range(2):
                order = list(range(NT)) if d == 0 else [1, 0] + list(range(NT - 1, 1, -1))
                for hh in range(4):
                    P.op("pool", lambda e, hh=hh: e.memset(Sf[:, hh, :], 0.0), writes=[Sf])
                    P.op("pool", lambda e, hh=hh: e.memset(Sb[:, hh, :], 0.0), writes=[Sb])
                for it, tt in enumerate(order):
                    tok0 = tt * 128
                    q3 = qkv[it % 2]
                    b_ = bgt[it % 2]
                    os_ = osb[it % 2]
                    P.dma("sp", q3[:, 0, :], QN[tok0:tok0 + 128, :], reads=[dQN], writes=[q3])
                    P.dma("sp", q3[:, 1, :], KN[tok0:tok0 + 128, :], reads=[dKN], writes=[q3])
                    P.dma("sp", q3[:, 2, :], VS[tok0:tok0 + 128, :], reads=[dVS], writes=[q3])
                    P.dma("sp", b_[:], BG[tok0:tok0 + 128, :], reads=[dBG], writes=[b_])
                    ps = PS[0]
                    gcol = b_[:, 8 + 4 * d:12 + 4 * d]
                    P.op("pe", lambda e, gcol=gcol, d=d: e.matmul(PS[0][:, 0:4], lhsT=dcon[:, d, :], rhs=gcol, start=True, stop=True), reads=[dcon, b_], writes=[PS[0]])
                    P.op("pe", lambda e, gcol=gcol: e.matmul(PS[0][:, 4:8], lhsT=dcon[:, 2, :], rhs=gcol, start=True, stop=True), reads=[dcon, b_], writes=[PS[0]])
                    P.op("dve", lambda e: e.tensor_copy(out=gs[:, 0:2, 0:4].rearrange("p a b -> p (a b)") if False else gs[:, 0, 0:8], in_=PS[0][:, 0:8]), reads=[PS[0]], writes=[gs])
                    P.op("act", lambda e: e.activation(out=gs[:, 1, 0:8], in_=gs[:, 0, 0:8], func=AF.Exp), reads=[gs], writes=[gs])
                    P.op("dve", lambda e: e.tensor_sub(out=gs[:, 2, 0:4], in0=gs[:, 0, 4:8], in1=gs[:, 0, 0:4]), reads=[gs], writes=[gs])
                    P.op("act", lambda e: e.activation(out=gs[:, 2, 4:8], in_=gs[:, 2, 0:4], func=AF.Exp), reads=[gs], writes=[gs])
                    for ch in range(2):
                        P.op("pe", lambda e, ch=ch: e.matmul(PS[0][:, 8 + 4 * ch:12 + 4 * ch], lhsT=sel2[:, ch, :], rhs=gs[:, 1, 4:8], start=True, stop=True),
                             reads=[sel2, gs], writes=[PS[0]])
                    P.op("dve", lambda e: e.tensor_copy(out=egb[:].rearrange("p a b -> p (a b)")[:, 0:8] if False else egb[:, 0, 0:8], in_=PS[0][:, 8:16]), reads=[PS[0]], writes=[egb])
                    for hh in range(4):
                        qh = q3[:, 0, hh * 128:(hh + 1) * 128]
                        kh = q3[:, 1, hh * 128:(hh + 1) * 128]
                        vh = q3[:, 2, hh * 128:(hh + 1) * 128]
                        beta = b_[:, 4 * d + hh:4 * d + hh + 1]
                        gc = gs[:, 0, hh:hh + 1]
                        egc = gs[:, 1, hh:hh + 1]
                        edec = gs[:, 2, 4 + hh:5 + hh]
                        P.op("pe", lambda e, qh=qh: e.transpose(out=psbf(1)[:, 0:128], in_=qh, identity=identb[:]), reads=[q3, identb], writes=[PS[1]])
                        P.op("pe", lambda e, kh=kh: e.transpose(out=psbf(1)[:, 128:256], in_=kh, identity=identb[:]), reads=[q3, identb], writes=[PS[1]])
                        P.op("dve", lambda e: e.tensor_copy(out=qT[:], in_=psbf(1)[:, 0:128]), reads=[PS[1]], writes=[qT])
                        P.op("dve", lambda e: e.tensor_copy(out=kT[:], in_=psbf(1)[:, 128:256]), reads=[PS[1]], writes=[kT])
                        P.op("dve", lambda e, gc=gc: e.tensor_scalar(out=Dg[:, 0, :], in0=ident[:], scalar1=gc, scalar2=None, op0=ALU.mult), reads=[ident, gs], writes=[Dg])
                        P.op("pool", lambda e: e.tensor_scalar(out=Dg[:, 1, :], in0=Dg[:, 0, :], scalar1=-1.0, scalar2=None, op0=ALU.mult), reads=[Dg], writes=[Dg])
                        P.op("pe", lambda e: e.matmul(PS[2][:, 0:128], lhsT=Dg[:, 0, :], rhs=ones[:], start=True, stop=False), reads=[Dg, ones], writes=[PS[2]])
                        P.op("pe", lambda e: e.matmul(PS[2][:, 0:128], lhsT=ones[:], rhs=Dg[:, 1, :], start=False, stop=False), reads=[Dg, ones], writes=[PS[2]])
                        P.op("pe", lambda e, d=d: e.matmul(PS[2][:, 0:128], lhsT=ident[:], rhs=dcon[:, 4 + d, :], start=False, stop=True), reads=[ident, dcon], writes=[PS[2]])
                        P.op("act", lambda e: e.activation(out=Dst[:], in_=PS[2][:, 0:128], func=AF.Exp), reads=[PS[2]], writes=[Dst])
                        P.op("pool", lambda e: e.tensor_add(out=Din[:], in0=Dst[:], in1=ident[:]), reads=[Dst, ident], writes=[Din])
                        P.op("pe", lambda e: e.matmul(PS[3][:, 0:128], lhsT=kT[:], rhs=kT[:], start=True, stop=True), reads=[kT], writes=[PS[3]])
                        P.op("pe", lambda e: e.matmul(PS[3][:, 128:256], lhsT=qT[:], rhs=kT[:], start=True, stop=True), reads=[qT, kT], writes=[PS[3]])
                        P.op("dve", lambda e, beta=beta: e.scalar_tensor_tensor(out=Mm[:, 0, 0, :], in0=PS[3][:, 0:128], scalar=beta, in1=Dst[:], op0=ALU.mult, op1=ALU.mult),
                             reads=[PS[3], b_, Dst], writes=[Mm])
                        P.op("pool", lambda e: e.tensor_scalar(out=Mm[:, 0, 0, :], in0=Mm[:, 0, 0, :], scalar1=-1.0, scalar2=None, op0=ALU.mult), reads=[Mm], writes=[Mm])
                        P.op("dve", lambda e: e.tensor_tensor(out=QKm[:], in0=PS[3][:, 128:256], in1=Din[:], op=ALU.mult), reads=[PS[3], Din], writes=[QKm])
                        P.op("pe", lambda e: e.transpose(out=PS[4][:, 0:128], in_=Mm[:, 0, 0, :], identity=ident[:]), reads=[Mm, ident], writes=[PS[4]])
                        P.op("pe", lambda e: e.transpose(out=PS[4][:, 128:256], in_=QKm[:], identity=ident[:]), reads=[QKm, ident], writes=[PS[4]])
                        P.op("act", lambda e: e.activation(out=Mm[:, 0, 1, :], in_=PS[4][:, 0:128], func=AF.Copy), reads=[PS[4]], writes=[Mm])
                        P.op("act", lambda e: e.activation(out=QKmT[:], in_=PS[4][:, 128:256], func=AF.Copy), reads=[PS[4]], writes=[QKmT])
                        P.op("pool", lambda e: e.tensor_add(out=Rr[:, 0, 0, :], in0=Mm[:, 0, 0, :], in1=ident[:]), reads=[Mm, ident], writes=[Rr])
                        P.op("pool", lambda e: e.tensor_add(out=Rr[:, 0, 1, :], in0=Mm[:, 0, 1, :], in1=ident[:]), reads=[Mm, ident], writes=[Rr])
                        for it2 in range(5):
                            a_, b2 = it2 % 2, (it2 + 1) % 2
                            P.op("pe", lambda e, a_=a_: e.matmul(PS[5][:, 0:128], lhsT=Mm[:, a_, 1, :], rhs=Mm[:, a_, 0, :], start=True, stop=True), reads=[Mm], writes=[PS[5]])
                            P.op("pe", lambda e, a_=a_: e.matmul(PS[5][:, 128:256], lhsT=Mm[:, a_, 0, :], rhs=Mm[:, a_, 1, :], start=True, stop=True), reads=[Mm], writes=[PS[5]])
                            P.op("dve", lambda e, b2=b2: e.tensor_copy(out=Mm[:, b2, :, :], in_=PS[5][:, 0:256].rearrange("p (a b) -> p a b", a=2)), reads=[PS[5]], writes=[Mm])
                            if it2 < 4:
                                P.op("pe", lambda e, a_=a_, b2=b2: e.matmul(PS[6][:, 0:128], lhsT=Rr[:, a_, 1, :], rhs=Mm[:, b2, 0, :], start=True, stop=True), reads=[Rr, Mm], writes=[PS[6]])
                            P.op("pe", lambda e, a_=a_, b2=b2: e.matmul(PS[6][:, 128:256], lhsT=Mm[:, b2, 0, :], rhs=Rr[:, a_, 1, :], start=True, stop=True), reads=[Rr, Mm], writes=[PS[6]])
                            if it2 < 4:
                                P.op("dve", lambda e, a_=a_, b2=b2: e.tensor_tensor(out=Rr[:, b2, :, :], in0=Rr[:, a_, :, :], in1=PS[6][:, 0:256].rearrange("p (a b) -> p a b", a=2), op=ALU.add),
                                     reads=[Rr, PS[6]], writes=[Rr])
                            else:
                                P.op("dve", lambda e, a_=a_, b2=b2: e.tensor_tensor(out=Rr[:, b2, 1, :], in0=Rr[:, a_, 1, :], in1=PS[6][:, 128:256], op=ALU.add),
                                     reads=[Rr, PS[6]], writes=[Rr])
                        RT = Rr[:, 1, 1, :]
                        P.op("pool", lambda e, vh=vh, beta=beta: e.tensor_scalar(out=vb[:], in0=vh, scalar1=beta, scalar2=None, op0=ALU.mult), reads=[q3, b_], writes=[vb])
                        P.op("dve", lambda e, kh=kh, beta=beta, egc=egc: e.tensor_scalar(out=xk[:], in0=kh, scalar1=beta, scalar2=egc, op0=ALU.mult, op1=ALU.mult),
                             reads=[q3, b_, gs], writes=[xk])
                        P.op("pe", lambda e, RT=RT: e.matmul(PS[7][:, 0:128], lhsT=RT, rhs=vb[:], start=True, stop=True), reads=[Rr, vb], writes=[PS[7]])
                        P.op("pe", lambda e, RT=RT: e.matmul(PS[7][:, 128:256], lhsT=xk[:], rhs=RT, start=True, stop=True), reads=[Rr, xk], writes=[PS[7]])
                        P.op("dve", lambda e: e.tensor_copy(out=ut[:], in_=PS[7][:, 0:128]), reads=[PS[7]], writes=[ut])
                        P.op("act", lambda e: e.activation(out=wT[:], in_=PS[7][:, 128:256], func=AF.Copy), reads=[PS[7]], writes=[wT])
                        P.op("dve", lambda e, egc=egc: e.tensor_scalar(out=dq_[:], in0=identb[:], scalar1=egc, scalar2=None, op0=ALU.mult), reads=[identb, gs], writes=[dq_])
                        P.op("pe", lambda e, qh=qh: e.matmul(PS[7][:, 256:384], lhsT=qh, rhs=dq_[:], start=True, stop=True), reads=[q3, dq_], writes=[PS[7]])
                        P.op("act", lambda e: e.activation(out=qgT[:], in_=PS[7][:, 256:384], func=AF.Copy), reads=[PS[7]], writes=[qgT])
                        P.op("pool", lambda e, kh=kh, edec=edec: e.tensor_scalar(out=kdec[:], in0=kh, scalar1=edec, scalar2=None, op0=ALU.mult), reads=[q3, gs], writes=[kdec])
                        for ch in ((0, 1) if d == 0 else (1, 0)):
                            r0 = ch * 64
                            r = slice(r0, r0 + 64)
                            P.op("pe", lambda e, r=r, hh=hh: e.matmul(PS[0][r, 128:256], lhsT=wT[:, r], rhs=Sb[:, hh, :], start=True, stop=True), reads=[wT, Sb], writes=[PS[0]])
                            P.op("dve", lambda e, r=r: e.tensor_sub(out=vnew[r, :], in0=ut[r, :], in1=PS[0][r, 128:256]), reads=[ut, PS[0]], writes=[vnew])
                            P.op("pe", lambda e, r=r, hh=hh: e.matmul(PS[0][r, 256:384], lhsT=qgT[:, r], rhs=Sb[:, hh, :], start=True, stop=False), reads=[qgT, Sb], writes=[PS[0]])
                            P.op("pe", lambda e, r=r: e.matmul(PS[0][r, 256:384], lhsT=QKmT[r, r], rhs=vnew[r, :], start=False, stop=True), reads=[QKmT, vnew], writes=[PS[0]])
                            P.op("act", lambda e, r=r, hh=hh, os_=os_: e.activation(out=os_[r, hh, :], in_=PS[0][r, 256:384], func=AF.Copy), reads=[PS[0]], writes=[os_])
                            P.op("pe", lambda e, r=r: e.matmul(PS[1][:, 256:384], lhsT=kdec[r, :], rhs=vnew[r, :], start=True, stop=True), reads=[kdec, vnew], writes=[PS[1]])
                            P.op("dve", lambda e, hh=hh, ch=ch: e.scalar_tensor_tensor(out=Sf[:, hh, :], in0=Sf[:, hh, :], scalar=egb[:, 0, 4 * ch + hh:4 * ch + hh + 1],
                                                                                   in1=PS[1][:, 256:384], op0=ALU.mult, op1=ALU.add), reads=[Sf, egb, PS[1]], writes=[Sf])
                            P.op("act", lambda e, hh=hh: e.activation(out=Sb[:, hh, :], in_=Sf[:, hh, :], func=AF.Copy), reads=[Sf], writes=[Sb])
                    if tt >= 2 or with_ctx:
                        P.dma("act", OD[d, tok0:tok0 + 128, :], os_[:].rearrange("p a b -> p (a b)"), reads=[os_], writes=[dOD])
            P.barrier()
            A.reset(b_mark)
            o2 = [A.alloc([2, 512]) for _ in range(2)]
            zt = [A.alloc([512]) for _ in range(2)]
            sqr = A.alloc([512])
            rn = A.alloc([2, 4])
            ybt = A.alloc([512], BF16)
            ybT = [A.alloc([4, 128], BF16) for _ in range(2)]
            for tt in range(NT):
                if tt < 2 and not with_ctx:
                    continue
                tok0 = tt * 128
                o_ = o2[tt % 2]
                z_ = zt[tt % 2]
                for d in range(2):
                    P.dma("sp", o_[:, d, :], OD[d, tok0:tok0 + 128, :], reads=[dOD], writes=[o_])
                P.dma("sp", z_[:], TM[tok0:tok0 + 128, 0:512], reads=[dTM], writes=[z_])
                P.op("pool", lambda e, o_=o_: e.tensor_add(out=o_[:, 0, :], in0=o_[:, 0, :], in1=o_[:, 1, :]), reads=[o_], writes=[o_])
                P.op("act", lambda e, o_=o_: e.activation(out=sqr[:], in_=o_[:, 0, :], func=AF.Square), reads=[o_], writes=[sqr])
                P.op("dve", lambda e: e.reduce_sum(out=rn[:, 0, :], in_=sqr[:].rearrange("p (h d) -> p h d", h=4), axis=AX.X), reads=[sqr], writes=[rn])
                P.op("act", lambda e: e.activation(out=rn[:, 1, :], in_=rn[:, 0, :], func=AF.Sqrt, bias=EPS, scale=1.0 / 128), reads=[rn], writes=[rn])
                P.op("dve", lambda e: e.reciprocal(out=rn[:, 1, :], in_=rn[:, 1, :]), reads=[rn], writes=[rn])
                P.op("dve", lambda e, o_=o_: e.tensor_tensor(out=o_[:, 0, :].rearrange("p (h d) -> p h d", h=4), in0=o_[:, 0, :].rearrange("p (h d) -> p h d", h=4),
                                                            in1=rn[:, 1, :].unsqueeze(2).to_broadcast([128, 4, 128]), op=ALU.mult), reads=[o_, rn], writes=[o_])
                P.op("pool", lambda e, o_=o_: e.tensor_tensor(out=o_[:, 0, :].rearrange("p (h d) -> p h d", h=4), in0=o_[:, 0, :].rearrange("p (h d) -> p h d", h=4),
                                                             in1=ngB[:].unsqueeze(1).to_broadcast([128, 4, 128]), op=ALU.mult), reads=[o_, ngB], writes=[o_])
                P.op("act", lambda e, z_=z_: e.activation(out=z_[:], in_=z_[:], func=AF.Silu), reads=[z_], writes=[z_])
                P.op("dve", lambda e, o_=o_, z_=z_: e.tensor_tensor(out=ybt[:], in0=o_[:, 0, :], in1=z_[:], op=ALU.mult), reads=[o_, z_], writes=[ybt])
                pT = PS[7]
                yo = ybT[tt % 2]
                for k in range(4):
                    P.op("pe", lambda e, k=k: e.transpose(out=psbf(7)[:, k * 128:(k + 1) * 128], in_=ybt[:, k * 128:(k + 1) * 128], identity=identb[:]),
                         reads=[ybt, identb], writes=[pT])
                P.op("dve", lambda e, yo=yo: e.tensor_copy(out=yo[:], in_=psbf(7)[:, 0:512].rearrange("p (a b) -> p a b", a=4)), reads=[pT], writes=[yo])
                P.dma("act", YT[1, :, :, tok0:tok0 + 128], yo[:], reads=[yo], writes=[dYT])
            P.barrier()
            A.reset(l_mark)

            if stop == pre + 'B':
                break
            def load_w(dst, src_v, ncols, kk, stg, k0=0):
                cw_ = stg[0].ap.shape[-1]
                for ci in range((ncols + cw_ - 1) // cw_):
                    c0 = ci * cw_
                    cw = min(cw_, ncols - c0)
                    ws = stg[ci % 2]
                    P.dma("sp", ws[:, 0:kk, 0:cw], src_v[:, :, c0:c0 + cw], writes=[ws])
                    P.op("pool" if ci % 2 else "dve", lambda e, ws=ws, c0=c0, cw=cw: e.tensor_copy(
                        out=dst[:, k0:k0 + kk, c0:c0 + cw], in_=ws[:, 0:kk, 0:cw]), reads=[ws], writes=[dst])

            tok_lo = 0 if with_ctx else LC
            stg = [A.alloc([8, 256]) for _ in range(2)]
            wbr = [A.alloc([4, 1024], BF16) for _ in range(3)]
            for b, nm in enumerate(("w_branch_a", "w_branch_b", "w_branch_c")):
                load_w(wbr[b], W[nm][l].rearrange("(k p) n -> p k n", p=128), 1024, 4, stg)
            wo = A.alloc([8, 1024], BF16)
            load_w(wo, W["w_out"][l].rearrange("(k p) n -> p k n", p=128), 1024, 8, stg)
            yin = [A.alloc([3, 4, 512], BF16) for _ in range(1)]
            gin = A.alloc([24, 512], BF16)
            mT = A.alloc([8, 512], BF16)
            t1 = [A.alloc([512]) for _ in range(3)]
            xts = [A.alloc([1024]) for _ in range(2)]
            t2 = A.alloc([512])
            GTv = GT.rearrange("(c p) t -> p c t", p=128)
            t0 = tok_lo
            while t0 < T:
                ST = min(512, T - t0)
                if t0 < LC:
                    ST = min(ST, LC - t0)
                cnd = 1 if t0 < LC else 0
                yb_ = yin[0]
                for b in range(3):
                    P.dma("sp", yb_[:, b, :, 0:ST], YT[b, :, :, t0:t0 + ST], reads=[dYT], writes=[yb_])
                P.dma("sp", gin[:, :, 0:ST], GTv[:, :, t0:t0 + ST], reads=[dGT], writes=[gin])
                for oc in range(8):
                    for b in range(3):
                        ps = PS[b + 3 * (oc % 2)]
                        for k in range(4):
                            P.op("pe", lambda e, b=b, k=k, oc=oc, ps=ps, ST=ST, yb_=yb_: e.matmul(
                                ps[:, 0:ST], lhsT=wbr[b][:, k, oc * 128:(oc + 1) * 128], rhs=yb_[:, b, k, 0:ST], start=(k == 0), stop=(k == 3)),
                                reads=[wbr[b], yb_], writes=[ps])
                        P.op("dve", lambda e, b=b, oc=oc, ps=ps, ST=ST: e.tensor_tensor(
                            out=t1[b][:, 0:ST], in0=ps[:, 0:ST], in1=gin[:, b * 8 + oc, 0:ST], op=ALU.mult), reads=[ps, gin], writes=[t1[b]])
                    P.op("pool", lambda e, ST=ST: e.tensor_add(out=t1[0][:, 0:ST], in0=t1[0][:, 0:ST], in1=t1[1][:, 0:ST]), reads=[t1[0], t1[1]], writes=[t1[0]])
                    P.op("pool", lambda e, ST=ST, oc=oc: e.tensor_add(out=mT[:, oc, 0:ST], in0=t1[0][:, 0:ST], in1=t1[2][:, 0:ST]), reads=[t1[0], t1[2]], writes=[mT])
                for sub in range(ST // 128):
                    tok0 = t0 + sub * 128
                    xt = xts[sub % 2]
                    P.dma("sp", xt[:], Xr[tok0:tok0 + 128, :], reads=[dX], writes=[xt])
                    for n in range(2):
                        ps = PS[6 + n]
                        for k in range(8):
                            P.op("pe", lambda e, k=k, n=n, ps=ps, sub=sub: e.matmul(
                                ps[:], lhsT=mT[:, k, sub * 128:(sub + 1) * 128], rhs=wo[:, k, n * 512:(n + 1) * 512], start=(k == 0), stop=(k == 7)),
                                reads=[mT, wo], writes=[ps])
                        P.op("dve", lambda e, n=n, ps=ps, cnd=cnd: e.tensor_tensor(out=t2[:], in0=ps[:], in1=Grow[:, 0, cnd, n * 512:(n + 1) * 512], op=ALU.mult),
                             reads=[ps, Grow], writes=[t2])
                        P.op("pool", lambda e, n=n, xt=xt: e.tensor_add(out=xt[:, n * 512:(n + 1) * 512], in0=xt[:, n * 512:(n + 1) * 512], in1=t2[:]),
                             reads=[xt, t2], writes=[xt])
                    P.dma("act", X[tok0:tok0 + 128, :], xt[:], reads=[xt], writes=[dX])
                t0 += ST
            P.barrier()
            A.reset(l_mark)

            if stop == pre + 'M':
                break
            last = (l == depth - 1)
            stg = [A.alloc([8, 128]) for _ in range(2)]
            w1 = A.alloc([8, 4096], BF16)
            load_w(w1, W["w_ff1"][l].rearrange("(k p) n -> p k n", p=128), 4096, 8, stg)
            w2 = A.alloc([32, 1024], BF16)
            w2v = W["w_ff2"][l].rearrange("(k p) n -> p k n", p=128)
            for kq in range(4):
                load_w(w2, w2v[:, kq * 8:(kq + 1) * 8, :], 1024, 8, stg, k0=kq * 8)
            fg = A.alloc([1024])
            if last:
                P.dma("sp", fg[:], W["final_norm_g"].partition_broadcast(128), writes=[fg])
            xts = [A.alloc([1024]) for _ in range(2)]
            junk = A.alloc([1024], BF16)
            xn = A.alloc([1024], BF16)
            ss = A.alloc([4])
            h2T = A.alloc([8, 256], BF16)
            aT = A.alloc([32, 256], BF16)
            rl = [A.alloc([256]) for _ in range(2)]
            t2 = A.alloc([512])
            t0 = tok_lo
            while t0 < T:
                ST = 256
                cnd = 1 if t0 < LC else 0
                for sub in range(2):
                    tok0 = t0 + sub * 128
                    xt = xts[sub]
                    P.dma("sp", xt[:], X[tok0:tok0 + 128, :], reads=[dX], writes=[xt])
                    P.op("act", lambda e, xt=xt: e.activation(out=junk[:], in_=xt[:], func=AF.Square, accum_out=ss[:, 0:1]),
                         reads=[xt], writes=[junk, ss])
                    P.op("act", lambda e: e.activation(out=ss[:, 1:2], in_=ss[:, 0:1], func=AF.Sqrt, scale=1.0 / D, bias=EPS), reads=[ss], writes=[ss])
                    P.op("dve", lambda e: e.reciprocal(out=ss[:, 1:2], in_=ss[:, 1:2]), reads=[ss], writes=[ss])
                    P.op("act", lambda e, xt=xt: e.activation(out=xn[:], in_=xt[:], func=AF.Copy, scale=ss[:, 1:2]), reads=[xt, ss], writes=[xn])
                    pT = PS[7]
                    for k in range(8):
                        P.op("pe", lambda e, k=k: e.transpose(out=psbf(7)[:, k * 128:(k + 1) * 128], in_=xn[:, k * 128:(k + 1) * 128],
                                                               identity=identb[:]), reads=[xn, identb], writes=[pT])
                    for k in range(8):
                        if k % 2:
                            P.op("dve", lambda e, k=k, sub=sub, cnd=cnd: e.tensor_scalar(
                                out=h2T[:, k, sub * 128:(sub + 1) * 128], in0=psbf(7)[:, k * 128:(k + 1) * 128],
                                scalar1=AB[:, 1, k, cnd:cnd + 1], scalar2=modF[:, 24 + k, cnd:cnd + 1], op0=ALU.mult, op1=ALU.add),
                                reads=[pT, AB, modF], writes=[h2T])
                        else:
                            P.op("act", lambda e, k=k, sub=sub, cnd=cnd: e.activation(
                                out=h2T[:, k, sub * 128:(sub + 1) * 128], in_=psbf(7)[:, k * 128:(k + 1) * 128], func=AF.Identity,
                                scale=AB[:, 1, k, cnd:cnd + 1], bias=modF[:, 24 + k, cnd:cnd + 1]), reads=[pT, AB, modF], writes=[h2T])
                for oc in range(32):
                    ps = PS[oc % 4]
                    r_ = rl[oc % 2]
                    for k in range(8):
                        P.op("pe", lambda e, k=k, oc=oc, ps=ps: e.matmul(ps[:, 0:256], lhsT=w1[:, k, oc * 128:(oc + 1) * 128], rhs=h2T[:, k, :],
                                                                        start=(k == 0), stop=(k == 7)), reads=[w1, h2T], writes=[ps])
                    P.op("act", lambda e, ps=ps, r_=r_: e.activation(out=r_[:], in_=ps[:, 0:256], func=AF.Relu), reads=[ps], writes=[r_])
                    P.op("pool", lambda e, oc=oc, r_=r_: e.tensor_tensor(out=aT[:, oc, :], in0=r_[:], in1=r_[:], op=ALU.mult), reads=[r_], writes=[aT])
                for sub in range(2):
                    tok0 = t0 + sub * 128
                    xt = xts[sub]
                    for n in range(2):
                        ps = PS[4 + n]
                        for k in range(32):
                            P.op("pe", lambda e, k=k, n=n, ps=ps, sub=sub: e.matmul(
                                ps[:], lhsT=aT[:, k, sub * 128:(sub + 1) * 128], rhs=w2[:, k, n * 512:(n + 1) * 512], start=(k == 0), stop=(k == 31)),
                                reads=[aT, w2], writes=[ps])
                        P.op("dve", lambda e, n=n, ps=ps, cnd=cnd: e.tensor_tensor(out=t2[:], in0=ps[:], in1=Grow[:, 1, cnd, n * 512:(n + 1) * 512], op=ALU.mult),
                             reads=[ps, Grow], writes=[t2])
                        P.op("pool", lambda e, n=n, xt=xt: e.tensor_add(out=xt[:, n * 512:(n + 1) * 512], in0=xt[:, n * 512:(n + 1) * 512], in1=t2[:]),
                             reads=[xt, t2], writes=[xt])
                    if not last:
                        P.dma("act", X[tok0:tok0 + 128, :], xt[:], reads=[xt], writes=[dX])
                    else:
                        P.op("act", lambda e, xt=xt: e.activation(out=junk[:], in_=xt[:], func=AF.Square, accum_out=ss[:, 2:3]),
                             reads=[xt], writes=[junk, ss])
                        P.op("act", lambda e: e.activation(out=ss[:, 3:4], in_=ss[:, 2:3], func=AF.Sqrt, scale=1.0 / D, bias=EPS), reads=[ss], writes=[ss])
                        P.op("dve", lambda e: e.reciprocal(out=ss[:, 3:4], in_=ss[:, 3:4]), reads=[ss], writes=[ss])
                        P.op("dve", lambda e, xt=xt: e.scalar_tensor_tensor(out=xt[:], in0=xt[:], scalar=ss[:, 3:4], in1=fg[:], op0=ALU.mult, op1=ALU.mult),
                             reads=[xt, ss, fg], writes=[xt])
                        P.dma("act", out_d[tok0 - LC:tok0 - LC + 128, :], xt[:], reads=[xt])
                t0 += ST
            P.barrier()
            A.reset(base_mark)
            if stop == pre + 'F':
                break

        P.emit()
    return nc


WSHAPES = {
    'norm1_g': (DEPTH, D), 'norm2_g': (DEPTH, D), 'w_mod': (DEPTH, D, 6 * D), 'b_mod': (DEPTH, 6 * D),
    'w_in': (DEPTH, D, D_IN),
    'ssm_lam_re': (DEPTH, 2, 32, 64), 'ssm_lam_im': (DEPTH, 2, 32, 64), 'ssm_log_dt': (DEPTH, 2, 32),
    'ssm_b_re': (DEPTH, 2, 32, 64, 16), 'ssm_b_im': (DEPTH, 2, 32, 64, 16),
    'ssm_c_re': (DEPTH, 2, 32, 16, 64), 'ssm_c_im': (DEPTH, 2, 32, 16, 64),
    'ssm_d': (DEPTH, 512), 'ssm_w_glu': (DEPTH, 512, 512),
    'dn_conv_w': (DEPTH, 5, 1536), 'dn_a_log': (DEPTH, 2, 4), 'dn_dt_bias': (DEPTH, 2, 4), 'dn_norm_g': (DEPTH, 128),
    'attn_sink': (DEPTH, 8),
    'w_branch_a': (DEPTH, 512, D), 'w_branch_b': (DEPTH, 512, D), 'w_branch_c': (DEPTH, 512, D),
    'w_out': (DEPTH, D, D), 'w_ff1': (DEPTH, D, 4 * D), 'w_ff2': (DEPTH, 4 * D, D), 'final_norm_g': (D,),
}


def make_in_maps(inputs, nb, L):
    x = np.asarray(inputs['x'], np.float32)
    ctx = np.asarray(inputs['ctx'], np.float32)
    c = np.asarray(inputs['c'], np.float32)
    c_ctx = np.asarray(inputs['c_ctx'], np.float32)
    shared = {nm: np.ascontiguousarray(np.asarray(inputs[nm], np.float32)) for nm in WSHAPES}
    shared["ident"] = np.eye(128, dtype=np.float32)
    q_ = np.arange(128)[:, None]
    j_ = np.arange(128)[None, :]
    mk = np.zeros((128, 3, 128), np.float32)
    mk[:, 0, :] = np.where(j_ >= q_, 0.0, -30000.0)
    mk[:, 1, :] = np.where(j_ <= q_, 0.0, -30000.0)
    mk[:, 2, :] = -30000.0
    shared["masks"] = mk
    same = (q_ // 64) == (j_ // 64)
    dc = np.zeros((128, 7, 128), np.float32)
    dc[:, 0, :] = (same & (q_ <= j_))
    dc[:, 1, :] = (same & (q_ >= j_))
    dc[:, 2, :] = same
    dc[:, 3, :] = (q_ != j_)
    dc[:, 4, :] = np.where(same & (j_ < q_), 0.0, -30000.0)
    dc[:, 5, :] = np.where(same & (j_ > q_), 0.0, -30000.0)
    shared["dncon"] = dc
    sl = np.zeros((128, 2, 128), np.float32)
    sl[0, 0, :] = 1.0
    sl[64, 1, :] = 1.0
    shared["dnsel"] = sl
    NL = L // 128
    rp = np.zeros((128, NL + 1), np.float32)
    for i in range(NL):
        rp[:, i] = 2 * i + (np.arange(128) >= 64)
    rp[:, NL] = np.arange(128) % 64
    shared["ropepos"] = rp
    shared["invfreq"] = np.tile((10000.0 ** (-np.arange(16, dtype=np.float32) / 16))[None, :], (128, 1)).astype(np.float32)
    maps = []
    for b in range(nb):
        m = dict(shared)
        m["xin"] = np.ascontiguousarray(np.concatenate([ctx[b], x[b]], axis=0))
        m["cc"] = np.ascontiguousarray(np.stack([c[b], c_ctx], axis=0))
        maps.append(m)
    return maps


_NC_CACHE = {}


def kernel(**inputs):
    x = np.asarray(inputs['x'])
    nb, L = x.shape[0], x.shape[1]
    if L not in _NC_CACHE:
        _NC_CACHE[L] = build(L)
    nc = _NC_CACHE[L]
    maps = make_in_maps(inputs, nb, L)
    res = run_bass_kernel_spmd(nc, maps, core_ids=list(range(nb)))
    return np.stack([np.asarray(r["out"], np.float32) for r in res.results], axis=0)
```

```python
import contextlib
import numpy as np
import concourse.bass as bass
import concourse.mybir as mybir
from concourse.bass_utils import run_bass_kernel_spmd

F32 = mybir.dt.float32
BF16 = mybir.dt.bfloat16
I32 = mybir.dt.int32
AF = mybir.ActivationFunctionType
ALU = mybir.AluOpType
AX = mybir.AxisListType

D = 1024
LC = 256
DEPTH = 2
EPS = 1e-6
D_IN = 6416
ENG = ("pe", "act", "dve", "pool", "sp")
NDMA = 24


class Buf:
    __slots__ = ("w", "r", "ap", "excl")

    def __init__(self, ap=None, excl=False):
        self.w = None
        self.r = []
        self.ap = ap
        self.excl = excl

    def __getitem__(self, k):
        return self.ap[k]


class _Rec:
    def __getattr__(self, name):
        def f(*a, **k):
            return (name, a, k)
        return f


_REC = _Rec()


class Prog:
    def __init__(self, nc):
        self.nc = nc
        self.q = {e: [] for e in ENG}
        self.cnt = {e: 0 for e in ENG}
        self.seen = {e: {} for e in ENG}
        self.dma_q = {"sp": 0, "act": 0, "pool": 0}
        self.limit = None
        self.dma_n = [0] * NDMA

    def _need(self, eng, waits, tok):
        if tok is None:
            return
        key, val = tok
        if key == eng and eng == "pe":
            return
        if self.seen[eng].get(key, 0) >= val:
            return
        if waits.get(key, 0) < val:
            waits[key] = val

    def _deps(self, eng, reads, writes):
        waits = {}
        for b in reads:
            self._need(eng, waits, b.w)
            if b.excl:
                for t in b.r:
                    if t[0] != eng:
                        self._need(eng, waits, t)
        for b in writes:
            self._need(eng, waits, b.w)
            for t in b.r:
                self._need(eng, waits, t)
        for k, v in waits.items():
            self.seen[eng][k] = v
        return waits

    def _mark(self, tok, reads, writes):
        for b in reads:
            b.r.append(tok)
            if len(b.r) > 64:
                b.r = b.r[-48:]
        for b in writes:
            b.w = tok
            b.r = []

    def op(self, eng, fn, reads=(), writes=()):
        if self.limit is not None:
            self.limit -= 1
            if self.limit < 0:
                return None
        waits = self._deps(eng, reads, writes)
        self.cnt[eng] += 1
        tok = (eng, self.cnt[eng])
        self.q[eng].append((list(waits.items()), fn(_REC), (eng, 1)))
        self._mark(tok, reads, writes)
        return tok

    def dma(self, qeng, out_ap, in_ap, reads=(), writes=(), **kw):
        if self.limit is not None:
            self.limit -= 1
            if self.limit < 0:
                return None
        lo, n = {"sp": (0, 12), "act": (12, 6), "pool": (18, 6)}[qeng]
        i = lo + self.dma_q[qeng] % n
        self.dma_q[qeng] += 1
        waits = {}
        if self.dma_n[i] > 0:
            self._need(qeng, waits, (("d", i), 16 * self.dma_n[i]))
        for k, v in waits.items():
            self.seen[qeng][k] = v
        w2 = self._deps(qeng, reads, writes)
        waits.update(w2)
        self.dma_n[i] += 1
        tok = (("d", i), 16 * self.dma_n[i])
        kw2 = dict(kw)
        kw2["out"] = out_ap
        kw2["in_"] = in_ap
        self.q[qeng].append((list(waits.items()), ("dma_start", (), kw2), (("d", i), 16)))
        self._mark(tok, reads, writes)
        return tok

    def barrier(self):
        toks = [(e, self.cnt[e]) for e in ENG if self.cnt[e] > 0]
        toks += [(("d", i), 16 * self.dma_n[i]) for i in range(NDMA) if self.dma_n[i] > 0]
        for e in ENG:
            waits = {}
            for t in toks:
                if t[0] == e:
                    continue
                self._need(e, waits, t)
            for k, v in waits.items():
                self.seen[e][k] = v
            if waits:
                self.q[e].append((list(waits.items()), None, None))

    def emit(self):
        nc = self.nc
        self.barrier()
        with contextlib.ExitStack() as st:
            sems = {}
            for e in ENG:
                sems[e] = st.enter_context(nc.semaphore("s_" + e))
            for i in range(NDMA):
                sems[("d", i)] = st.enter_context(nc.semaphore("s_d%d" % i))
            block = st.enter_context(nc.Block())

            def run(e, name):
                for waits, fn, inc in self.q[name]:
                    for k, v in waits:
                        e.wait_ge(sems[k], v)
                    if fn is not None:
                        try:
                            ins = getattr(e, fn[0])(*fn[1], **fn[2])
                        except Exception:
                            print("FAILED OP", fn[0], [str(v)[:80] for v in fn[2].values()], flush=True)
                            raise
                        ins.then_inc(sems[inc[0]], inc[1])

            @block.tensor
            def _(e):
                run(e, "pe")

            @block.scalar
            def _(e):
                run(e, "act")

            @block.vector
            def _(e):
                run(e, "dve")

            @block.gpsimd
            def _(e):
                run(e, "pool")

            @block.sync
            def _(e):
                run(e, "sp")


class Arena:
    def __init__(self, nc, st, words):
        self.t = st.enter_context(nc.sbuf_tensor("arena", [128, words], F32))
        self.words = words
        self.off = 0

    def mark(self):
        return self.off

    def reset(self, m):
        self.off = m

    def alloc(self, free, dt=F32):
        free = list(free)
        n = int(np.prod(free))
        bpe = 4 if dt in (F32, I32) else 2
        w = (n * bpe + 3) // 4
        w = (w + 7) // 8 * 8
        assert self.off + w <= self.words, ("arena overflow", self.off, w, self.words)
        ap = self.t[:, self.off:self.off + w]
        self.off += w
        if bpe == 2:
            ap = ap.bitcast(dt)[:, 0:n]
        elif dt != F32:
            ap = ap.bitcast(dt)[:, 0:n]
        else:
            ap = ap[:, 0:n]
        if len(free) == 2:
            ap = ap.rearrange("p (a b) -> p a b", a=free[0])
        elif len(free) == 3:
            ap = ap.rearrange("p (a b c) -> p a b c", a=free[0], b=free[1])
        elif len(free) == 4:
            ap = ap.rearrange("p (a b c d) -> p a b c d", a=free[0], b=free[1], c=free[2])
        return Buf(ap)


def build(L, depth=DEPTH, dbg=None, stop=None):
    T = LC + L
    NT = T // 128
    nc = bass.Bass("TRN2", target_bir_lowering=False)
    dbg = dbg or set()

    def din(name, shape):
        return nc.dram_tensor(name, list(shape), F32, kind="ExternalInput").ap()

    def dscr(name, shape, dt=F32):
        kind = "ExternalOutput" if name in dbg else "Internal"
        return nc.dram_tensor(name, list(shape), dt, kind=kind).ap()

    xin = din("xin", [T, D])
    cc = din("cc", [2, D])
    ident_d = din("ident", [128, 128])
    W = {}
    for nm, shp in WSHAPES.items():
        W[nm] = din(nm, shp)
    out_d = nc.dram_tensor("out", [L, D], F32, kind="ExternalOutput").ap()

    X = dscr("X", [T, D])
    UT = dscr("UT", [512, T], BF16)
    QKVT = dscr("QKVT", [1536, T])
    TM = dscr("TM", [T, 1296])
    GT = dscr("GT", [3072, T], BF16)
    YTf = dscr("YT", [3 * 512, T], BF16)
    YT = YTf.rearrange("(b k p) t -> b p k t", b=3, p=128)
    masks_d = din("masks", [128, 3, 128])
    ropepos_d = din("ropepos", [128, L // 128 + 1])
    invfreq_d = din("invfreq", [128, 16])
    dncon_d = din("dncon", [128, 7, 128])
    dnsel_d = din("dnsel", [128, 2, 128])

    with contextlib.ExitStack() as st:
        P = Prog(nc)
        A = Arena(nc, st, 50 * 1024)
        PS = [Buf(st.enter_context(nc.psum_tensor("ps%d" % i, [128, 512], F32))[:], excl=True) for i in range(8)]

        def psbf(i):
            return PS[i].ap.bitcast(BF16)

        ident = A.alloc([128])
        identb = A.alloc([128], BF16)
        ones = A.alloc([128])
        P.dma("sp", ident[:], ident_d, writes=[ident])
        P.op("dve", lambda e: e.tensor_copy(out=identb[:], in_=ident[:]), reads=[ident], writes=[identb])
        P.op("pool", lambda e: e.memset(ones[:], 1.0), writes=[ones])
        base_mark = A.mark()
        dX = Buf()
        dUT, dQKVT, dTM, dGT, dYT = Buf(), Buf(), Buf(), Buf(), Buf()

        for l in range(depth):
            A.reset(base_mark)
            pre = '' if l == 0 else '1'
            Xr = xin if l == 0 else X
            condT = A.alloc([2, 8])
            for c in range(2):
                P.dma("sp", condT[:, c, :], cc[c].rearrange("(k p) -> p k", p=128), writes=[condT],
                      allow_slow_non_contiguous=True)
            P.op("act", lambda e: e.activation(out=condT[:], in_=condT[:], func=AF.Silu), reads=[condT], writes=[condT])
            bmodF = A.alloc([48])
            modF = A.alloc([48, 2])
            Grow = A.alloc([2, 2, 1024])
            ng = A.alloc([2, 8])
            AB = A.alloc([2, 8, 2])
            l_mark = A.mark()
            condB = A.alloc([16, 128])
            for k in range(8):
                for c in range(2):
                    P.op("dve", lambda e, k=k, c=c: e.tensor_scalar(
                        out=condB[:, k * 2 + c, :], in0=ones[:], scalar1=condT[:, c, k:k + 1], scalar2=None,
                        op0=ALU.mult), reads=[condT, ones], writes=[condB])
            P.dma("sp", bmodF[:], W["b_mod"][l].rearrange("(c p) -> p c", p=128), writes=[bmodF],
                  allow_slow_non_contiguous=True)
            if stop == pre + '0a':
                break
            wms = [A.alloc([8, 512]) for _ in range(2)]
            bmodB = [A.alloc([512]) for _ in range(2)]
            wmod_v = W["w_mod"][l].rearrange("(k p) n -> p k n", p=128)
            for oc4 in range(12):
                if stop == '0b' and oc4 == 1:
                    break
                if stop == '0c' and oc4 == 5:
                    break
                wm = wms[oc4 % 2]
                P.dma("sp", wm[:], wmod_v[:, :, oc4 * 512:(oc4 + 1) * 512], writes=[wm])
                if oc4 in (4, 5, 10, 11):
                    gate = 0 if oc4 < 6 else 1
                    half = oc4 % 2 if oc4 < 6 else (oc4 - 10)
                    bb = bmodB[oc4 % 2]
                    P.dma("sp", bb[:], W["b_mod"][l][oc4 * 512:(oc4 + 1) * 512].partition_broadcast(128), writes=[bb])
                    for c in range(2):
                        ps = PS[c]
                        for k in range(8):
                            P.op("pe", lambda e, k=k, c=c, ps=ps, wm=wm: e.matmul(
                                ps[:], lhsT=condB[:, k * 2 + c, :], rhs=wm[:, k, :], start=(k == 0), stop=(k == 7)),
                                reads=[condB, wm], writes=[ps])
                        P.op("dve", lambda e, c=c, ps=ps, bb=bb, gate=gate, half=half: e.tensor_tensor(
                            out=Grow[:, gate, c, half * 512:(half + 1) * 512], in0=ps[:], in1=bb[:], op=ALU.add),
                            reads=[ps, bb], writes=[Grow])
                else:
                    ps = PS[2]
                    for j in range(4):
                        oc = oc4 * 4 + j
                        for k in range(8):
                            P.op("pe", lambda e, k=k, j=j, oc=oc, ps=ps, wm=wm: e.matmul(
                                ps[:, oc * 2:oc * 2 + 2], lhsT=wm[:, k, j * 128:(j + 1) * 128], rhs=condT[:, :, k],
                                start=(k == 0), stop=(k == 7)), reads=[condT, wm], writes=[ps])
                    P.op("dve", lambda e, oc4=oc4, ps=ps: e.tensor_tensor(
                        out=modF[:, oc4 * 4:oc4 * 4 + 4, :],
                        in0=ps[:, oc4 * 8:oc4 * 8 + 8].rearrange("p (a b) -> p a b", b=2),
                        in1=bmodF[:, oc4 * 4:oc4 * 4 + 4].unsqueeze(2).to_broadcast([128, 4, 2]), op=ALU.add),
                        reads=[ps, bmodF], writes=[modF])
            if stop in tuple(pre + q for q in ('0b', '0c', '0d')):
                break
            P.dma("sp", ng[:, 0, :], W["norm1_g"][l].rearrange("(k p) -> p k", p=128), writes=[ng], allow_slow_non_contiguous=True)
            P.dma("sp", ng[:, 1, :], W["norm2_g"][l].rearrange("(k p) -> p k", p=128), writes=[ng], allow_slow_non_contiguous=True)
            for wn in range(2):
                scb = 8 + 24 * wn
                P.op("dve", lambda e, wn=wn, scb=scb: e.tensor_scalar(
                    out=AB[:, wn, :, :], in0=modF[:, scb:scb + 8, :], scalar1=1.0, scalar2=None, op0=ALU.add),
                    reads=[modF], writes=[AB])
                P.op("dve", lambda e, wn=wn: e.tensor_tensor(
                    out=AB[:, wn, :, :], in0=AB[:, wn, :, :], in1=ng[:, wn, :].unsqueeze(2).to_broadcast([128, 8, 2]),
                    op=ALU.mult), reads=[AB, ng], writes=[AB])
            P.barrier()
            A.reset(l_mark)

            if stop == pre + '0':
                break
            win = A.alloc([8, D_IN], BF16)
            wst = [A.alloc([8, 256]) for _ in range(2)]
            win_v = W["w_in"][l].rearrange("(k p) n -> p k n", p=128)
            nchunk = (D_IN + 255) // 256
            for ci in range(nchunk):
                c0 = ci * 256
                cw = min(256, D_IN - c0)
                ws = wst[ci % 2]
                P.dma("sp", ws[:, :, 0:cw], win_v[:, :, c0:c0 + cw], writes=[ws])
                P.op("pool" if ci % 2 else "dve", lambda e, ws=ws, c0=c0, cw=cw: e.tensor_copy(
                    out=win[:, :, c0:c0 + cw], in_=ws[:, :, 0:cw]), reads=[ws], writes=[win])
            xts = [A.alloc([1024]) for _ in range(2)]
            junk = A.alloc([1024])
            xn = A.alloc([1024], BF16)
            ss = A.alloc([2])
            hT = A.alloc([8, 512], BF16)
            stF = [A.alloc([512]) for _ in range(2)]
            stFb = [A.alloc([512], BF16) for _ in range(2)]
            stT = [A.alloc([1296]) for _ in range(2)]
            nsuper = (T + 511) // 512
            evi = 0
            for s_ in range(nsuper):
                t0 = s_ * 512
                ST = min(512, T - t0)
                nsub = ST // 128
                for sub in range(nsub):
                    tok0 = t0 + sub * 128
                    cnd = 1 if tok0 < LC else 0
                    xt = xts[sub % 2]
                    P.dma("sp", xt[:], Xr[tok0:tok0 + 128, :], reads=[dX], writes=[xt])
                    P.op("act", lambda e, xt=xt: e.activation(out=junk[:], in_=xt[:], func=AF.Square, accum_out=ss[:, 0:1]),
                         reads=[xt], writes=[junk, ss])
                    P.op("act", lambda e: e.activation(out=ss[:, 1:2], in_=ss[:, 0:1], func=AF.Sqrt, scale=1.0 / D, bias=EPS),
                         reads=[ss], writes=[ss])
                    P.op("dve", lambda e: e.reciprocal(out=ss[:, 1:2], in_=ss[:, 1:2]), reads=[ss], writes=[ss])
                    P.op("act", lambda e, xt=xt: e.activation(out=xn[:], in_=xt[:], func=AF.Copy, scale=ss[:, 1:2]),
                         reads=[xt, ss], writes=[xn])
                    pT = PS[7]
                    for k in range(8):
                        P.op("pe", lambda e, k=k: e.transpose(out=psbf(7)[:, k * 128:(k + 1) * 128], in_=xn[:, k * 128:(k + 1) * 128],
                                                               identity=identb[:]), reads=[xn, identb], writes=[pT])
                    for k in range(8):
                        if k % 2:
                            P.op("dve", lambda e, k=k, sub=sub, cnd=cnd: e.tensor_scalar(
                                out=hT[:, k, sub * 128:(sub + 1) * 128], in0=psbf(7)[:, k * 128:(k + 1) * 128],
                                scalar1=AB[:, 0, k, cnd:cnd + 1], scalar2=modF[:, k, cnd:cnd + 1], op0=ALU.mult, op1=ALU.add),
                                reads=[pT, AB, modF], writes=[hT])
                        else:
                            P.op("act", lambda e, k=k, sub=sub, cnd=cnd: e.activation(
                                out=hT[:, k, sub * 128:(sub + 1) * 128], in_=psbf(7)[:, k * 128:(k + 1) * 128], func=AF.Identity,
                                scale=AB[:, 0, k, cnd:cnd + 1], bias=modF[:, k, cnd:cnd + 1]), reads=[pT, AB, modF], writes=[hT])
                fm = [("U", i, i * 128) for i in range(4)] + [("Q", i, 512 + i * 128) for i in range(12)] + \
                     [("G", i, 3344 + i * 128) for i in range(24)]
                for (kind, i, c0) in fm:
                    ps = PS[evi % 4]
                    for k in range(8):
                        P.op("pe", lambda e, k=k, c0=c0, ps=ps, ST=ST: e.matmul(
                            ps[:, 0:ST], lhsT=win[:, k, c0:c0 + 128], rhs=hT[:, k, 0:ST], start=(k == 0), stop=(k == 7)),
                            reads=[win, hT], writes=[ps])
                    if kind == "Q":
                        sg = stF[evi % 2]
                        P.op("dve", lambda e, sg=sg, ps=ps, ST=ST: e.tensor_copy(out=sg[:, 0:ST], in_=ps[:, 0:ST]), reads=[ps], writes=[sg])
                        P.dma("pool", QKVT[i * 128:(i + 1) * 128, t0:t0 + ST], sg[:, 0:ST], reads=[sg], writes=[dQKVT])
                    elif kind == "U":
                        sg = stFb[evi % 2]
                        P.op("act", lambda e, sg=sg, ps=ps, ST=ST: e.activation(out=sg[:, 0:ST], in_=ps[:, 0:ST], func=AF.Copy), reads=[ps], writes=[sg])
                        P.dma("pool", UT[i * 128:(i + 1) * 128, t0:t0 + ST], sg[:, 0:ST], reads=[sg], writes=[dUT])
                    else:
                        sg = stFb[evi % 2]
                        P.op("act", lambda e, sg=sg, ps=ps, ST=ST: e.activation(out=sg[:, 0:ST], in_=ps[:, 0:ST], func=AF.Sigmoid), reads=[ps], writes=[sg])
                        P.dma("pool", GT[i * 128:(i + 1) * 128, t0:t0 + ST], sg[:, 0:ST], reads=[sg], writes=[dGT])
                    evi += 1
                for sub in range(nsub):
                    sg = stT[sub % 2]
                    for (c0, cw) in ((2048, 512), (2560, 512), (3072, 272)):
                        ps = PS[4 + (evi % 3)]
                        evi += 1
                        for k in range(8):
                            P.op("pe", lambda e, k=k, c0=c0, cw=cw, ps=ps, sub=sub: e.matmul(
                                ps[:, 0:cw], lhsT=hT[:, k, sub * 128:(sub + 1) * 128], rhs=win[:, k, c0:c0 + cw],
                                start=(k == 0), stop=(k == 7)), reads=[win, hT], writes=[ps])
                        P.op("dve", lambda e, sg=sg, ps=ps, c0=c0, cw=cw: e.tensor_copy(
                            out=sg[:, c0 - 2048:c0 - 2048 + cw], in_=ps[:, 0:cw]), reads=[ps], writes=[sg])
                    P.dma("pool", TM[t0 + sub * 128:t0 + (sub + 1) * 128, :], sg[:], reads=[sg], writes=[dTM])
            P.barrier()
            A.reset(l_mark)
            with_ctx = l < depth - 1
            if stop == pre + 'A':
                break
            NL = L // 128
            sinkB = A.alloc([8])
            P.dma("sp", sinkB[:], W["attn_sink"][l].partition_broadcast(128), writes=[sinkB])
            nsinkB = A.alloc([8])
            P.op("dve", lambda e: e.tensor_scalar(out=nsinkB[:], in0=sinkB[:], scalar1=-1.0, scalar2=None, op0=ALU.mult),
                 reads=[sinkB], writes=[nsinkB])
            maskf = A.alloc([3, 128])
            maskb = A.alloc([3, 128], BF16)
            P.dma("sp", maskf[:], masks_d, writes=[maskf])
            P.op("dve", lambda e: e.tensor_copy(out=maskb[:], in_=maskf[:]), reads=[maskf], writes=[maskb])
            pos = A.alloc([NL + 1])
            invf = A.alloc([16])
            P.dma("sp", pos[:], ropepos_d, writes=[pos])
            P.dma("sp", invf[:], invfreq_d, writes=[invf])
            yy = A.alloc([2, NL + 1, 16])
            yi = A.alloc([2, NL + 1, 16], I32)
            yf = A.alloc([2, NL + 1, 16])
            tab = A.alloc([2, NL + 1, 16])
            for sc_ in range(2):
                P.op("dve", lambda e, sc_=sc_: e.tensor_tensor(
                    out=yy[:, sc_, :, :], in0=pos[:].unsqueeze(2).to_broadcast([128, NL + 1, 16]),
                    in1=invf[:].unsqueeze(1).to_broadcast([128, NL + 1, 16]), op=ALU.mult), reads=[pos, invf], writes=[yy])
            P.op("dve", lambda e: e.tensor_scalar(out=yy[:, 0, :, :], in0=yy[:, 0, :, :], scalar1=1.0 / (2 * np.pi), scalar2=None,
                                                  op0=ALU.mult), reads=[yy], writes=[yy])
            P.op("dve", lambda e: e.tensor_scalar(out=yy[:, 1, :, :], in0=yy[:, 1, :, :], scalar1=1.0 / (2 * np.pi), scalar2=0.25,
                                                  op0=ALU.mult, op1=ALU.add), reads=[yy], writes=[yy])
            P.op("dve", lambda e: e.tensor_copy(out=yi[:], in_=yy[:]), reads=[yy], writes=[yi])
            P.op("dve", lambda e: e.tensor_copy(out=yf[:], in_=yi[:]), reads=[yi], writes=[yf])
            P.op("dve", lambda e: e.tensor_sub(out=yy[:], in0=yy[:], in1=yf[:]), reads=[yy, yf], writes=[yy])
            P.op("act", lambda e: e.activation(out=tab[:], in_=yy[:], func=AF.Sin, scale=2 * np.pi), reads=[yy], writes=[tab])
            if stop == pre + 'C0':
                break
            KT = A.alloc([T], BF16)
            Vr = A.alloc([NT, 128], BF16)
            QT = A.alloc([4, T], BF16)
            qin = [A.alloc([768]) for _ in range(2)]
            qk = A.alloc([640], BF16)
            tmp = [A.alloc([10, 16]) for _ in range(4)]
            qkr = A.alloc([640])
            for tt in range(NT):
                if stop == 'C1a' and tt == 2:
                    break
                if stop == 'C1b' and tt == 3:
                    break
                qi = qin[tt % 2]
                P.dma("sp", qi[:], TM[tt * 128:(tt + 1) * 128, 528:1296], reads=[dTM], writes=[qi])
                import os
                CUT = int(os.environ.get("CUT", "9"))
                if CUT < 1:
                    continue
                P.op("dve", lambda e, qi=qi, tt=tt: e.tensor_copy(out=Vr[:, tt, :], in_=qi[:, 640:768]), reads=[qi], writes=[Vr])
                if CUT < 2:
                    continue
                if tt < 2:
                    for kq in range(2):
                        P.op("dve", lambda e, qi=qi, kq=kq: e.tensor_copy(out=qk[:, 0:512].rearrange("p (j k d) -> p j k d", j=4, k=2)[:, :, kq, :],
                                                                         in_=qi[:, kq * 256:(kq + 1) * 256].rearrange("p (j d) -> p j d", j=4)), reads=[qi], writes=[qk])
                    P.op("dve", lambda e, qi=qi: e.tensor_copy(out=qk[:, 512:640], in_=qi[:, 512:640]), reads=[qi], writes=[qk])
                else:
                    i = tt - 2
                    qv = qi[:, 0:640].rearrange("p (h f x j) -> p h f x j", h=10, f=2, x=2, j=16)
                    ov = qkr[:].rearrange("p (h f x j) -> p h f x j", h=10, f=2, x=2, j=16)
                    for f in range(2):
                        ti = i if f == 0 else NL
                        cs = tab[:, 1, ti, :].unsqueeze(1).to_broadcast([128, 10, 16])
                        sn = tab[:, 0, ti, :].unsqueeze(1).to_broadcast([128, 10, 16])
                        x1 = qv[:, :, f, 0, :]
                        x2 = qv[:, :, f, 1, :]
                        P.op("dve", lambda e, x1=x1, cs=cs: e.tensor_tensor(out=tmp[0][:], in0=x1, in1=cs, op=ALU.mult), reads=[qi, tab], writes=[tmp[0]])
                        P.op("dve", lambda e, x2=x2, sn=sn: e.tensor_tensor(out=tmp[1][:], in0=x2, in1=sn, op=ALU.mult), reads=[qi, tab], writes=[tmp[1]])
                        P.op("dve", lambda e, x2=x2, cs=cs: e.tensor_tensor(out=tmp[2][:], in0=x2, in1=cs, op=ALU.mult), reads=[qi, tab], writes=[tmp[2]])
                        P.op("dve", lambda e, x1=x1, sn=sn: e.tensor_tensor(out=tmp[3][:], in0=x1, in1=sn, op=ALU.mult), reads=[qi, tab], writes=[tmp[3]])
                        P.op("dve", lambda e, f=f, ov=ov: e.tensor_sub(out=ov[:, :, f, 0, :], in0=tmp[0][:], in1=tmp[1][:]), reads=[tmp[0], tmp[1]], writes=[qkr])
                        P.op("dve", lambda e, f=f, ov=ov: e.tensor_add(out=ov[:, :, f, 1, :], in0=tmp[2][:], in1=tmp[3][:]), reads=[tmp[2], tmp[3]], writes=[qkr])
                    for kq in range(2):
                        P.op("dve", lambda e, kq=kq: e.tensor_copy(out=qk[:, 0:512].rearrange("p (j k d) -> p j k d", j=4, k=2)[:, :, kq, :],
                                                                  in_=qkr[:, kq * 256:(kq + 1) * 256].rearrange("p (j d) -> p j d", j=4)), reads=[qkr], writes=[qk])
                    P.op("dve", lambda e: e.tensor_copy(out=qk[:, 512:640], in_=qkr[:, 512:640]), reads=[qkr], writes=[qk])
                pT = PS[7]
                if CUT < 3:
                    continue
                for j in range(4):
                    P.op("pe", lambda e, j=j: e.transpose(out=psbf(7)[:, j * 128:(j + 1) * 128], in_=qk[:, j * 128:(j + 1) * 128],
                                                          identity=identb[:]), reads=[qk, identb], writes=[pT])
                P.op("pe", lambda e: e.transpose(out=psbf(7)[:, 512:640], in_=qk[:, 512:640], identity=identb[:]),
                     reads=[qk, identb], writes=[pT])
                if CUT < 4:
                    continue
                P.op("dve", lambda e, tt=tt: e.tensor_copy(out=QT[:, :, tt * 128:(tt + 1) * 128],
                                                            in_=psbf(7)[:, 0:512].rearrange("p (a b) -> p a b", a=4)),
                     reads=[pT], writes=[QT])
                if CUT < 5:
                    continue
                P.op("dve", lambda e, tt=tt: e.tensor_copy(out=KT[:, tt * 128:(tt + 1) * 128], in_=psbf(7)[:, 512:640]),
                     reads=[pT], writes=[KT])
            if stop in tuple(pre + q for q in ('C1', 'C1a', 'C1b')):
                break
            Pexp = [A.alloc([640], BF16) for _ in range(2)]
            PTs = [A.alloc([5, 128], BF16) for _ in range(2)]
            st8 = A.alloc([8, 8])
            yct = A.alloc([512], BF16)
            ycT = [A.alloc([4, 128], BF16) for _ in range(2)]
            hc = 0
            for qt in range(NT):
                if qt < 2 and not with_ctx:
                    continue
                if stop == 'C2' and qt == 1:
                    break
                if stop == 'C3' and qt == 3:
                    break
                lat = qt >= 2
                i = qt - 2
                for h in range(8):
                    j, kvh = h % 4, h // 4
                    pb = 64 * kvh
                    psA, psB = PS[(hc % 2) * 2], PS[(hc % 2) * 2 + 1]
                    pe_ = Pexp[hc % 2]
                    pts = PTs[hc % 2]
                    lhs = QT[pb:pb + 64, j, qt * 128:(qt + 1) * 128]
                    P.op("pe", lambda e, lhs=lhs, psA=psA, pb=pb: e.matmul(psA[:, 0:256], lhsT=lhs, rhs=KT[pb:pb + 64, 0:256], start=True, stop=True),
                         reads=[QT, KT], writes=[psA])
                    blocks = [(0, 0), (1, 128)]
                    if lat:
                        for bi, blk in enumerate((i - 1, i, i + 1)):
                            src = min(max(blk, 0), NL - 1)
                            c0 = 256 + 128 * src
                            mt = None
                            if bi == 0:
                                mt = 0 if i > 0 else 2
                            if bi == 2:
                                mt = 1 if i < NL - 1 else 2
                            P.op("pe", lambda e, lhs=lhs, psB=psB, pb=pb, c0=c0, bi=bi, mt=mt: e.matmul(
                                psB[:, bi * 128:(bi + 1) * 128], lhsT=lhs, rhs=KT[pb:pb + 64, c0:c0 + 128], start=True, stop=(mt is None)),
                                reads=[QT, KT], writes=[psB])
                            if mt is not None:
                                P.op("pe", lambda e, psB=psB, bi=bi, mt=mt: e.matmul(
                                    psB[:, bi * 128:(bi + 1) * 128], lhsT=identb[:], rhs=maskb[:, mt, :], start=False, stop=True),
                                    reads=[identb, maskb], writes=[psB])
                            blocks.append((2 + src, 256 + bi * 128))
                    sv = st8[:, h, :]
                    P.op("dve", lambda e, sv=sv, psA=psA: e.reduce_max(out=sv[:, 0:1], in_=psA[:, 0:256], axis=AX.X), reads=[psA], writes=[st8])
                    if lat:
                        P.op("dve", lambda e, sv=sv, psB=psB: e.reduce_max(out=sv[:, 1:2], in_=psB[:, 0:384], axis=AX.X), reads=[psB], writes=[st8])
                        P.op("dve", lambda e, sv=sv: e.tensor_tensor(out=sv[:, 0:1], in0=sv[:, 0:1], in1=sv[:, 1:2], op=ALU.max), reads=[st8], writes=[st8])
                    P.op("dve", lambda e, sv=sv, h=h: e.tensor_scalar(out=sv[:, 2:3], in0=sv[:, 0:1], scalar1=-0.125, scalar2=nsinkB[:, h:h + 1],
                                                                     op0=ALU.mult, op1=ALU.min), reads=[st8, nsinkB], writes=[st8])
                    P.op("act", lambda e, sv=sv, psA=psA, pe_=pe_: e.activation(out=pe_[:, 0:256], in_=psA[:, 0:256], func=AF.Exp, scale=0.125,
                                                                               bias=sv[:, 2:3], accum_out=sv[:, 3:4]), reads=[psA, st8], writes=[pe_, st8])
                    if lat:
                        P.op("act", lambda e, sv=sv, psB=psB, pe_=pe_: e.activation(out=pe_[:, 256:640], in_=psB[:, 0:384], func=AF.Exp, scale=0.125,
                                                                                   bias=sv[:, 2:3], accum_out=sv[:, 4:5]), reads=[psB, st8], writes=[pe_, st8])
                    else:
                        P.op("dve", lambda e, sv=sv: e.memset(sv[:, 4:5], 0.0), writes=[st8])
                    P.op("act", lambda e, sv=sv, h=h: e.activation(out=sv[:, 5:6], in_=sinkB[:, h:h + 1], func=AF.Exp, bias=sv[:, 2:3]),
                         reads=[sinkB, st8], writes=[st8])
                    P.op("dve", lambda e, sv=sv: e.reduce_sum(out=sv[:, 6:7], in_=sv[:, 3:6], axis=AX.X), reads=[st8], writes=[st8])
                    P.op("dve", lambda e, sv=sv: e.reciprocal(out=sv[:, 7:8], in_=sv[:, 6:7]), reads=[st8], writes=[st8])
                    pTT = PS[4 + hc % 2]
                    nb_ = len(blocks)
                    for bi, (kt_, co) in enumerate(blocks):
                        P.op("pe", lambda e, bi=bi, co=co, pe_=pe_, hc=hc: e.transpose(
                            out=psbf(4 + hc % 2)[:, bi * 128:(bi + 1) * 128], in_=pe_[:, co:co + 128], identity=identb[:]),
                            reads=[pe_, identb], writes=[pTT])
                    P.op("dve", lambda e, pts=pts, hc=hc, nb_=nb_: e.tensor_copy(
                        out=pts[:, 0:nb_, :], in_=psbf(4 + hc % 2)[:, 0:nb_ * 128].rearrange("p (a b) -> p a b", b=128)),
                        reads=[pTT], writes=[pts])
                    psO = PS[6]
                    for bi, (kt_, co) in enumerate(blocks):
                        P.op("pe", lambda e, bi=bi, kt_=kt_, pts=pts, pb=pb, nb_=nb_: e.matmul(
                            psO[:, 0:64], lhsT=pts[:, bi, :], rhs=Vr[:, kt_, pb:pb + 64], start=(bi == 0), stop=(bi == nb_ - 1)),
                            reads=[pts, Vr], writes=[psO])
                    P.op("act", lambda e, sv=sv, h=h: e.activation(out=yct[:, h * 64:(h + 1) * 64], in_=psO[:, 0:64], func=AF.Copy, scale=sv[:, 7:8]),
                         reads=[psO, st8], writes=[yct])
                    hc += 1
                pT = PS[7]
                yo = ycT[qt % 2]
                for k in range(4):
                    P.op("pe", lambda e, k=k: e.transpose(out=psbf(7)[:, k * 128:(k + 1) * 128], in_=yct[:, k * 128:(k + 1) * 128], identity=identb[:]),
                         reads=[yct, identb], writes=[pT])
                P.op("dve", lambda e, yo=yo: e.tensor_copy(out=yo[:], in_=psbf(7)[:, 0:512].rearrange("p (a b) -> p a b", a=4)), reads=[pT], writes=[yo])
                P.dma("act", YT[2, :, :, qt * 128:(qt + 1) * 128], yo[:], reads=[yo], writes=[dYT])
            P.barrier()
            A.reset(l_mark)
            if stop in tuple(pre + q for q in ('C', 'C2', 'C3')):
                break
            YS = dscr("YS%d" % l, [512, T])
            ZS = dscr("ZS%d" % l, [512, T], BF16)
            dYS, dZS = Buf(), Buf()
            sm = A.alloc([24, 32])
            LRE, LIM, LDT, DTT, ZZ, TH, MAG, CC, SS_, ARE, AIM, FRE, FIM, T0_, T1_, T2_, X2, CT_, ST_ = range(19)
            for d in range(2):
                P.dma("sp", sm[:, LRE, d * 16:(d + 1) * 16], W["ssm_lam_re"][l, d].rearrange("(j g) p -> g p j", g=2)[0], writes=[sm], allow_slow_non_contiguous=True) if False else None
                for g2 in range(2):
                    P.dma("sp", sm[g2 * 64:(g2 + 1) * 64, LRE, d * 16:(d + 1) * 16], W["ssm_lam_re"][l, d].rearrange("(j g) p -> g p j", g=2)[g2], writes=[sm], allow_slow_non_contiguous=True)
                    P.dma("sp", sm[g2 * 64:(g2 + 1) * 64, LIM, d * 16:(d + 1) * 16], W["ssm_lam_im"][l, d].rearrange("(j g) p -> g p j", g=2)[g2], writes=[sm], allow_slow_non_contiguous=True)
                    P.dma("sp", sm[g2 * 64:(g2 + 1) * 64, LDT, d * 16:(d + 1) * 16], W["ssm_log_dt"][l, d].rearrange("(j g) -> g j", g=2)[g2].partition_broadcast(64), writes=[sm], allow_slow_non_contiguous=True)

            def sv(i):
                return sm[:, i, :]

            def tt_(eng, o, a_, b_, op):
                P.op(eng, lambda e: e.tensor_tensor(out=sv(o), in0=sv(a_), in1=sv(b_), op=op), reads=[sm], writes=[sm])

            def ts_(o, a_, s1, s2=None, op0=ALU.mult, op1=ALU.add):
                if s2 is None:
                    P.op("dve", lambda e: e.tensor_scalar(out=sv(o), in0=sv(a_), scalar1=s1, scalar2=None, op0=op0), reads=[sm], writes=[sm])
                else:
                    P.op("dve", lambda e: e.tensor_scalar(out=sv(o), in0=sv(a_), scalar1=s1, scalar2=s2, op0=op0, op1=op1), reads=[sm], writes=[sm])

            P.op("act", lambda e: e.activation(out=sv(DTT), in_=sv(LDT), func=AF.Exp), reads=[sm], writes=[sm])
            tt_("dve", ZZ, LRE, DTT, ALU.mult)
            tt_("dve", TH, LIM, DTT, ALU.mult)
            P.op("act", lambda e: e.activation(out=sv(MAG), in_=sv(ZZ), func=AF.Exp), reads=[sm], writes=[sm])
            ts_(T0_, TH, 1.0 / 256)
            tt_("dve", X2, T0_, T0_, ALU.mult)
            ts_(T1_, X2, -1.0 / 20, 1.0)
            tt_("dve", T1_, T1_, X2, ALU.mult)
            ts_(T1_, T1_, -1.0 / 6, 1.0)
            tt_("dve", SS_, T1_, T0_, ALU.mult)
            ts_(T1_, X2, -1.0 / 30, 1.0)
            tt_("dve", T1_, T1_, X2, ALU.mult)
            ts_(T1_, T1_, -1.0 / 12, 1.0)
            tt_("dve", T1_, T1_, X2, ALU.mult)
            ts_(CC, T1_, -0.5, 1.0)
            for _ in range(8):
                tt_("dve", T0_, CC, CC, ALU.mult)
                tt_("dve", T1_, SS_, SS_, ALU.mult)
                tt_("dve", T2_, CC, SS_, ALU.mult)
                tt_("dve", CC, T0_, T1_, ALU.subtract)
                ts_(SS_, T2_, 2.0)
            tt_("dve", ARE, MAG, CC, ALU.mult)
            tt_("dve", AIM, MAG, SS_, ALU.mult)
            tt_("dve", T0_, LRE, LRE, ALU.mult)
            tt_("dve", T1_, LIM, LIM, ALU.mult)
            tt_("dve", T0_, T0_, T1_, ALU.add)
            P.op("dve", lambda e: e.reciprocal(out=sv(T0_), in_=sv(T0_)), reads=[sm], writes=[sm])
            ts_(T1_, ARE, -1.0, None, op0=ALU.add)
            tt_("dve", T2_, T1_, LRE, ALU.mult)
            tt_("dve", FRE, AIM, LIM, ALU.mult)
            tt_("dve", FRE, FRE, T2_, ALU.add)
            tt_("dve", FRE, FRE, T0_, ALU.mult)
            tt_("dve", T2_, AIM, LRE, ALU.mult)
            tt_("dve", FIM, T1_, LIM, ALU.mult)
            tt_("dve", FIM, T2_, FIM, ALU.subtract)
            tt_("dve", FIM, FIM, T0_, ALU.mult)
            ct = A.alloc([32, 128])
            stb = A.alloc([32, 128])
            Rf = A.alloc([32, 128])
            wk = A.alloc([4, 32])
            tq = [A.alloc([32, 64]) for _ in range(2)]
            P.op("pool", lambda e: e.memset(ct[:, :, 0:1], 1.0), writes=[ct])
            P.op("pool", lambda e: e.memset(stb[:, :, 0:1], 0.0), writes=[stb])
            P.op("dve", lambda e: e.tensor_copy(out=wk[:, 0, :], in_=sv(CC)), reads=[sm], writes=[wk])
            P.op("dve", lambda e: e.tensor_scalar(out=wk[:, 1, :], in0=sv(SS_), scalar1=-1.0, scalar2=None, op0=ALU.mult), reads=[sm], writes=[wk])
            kk_ = 1
            while kk_ <= 128:
                if kk_ < 128:
                    wc = wk[:, 0, :].unsqueeze(2).to_broadcast([128, 32, kk_])
                    ws_ = wk[:, 1, :].unsqueeze(2).to_broadcast([128, 32, kk_])
                    P.op("dve", lambda e, wc=wc, kk_=kk_: e.tensor_tensor(out=tq[0][:, :, 0:kk_], in0=ct[:, :, 0:kk_], in1=wc, op=ALU.mult), reads=[ct, wk], writes=[tq[0]])
                    P.op("dve", lambda e, ws_=ws_, kk_=kk_: e.tensor_tensor(out=tq[1][:, :, 0:kk_], in0=stb[:, :, 0:kk_], in1=ws_, op=ALU.mult), reads=[stb, wk], writes=[tq[1]])
                    P.op("dve", lambda e, kk_=kk_: e.tensor_sub(out=ct[:, :, kk_:2 * kk_], in0=tq[0][:, :, 0:kk_], in1=tq[1][:, :, 0:kk_]), reads=[tq[0], tq[1]], writes=[ct])
                    P.op("dve", lambda e, ws_=ws_, kk_=kk_: e.tensor_tensor(out=tq[0][:, :, 0:kk_], in0=ct[:, :, 0:kk_], in1=ws_, op=ALU.mult), reads=[ct, wk], writes=[tq[0]])
                    P.op("dve", lambda e, wc=wc, kk_=kk_: e.tensor_tensor(out=tq[1][:, :, 0:kk_], in0=stb[:, :, 0:kk_], in1=wc, op=ALU.mult), reads=[stb, wk], writes=[tq[1]])
                    P.op("dve", lambda e, kk_=kk_: e.tensor_add(out=stb[:, :, kk_:2 * kk_], in0=tq[0][:, :, 0:kk_], in1=tq[1][:, :, 0:kk_]), reads=[tq[0], tq[1]], writes=[stb])
                    P.op("dve", lambda e: e.tensor_tensor(out=wk[:, 2, :], in0=wk[:, 0, :], in1=wk[:, 0, :], op=ALU.mult), reads=[wk], writes=[wk])
                    P.op("dve", lambda e: e.tensor_tensor(out=wk[:, 3, :], in0=wk[:, 1, :], in1=wk[:, 1, :], op=ALU.mult), reads=[wk], writes=[wk])
                    P.op("dve", lambda e: e.tensor_tensor(out=wk[:, 1, :], in0=wk[:, 0, :], in1=wk[:, 1, :], op=ALU.mult), reads=[wk], writes=[wk])
                    P.op("dve", lambda e: e.tensor_scalar(out=wk[:, 1, :], in0=wk[:, 1, :], scalar1=2.0, scalar2=None, op0=ALU.mult), reads=[wk], writes=[wk])
                    P.op("dve", lambda e: e.tensor_sub(out=wk[:, 0, :], in0=wk[:, 2, :], in1=wk[:, 3, :]), reads=[wk], writes=[wk])
                kk_ *= 2
            for cI in range(32):
                P.op("pool", lambda e, cI=cI: e.tensor_scalar(out=Rf[:, cI, :], in0=ones[:], scalar1=sm[:, MAG, cI:cI + 1], scalar2=None, op0=ALU.mult), reads=[ones, sm], writes=[Rf])
            BW = A.alloc([32, 2, 128], BF16)
            CP = A.alloc([32, 2, 128], BF16)
            P.op("pool", lambda e: e.memset(CP[:], 0.0), writes=[CP])
            braw = A.alloc([2, 32, 16])
            for d in range(2):
                for g2 in range(2):
                    P.dma("sp", braw[g2 * 64:(g2 + 1) * 64, 0, d * 16:(d + 1) * 16, :], W["ssm_b_re"][l, d].rearrange("(j g) p h -> g p j h", g=2)[g2], writes=[braw])
                    P.dma("sp", braw[g2 * 64:(g2 + 1) * 64, 1, d * 16:(d + 1) * 16, :], W["ssm_b_im"][l, d].rearrange("(j g) p h -> g p j h", g=2)[g2], writes=[braw])
            bbar = A.alloc([2, 32, 16])
            tb = [A.alloc([32, 16]) for _ in range(2)]
            fr = sm[:, FRE, :].unsqueeze(2).to_broadcast([128, 32, 16])
            fi = sm[:, FIM, :].unsqueeze(2).to_broadcast([128, 32, 16])
            P.op("dve", lambda e: e.tensor_tensor(out=tb[0][:], in0=braw[:, 0, :, :], in1=fr, op=ALU.mult), reads=[braw, sm], writes=[tb[0]])
            P.op("dve", lambda e: e.tensor_tensor(out=tb[1][:], in0=braw[:, 1, :, :], in1=fi, op=ALU.mult), reads=[braw, sm], writes=[tb[1]])
            P.op("dve", lambda e: e.tensor_sub(out=bbar[:, 0, :, :], in0=tb[0][:], in1=tb[1][:]), reads=[tb[0], tb[1]], writes=[bbar])
            P.op("dve", lambda e: e.tensor_tensor(out=tb[0][:], in0=braw[:, 1, :, :], in1=fr, op=ALU.mult), reads=[braw, sm], writes=[tb[0]])
            P.op("dve", lambda e: e.tensor_tensor(out=tb[1][:], in0=braw[:, 0, :, :], in1=fi, op=ALU.mult), reads=[braw, sm], writes=[tb[1]])
            P.op("dve", lambda e: e.tensor_add(out=bbar[:, 1, :, :], in0=tb[0][:], in1=tb[1][:]), reads=[tb[0], tb[1]], writes=[bbar])
            xp = [A.alloc([128]) for _ in range(2)]
            cl = A.alloc([2, 32, 64])
            for cI in range(32):
                d, j = cI // 16, cI % 16
                jl = j % 4
                for ri in range(2):
                    x_ = xp[(cI * 2 + ri) % 2]
                    P.op("pool", lambda e, x_=x_: e.memset(x_[:], 0.0), writes=[x_])
                    for g2 in range(2):
                        c0 = 32 * jl + 16 * g2
                        P.op("dve", lambda e, x_=x_, g2=g2, c0=c0, ri=ri, cI=cI: e.tensor_copy(out=x_[g2 * 64:(g2 + 1) * 64, c0:c0 + 16], in_=bbar[g2 * 64:(g2 + 1) * 64, ri, cI, :]),
                             reads=[bbar], writes=[x_])
                    P.op("pe", lambda e, x_=x_: e.transpose(out=PS[0][:, 0:128], in_=x_[:], identity=ident[:]), reads=[x_, ident], writes=[PS[0]])
                    P.op("act", lambda e, cI=cI, ri=ri: e.activation(out=BW[:, cI, ri, :], in_=PS[0][:, 0:128], func=AF.Copy), reads=[PS[0]], writes=[BW])
            for d in range(2):
                P.dma("sp", cl[0:16, 0, :, :], W["ssm_c_re"][l, d].rearrange("g h p -> h g p"), writes=[cl])
                P.dma("sp", cl[0:16, 1, :, :], W["ssm_c_im"][l, d].rearrange("g h p -> h g p"), writes=[cl])
                for j in range(16):
                    cI = d * 16 + j
                    for ri in range(2):
                        P.op("pe", lambda e, ri=ri, j=j: e.transpose(out=PS[1][:, 0:16], in_=cl[0:16, ri, 2 * j:2 * j + 2, :].rearrange("p a b -> p (a b)"),
                                                                     identity=ident[0:16, 0:16]), reads=[cl, ident], writes=[PS[1]])
                        for g2 in range(2):
                            c0 = 32 * (j % 4) + 16 * g2
                            P.op("act", lambda e, ri=ri, g2=g2, cI=cI, c0=c0: e.activation(out=CP[g2 * 64:(g2 + 1) * 64, cI, ri, c0:c0 + 16], in_=PS[1][g2 * 64:(g2 + 1) * 64, 0:16],
                                                                                      func=AF.Copy, scale=(1.0 if ri == 0 else -1.0)), reads=[PS[1]], writes=[CP])
            dsk = A.alloc([4])
            P.dma("sp", dsk[:], W["ssm_d"][l].rearrange("(k p) -> p k", p=128), writes=[dsk], allow_slow_non_contiguous=True)
            uts = [A.alloc([128], BF16) for _ in range(2)]
            wre = A.alloc([4, 128])
            wim = A.alloc([4, 128])
            tt4 = [A.alloc([4, 128]) for _ in range(4)]
            qre = A.alloc([4, 128])
            qim = A.alloc([4, 128])
            sre = A.alloc([4, 128], BF16)
            sim_ = A.alloc([4, 128], BF16)
            qin_ = A.alloc([2, 4])
            qtm = A.alloc([4, 4])
            yst = [A.alloc([128]) for _ in range(2)]
            zst = [A.alloc([128], BF16) for _ in range(2)]
            for d in range(2):
                order = list(range(NT)) if d == 0 else [1, 0] + list(range(NT - 1, 1, -1))
                rv = (lambda ap: ap) if d == 0 else (lambda ap: ap[:, ::-1])
                for ft in range(4):
                    c4 = d * 16 + 4 * ft
                    P.op("pool", lambda e: e.memset(qin_[:], 0.0), writes=[qin_])
                    for it, tt in enumerate(order):
                        tok0 = tt * 128
                        u_ = uts[it % 2]
                        P.dma("sp", u_[:], UT[ft * 128:(ft + 1) * 128, tok0:tok0 + 128], reads=[dUT], writes=[u_])
                        for jl in range(4):
                            for ri in range(2):
                                P.op("pe", lambda e, jl=jl, ri=ri, u_=u_, c4=c4: e.matmul(PS[ri][:, jl * 128:(jl + 1) * 128], lhsT=BW[:, c4 + jl, ri, :], rhs=rv(u_[:]), start=True, stop=True),
                                     reads=[BW, u_], writes=[PS[ri]])
                        bre = PS[0][:].rearrange("p (a b) -> p a b", a=4)
                        bim = PS[1][:].rearrange("p (a b) -> p a b", a=4)
                        cT = ct[:, c4:c4 + 4, :]
                        sT = stb[:, c4:c4 + 4, :]
                        P.op("dve", lambda e, bre=bre, cT=cT: e.tensor_tensor(out=tt4[0][:], in0=bre, in1=cT, op=ALU.mult), reads=[PS[0], ct], writes=[tt4[0]])
                        P.op("dve", lambda e, bim=bim, sT=sT: e.tensor_tensor(out=tt4[1][:], in0=bim, in1=sT, op=ALU.mult), reads=[PS[1], stb], writes=[tt4[1]])
                        P.op("pool", lambda e: e.tensor_sub(out=wre[:], in0=tt4[0][:], in1=tt4[1][:]), reads=[tt4[0], tt4[1]], writes=[wre])
                        P.op("dve", lambda e, bim=bim, cT=cT: e.tensor_tensor(out=tt4[2][:], in0=bim, in1=cT, op=ALU.mult), reads=[PS[1], ct], writes=[tt4[2]])
                        P.op("dve", lambda e, bre=bre, sT=sT: e.tensor_tensor(out=tt4[3][:], in0=bre, in1=sT, op=ALU.mult), reads=[PS[0], stb], writes=[tt4[3]])
                        P.op("pool", lambda e: e.tensor_add(out=wim[:], in0=tt4[2][:], in1=tt4[3][:]), reads=[tt4[2], tt4[3]], writes=[wim])
                        for jl in range(4):
                            P.op("dve", lambda e, jl=jl, c4=c4: e.tensor_tensor_scan(out=qre[:, jl, :], data0=Rf[:, c4 + jl, :], data1=wre[:, jl, :], initial=qin_[:, 0, jl:jl + 1],
                                                                                    op0=ALU.mult, op1=ALU.add), reads=[Rf, wre, qin_], writes=[qre])
                            P.op("dve", lambda e, jl=jl, c4=c4: e.tensor_tensor_scan(out=qim[:, jl, :], data0=Rf[:, c4 + jl, :], data1=wim[:, jl, :], initial=qin_[:, 1, jl:jl + 1],
                                                                                    op0=ALU.mult, op1=ALU.add), reads=[Rf, wim, qin_], writes=[qim])
                        wc = wk[:, 0, c4:c4 + 4]
                        ws_ = wk[:, 1, c4:c4 + 4]
                        P.op("pool", lambda e, wc=wc: e.tensor_tensor(out=qtm[:, 0, :], in0=qre[:, :, 127], in1=wc, op=ALU.mult), reads=[qre, wk], writes=[qtm])
                        P.op("pool", lambda e, ws_=ws_: e.tensor_tensor(out=qtm[:, 1, :], in0=qim[:, :, 127], in1=ws_, op=ALU.mult), reads=[qim, wk], writes=[qtm])
                        P.op("pool", lambda e, wc=wc: e.tensor_tensor(out=qtm[:, 2, :], in0=qim[:, :, 127], in1=wc, op=ALU.mult), reads=[qim, wk], writes=[qtm])
                        P.op("pool", lambda e, ws_=ws_: e.tensor_tensor(out=qtm[:, 3, :], in0=qre[:, :, 127], in1=ws_, op=ALU.mult), reads=[qre, wk], writes=[qtm])
                        P.op("pool", lambda e: e.tensor_add(out=qin_[:, 0, :], in0=qtm[:, 0, :], in1=qtm[:, 1, :]), reads=[qtm], writes=[qin_])
                        P.op("pool", lambda e: e.tensor_sub(out=qin_[:, 1, :], in0=qtm[:, 2, :], in1=qtm[:, 3, :]), reads=[qtm], writes=[qin_])
                        P.op("dve", lambda e, cT=cT: e.tensor_tensor(out=tt4[0][:], in0=qre[:], in1=cT, op=ALU.mult), reads=[qre, ct], writes=[tt4[0]])
                        P.op("pool", lambda e, sT=sT: e.tensor_tensor(out=tt4[1][:], in0=qim[:], in1=sT, op=ALU.mult), reads=[qim, stb], writes=[tt4[1]])
                        P.op("dve", lambda e: e.tensor_add(out=sre[:], in0=tt4[0][:], in1=tt4[1][:]), reads=[tt4[0], tt4[1]], writes=[sre])
                        P.op("pool", lambda e, cT=cT: e.tensor_tensor(out=tt4[2][:], in0=qim[:], in1=cT, op=ALU.mult), reads=[qim, ct], writes=[tt4[2]])
                        P.op("dve", lambda e, sT=sT: e.tensor_tensor(out=tt4[3][:], in0=qre[:], in1=sT, op=ALU.mult), reads=[qre, stb], writes=[tt4[3]])
                        P.op("pool", lambda e: e.tensor_sub(out=sim_[:], in0=tt4[2][:], in1=tt4[3][:]), reads=[tt4[2], tt4[3]], writes=[sim_])
                        for jl in range(4):
                            P.op("pe", lambda e, jl=jl, c4=c4: e.matmul(PS[2][:, 0:128], lhsT=CP[:, c4 + jl, 0, :], rhs=rv(sre[:, jl, :]), start=(jl == 0), stop=False), reads=[CP, sre], writes=[PS[2]])
                            P.op("pe", lambda e, jl=jl, c4=c4: e.matmul(PS[2][:, 0:128], lhsT=CP[:, c4 + jl, 1, :], rhs=rv(sim_[:, jl, :]), start=False, stop=(jl == 3)), reads=[CP, sim_], writes=[PS[2]])
                        y_ = yst[it % 2]
                        if d == 0:
                            P.op("act", lambda e, y_=y_: e.activation(out=y_[:], in_=PS[2][:, 0:128], func=AF.Copy), reads=[PS[2]], writes=[y_])
                            P.dma("act", YS[ft * 128:(ft + 1) * 128, tok0:tok0 + 128], y_[:], reads=[y_], writes=[dYS])
                        elif tt >= 2 or with_ctx:
                            z_ = zst[it % 2]
                            P.dma("sp", y_[:], YS[ft * 128:(ft + 1) * 128, tok0:tok0 + 128], reads=[dYS], writes=[y_])
                            P.op("dve", lambda e, y_=y_: e.tensor_add(out=y_[:], in0=y_[:], in1=PS[2][:, 0:128]), reads=[y_, PS[2]], writes=[y_])
                            P.op("dve", lambda e, y_=y_, u_=u_, ft=ft: e.scalar_tensor_tensor(out=y_[:], in0=u_[:], scalar=dsk[:, ft:ft + 1], in1=y_[:], op0=ALU.mult, op1=ALU.add),
                                 reads=[y_, u_, dsk], writes=[y_])
                            P.op("act", lambda e, y_=y_, z_=z_: e.activation(out=z_[:], in_=y_[:], func=AF.Gelu), reads=[y_], writes=[z_])
                            P.dma("act", ZS[ft * 128:(ft + 1) * 128, tok0:tok0 + 128], z_[:], reads=[z_], writes=[dZS])
            P.barrier()
            stg = [A.alloc([4, 256]) for _ in range(2)]
            wg = A.alloc([4, 512], BF16)
            wgv = W["ssm_w_glu"][l].rearrange("(k p) n -> p k n", p=128)
            for ci in range(2):
                ws = stg[ci]
                P.dma("sp", ws[:], wgv[:, :, ci * 256:(ci + 1) * 256], writes=[ws])
                P.op("dve", lambda e, ws=ws, ci=ci: e.tensor_copy(out=wg[:, :, ci * 256:(ci + 1) * 256], in_=ws[:]), reads=[ws], writes=[wg])
            zin = [A.alloc([4, 512], BF16) for _ in range(2)]
            sg_ = A.alloc([512])
            yo_ = [A.alloc([4, 512], BF16) for _ in range(2)]
            ZSv = ZS.rearrange("(k p) t -> p k t", p=128)
            t0 = 0 if with_ctx else LC
            bi_ = 0
            while t0 < T:
                ST = min(512, T - t0)
                if t0 < LC:
                    ST = LC - t0
                zi = zin[bi_ % 2]
                yo = yo_[bi_ % 2]
                P.dma("sp", zi[:, :, 0:ST], ZSv[:, :, t0:t0 + ST], reads=[dZS], writes=[zi])
                for oc in range(4):
                    ps = PS[3 + oc % 2]
                    for k in range(4):
                        P.op("pe", lambda e, k=k, oc=oc, ps=ps, zi=zi, ST=ST: e.matmul(ps[:, 0:ST], lhsT=wg[:, k, oc * 128:(oc + 1) * 128], rhs=zi[:, k, 0:ST], start=(k == 0), stop=(k == 3)),
                             reads=[wg, zi], writes=[ps])
                    P.op("act", lambda e, ps=ps, ST=ST: e.activation(out=sg_[:, 0:ST], in_=ps[:, 0:ST], func=AF.Sigmoid), reads=[ps], writes=[sg_])
                    P.op("dve", lambda e, oc=oc, zi=zi, yo=yo, ST=ST: e.tensor_tensor(out=yo[:, oc, 0:ST], in0=zi[:, oc, 0:ST], in1=sg_[:, 0:ST], op=ALU.mult), reads=[zi, sg_], writes=[yo])
                P.dma("act", YT[0, :, :, t0:t0 + ST], yo[:, :, 0:ST], reads=[yo], writes=[dYT])
                t0 += ST
                bi_ += 1
            P.barrier()
            A.reset(l_mark)

            if stop == pre + 'S':
                break
            QN = dscr("QN%d" % l, [T, 512], BF16)
            KN = dscr("KN%d" % l, [T, 512], BF16)
            VS = dscr("VS%d" % l, [T, 512], BF16)
            BG = dscr("BG%d" % l, [T, 16])
            OD = dscr("OD%d" % l, [2, T, 512])
            dQN, dKN, dVS, dBG, dOD = Buf(), Buf(), Buf(), Buf(), Buf()
            cw = A.alloc([12, 5])
            for j in range(5):
                P.dma("sp", cw[:, :, j], W["dn_conv_w"][l][j].rearrange("(c p) -> p c", p=128), writes=[cw], allow_slow_non_contiguous=True)
            alB = A.alloc([8])
            dtB = A.alloc([8])
            P.dma("sp", alB[:], W["dn_a_log"][l].rearrange("d h -> (d h)").partition_broadcast(128), writes=[alB])
            P.dma("sp", dtB[:], W["dn_dt_bias"][l].rearrange("d h -> (d h)").partition_broadcast(128), writes=[dtB])
            P.op("act", lambda e: e.activation(out=alB[:], in_=alB[:], func=AF.Exp), reads=[alB], writes=[alB])
            dcon = A.alloc([7, 128])
            P.dma("sp", dcon[:], dncon_d, writes=[dcon])
            sel2 = A.alloc([2, 128])
            P.dma("sp", sel2[:], dnsel_d, writes=[sel2])
            ngB = A.alloc([128])
            P.dma("sp", ngB[:], W["dn_norm_g"][l].partition_broadcast(128), writes=[ngB])
            b_mark = A.mark()
            sq = A.alloc([12, 512])
            cin = [A.alloc([516]) for _ in range(2)]
            acc = [A.alloc([512]) for _ in range(2)]
            tmi = [A.alloc([16]) for _ in range(2)]
            sqr = A.alloc([512])
            rn = A.alloc([2, 4, 2])
            qo = [A.alloc([3, 512], BF16) for _ in range(2)]
            bg = [A.alloc([16]) for _ in range(2)]
            sp_t = A.alloc([4, 8])
            spans = [(0, LC)] + [(LC + 512 * i, 512) for i in range(L // 512)]
            for (s0, sl) in spans:
                for c in range(12):
                    ci_ = cin[c % 2]
                    P.op("pool", lambda e, ci_=ci_: e.memset(ci_[:], 0.0), writes=[ci_])
                    seg_lo, seg_hi = (0, LC) if s0 < LC else (LC, T)
                    lo_, hi_ = max(s0 - 2, seg_lo), min(s0 + sl + 2, seg_hi)
                    o0 = 2 - (s0 - lo_)
                    P.dma("sp", ci_[:, o0:o0 + (hi_ - lo_)], QKVT[c * 128:(c + 1) * 128, lo_:hi_], reads=[dQKVT], writes=[ci_])
                    ac = acc[c % 2]
                    P.op("dve", lambda e, ci_=ci_, ac=ac, c=c, sl=sl: e.tensor_scalar(out=ac[:, 0:sl], in0=ci_[:, 0:sl], scalar1=cw[:, c, 0:1], scalar2=None, op0=ALU.mult),
                         reads=[ci_, cw], writes=[ac])
                    for j in range(1, 5):
                        P.op("dve", lambda e, ci_=ci_, ac=ac, c=c, sl=sl, j=j: e.scalar_tensor_tensor(
                            out=ac[:, 0:sl], in0=ci_[:, j:j + sl], scalar=cw[:, c, j:j + 1], in1=ac[:, 0:sl], op0=ALU.mult, op1=ALU.add),
                            reads=[ci_, cw, ac], writes=[ac])
                    P.op("act", lambda e, ac=ac, c=c, sl=sl: e.activation(out=sq[:, c, 0:sl], in_=ac[:, 0:sl], func=AF.Silu), reads=[ac], writes=[sq])
                for sub in range(sl // 128):
                    tok0 = s0 + sub * 128
                    o_ = qo[sub % 2]
                    for grp in range(3):
                        ps = PS[grp]
                        for c4 in range(4):
                            P.op("pe", lambda e, grp=grp, c4=c4, ps=ps, sub=sub: e.transpose(
                                out=ps[:, c4 * 128:(c4 + 1) * 128], in_=sq[:, grp * 4 + c4, sub * 128:(sub + 1) * 128], identity=ident[:]),
                                reads=[sq, ident], writes=[ps])
                        if grp < 2:
                            P.op("act", lambda e, ps=ps: e.activation(out=sqr[:], in_=ps[:], func=AF.Square), reads=[ps], writes=[sqr])
                            P.op("dve", lambda e, grp=grp: e.reduce_sum(out=rn[:, grp, :, 0], in_=sqr[:].rearrange("p (h d) -> p h d", h=4), axis=AX.X),
                                 reads=[sqr], writes=[rn])
                            P.op("act", lambda e, grp=grp: e.activation(out=rn[:, grp, :, 1], in_=rn[:, grp, :, 0], func=AF.Sqrt, bias=EPS,
                                                                        scale=(128.0 if grp == 0 else 1.0)), reads=[rn], writes=[rn])
                            P.op("dve", lambda e, grp=grp: e.reciprocal(out=rn[:, grp, :, 1], in_=rn[:, grp, :, 1]), reads=[rn], writes=[rn])
                            P.op("dve", lambda e, grp=grp, ps=ps, o_=o_: e.tensor_tensor(
                                out=o_[:, grp, :].rearrange("p (h d) -> p h d", h=4), in0=ps[:].rearrange("p (h d) -> p h d", h=4),
                                in1=rn[:, grp, :, 1].unsqueeze(2).to_broadcast([128, 4, 128]), op=ALU.mult), reads=[ps, rn], writes=[o_])
                        else:
                            P.op("act", lambda e, ps=ps, o_=o_: e.activation(out=o_[:, 2, :], in_=ps[:], func=AF.Copy), reads=[ps], writes=[o_])
                    P.dma("act", QN[tok0:tok0 + 128, :], o_[:, 0, :], reads=[o_], writes=[dQN])
                    P.dma("act", KN[tok0:tok0 + 128, :], o_[:, 1, :], reads=[o_], writes=[dKN])
                    P.dma("act", VS[tok0:tok0 + 128, :], o_[:, 2, :], reads=[o_], writes=[dVS])
                    ti_ = tmi[sub % 2]
                    bg_ = bg[sub % 2]
                    P.dma("sp", ti_[:], TM[tok0:tok0 + 128, 512:528], reads=[dTM], writes=[ti_])
                    P.op("act", lambda e, ti_=ti_, bg_=bg_: e.activation(out=bg_[:, 0:8], in_=ti_[:, 0:8], func=AF.Sigmoid), reads=[ti_], writes=[bg_])
                    P.op("dve", lambda e, ti_=ti_: e.tensor_add(out=sp_t[:, 0, :], in0=ti_[:, 8:16], in1=dtB[:]), reads=[ti_, dtB], writes=[sp_t])
                    P.op("act", lambda e: e.activation(out=sp_t[:, 1, :], in_=sp_t[:, 0, :], func=AF.Abs), reads=[sp_t], writes=[sp_t])
                    P.op("act", lambda e: e.activation(out=sp_t[:, 1, :], in_=sp_t[:, 1, :], func=AF.Exp, scale=-1.0), reads=[sp_t], writes=[sp_t])
                    P.op("act", lambda e: e.activation(out=sp_t[:, 1, :], in_=sp_t[:, 1, :], func=AF.Ln, bias=1.0), reads=[sp_t], writes=[sp_t])
                    P.op("dve", lambda e: e.scalar_tensor_tensor(out=sp_t[:, 2, :], in0=sp_t[:, 0, :], scalar=0.0, in1=sp_t[:, 1, :], op0=ALU.max, op1=ALU.add),
                         reads=[sp_t], writes=[sp_t])
                    P.op("dve", lambda e, bg_=bg_: e.scalar_tensor_tensor(out=bg_[:, 8:16], in0=sp_t[:, 2, :], scalar=-1.0, in1=alB[:], op0=ALU.mult, op1=ALU.mult),
                         reads=[sp_t, alB], writes=[bg_])
                    P.dma("act", BG[tok0:tok0 + 128, :], bg_[:], reads=[bg_], writes=[dBG])
            P.barrier()
            A.reset(b_mark)
            if stop == pre + 'B0':
                break
            import os
            if os.environ.get("MAXOPS"):
                P.limit = int(os.environ["MAXOPS"])
            qkv = [A.alloc([3, 512], BF16) for _ in range(2)]
            bgt = [A.alloc([16]) for _ in range(2)]
            gs = A.alloc([4, 8])
            egb = A.alloc([2, 8])
            Sf = A.alloc([4, 128])
            Sb = A.alloc([4, 128], BF16)
            H = []
            for hh in range(4):
                H.append(dict(
                    qT=A.alloc([128], BF16), kT=A.alloc([128], BF16), Dg=A.alloc([2, 128]), Dst=A.alloc([128]), Din=A.alloc([128]),
                    Mm=A.alloc([2, 2, 128]), Rr=A.alloc([2, 2, 128]), QKm=A.alloc([128]), QKmT=A.alloc([128], BF16),
                    vb=A.alloc([128]), xk=A.alloc([128]), ut=A.alloc([128]), wT=A.alloc([128], BF16), qgT=A.alloc([128], BF16),
                    dq=A.alloc([128], BF16), kdec=A.alloc([128], BF16), vnew=A.alloc([128], BF16),
                    Sf=Buf(Sf[:, hh, :]), Sb=Buf(Sb[:, hh, :]), pa=PS[hh], pb=PS[4 + hh], hh=hh))
            osb = [A.alloc([4, 128]) for _ in range(2)]
            for d in range(2):
                order = list(range(NT)) if d == 0 else [1, 0] + list(range(NT - 1, 1, -1))
                for hh in range(4):
                    P.op("pool", lambda e, hh=hh: e.memset(H[hh]["Sf"][:], 0.0), writes=[H[hh]["Sf"]])
                    P.op("pool", lambda e, hh=hh: e.memset(H[hh]["Sb"][:], 0.0), writes=[H[hh]["Sb"]])
                for it, tt in enumerate(order):
                    tok0 = tt * 128
                    q3 = qkv[it % 2]
                    b_ = bgt[it % 2]
                    os_ = osb[it % 2]
                    P.dma("sp", q3[:, 0, :], QN[tok0:tok0 + 128, :], reads=[dQN], writes=[q3])
                    P.dma("sp", q3[:, 1, :], KN[tok0:tok0 + 128, :], reads=[dKN], writes=[q3])
                    P.dma("sp", q3[:, 2, :], VS[tok0:tok0 + 128, :], reads=[dVS], writes=[q3])
                    P.dma("sp", b_[:], BG[tok0:tok0 + 128, :], reads=[dBG], writes=[b_])
                    gcol = b_[:, 8 + 4 * d:12 + 4 * d]
                    P.op("pe", lambda e, gcol=gcol, d=d: e.matmul(PS[0][:, 0:4], lhsT=dcon[:, d, :], rhs=gcol, start=True, stop=True), reads=[dcon, b_], writes=[PS[0]])
                    P.op("pe", lambda e, gcol=gcol: e.matmul(PS[0][:, 4:8], lhsT=dcon[:, 2, :], rhs=gcol, start=True, stop=True), reads=[dcon, b_], writes=[PS[0]])
                    P.op("dve", lambda e: e.tensor_copy(out=gs[:, 0, 0:8], in_=PS[0][:, 0:8]), reads=[PS[0]], writes=[gs])
                    P.op("act", lambda e: e.activation(out=gs[:, 1, 0:8], in_=gs[:, 0, 0:8], func=AF.Exp), reads=[gs], writes=[gs])
                    P.op("dve", lambda e: e.tensor_sub(out=gs[:, 2, 0:4], in0=gs[:, 0, 4:8], in1=gs[:, 0, 0:4]), reads=[gs], writes=[gs])
                    P.op("act", lambda e: e.activation(out=gs[:, 2, 4:8], in_=gs[:, 2, 0:4], func=AF.Exp), reads=[gs], writes=[gs])
                    for ch in range(2):
                        P.op("pe", lambda e, ch=ch: e.matmul(PS[0][:, 8 + 4 * ch:12 + 4 * ch], lhsT=sel2[:, ch, :], rhs=gs[:, 1, 4:8], start=True, stop=True),
                             reads=[sel2, gs], writes=[PS[0]])
                    P.op("dve", lambda e: e.tensor_copy(out=egb[:, 0, 0:8], in_=PS[0][:, 8:16]), reads=[PS[0]], writes=[egb])

                    def hv(h):
                        hh = h["hh"]
                        return dict(qh=q3[:, 0, hh * 128:(hh + 1) * 128], kh=q3[:, 1, hh * 128:(hh + 1) * 128], vh=q3[:, 2, hh * 128:(hh + 1) * 128],
                                    beta=b_[:, 4 * d + hh:4 * d + hh + 1], gc=gs[:, 0, hh:hh + 1], egc=gs[:, 1, hh:hh + 1], edec=gs[:, 2, 4 + hh:5 + hh])

                    def pabf(h):
                        return h["pa"].ap.bitcast(BF16)

                    def s_tr(h, v):
                        P.op("pe", lambda e: e.transpose(out=pabf(h)[:, 0:128], in_=v["qh"], identity=identb[:]), reads=[q3, identb], writes=[h["pa"]])
                        P.op("pe", lambda e: e.transpose(out=pabf(h)[:, 128:256], in_=v["kh"], identity=identb[:]), reads=[q3, identb], writes=[h["pa"]])
                        P.op("dve", lambda e: e.tensor_copy(out=h["qT"][:], in_=pabf(h)[:, 0:128]), reads=[h["pa"]], writes=[h["qT"]])
                        P.op("dve", lambda e: e.tensor_copy(out=h["kT"][:], in_=pabf(h)[:, 128:256]), reads=[h["pa"]], writes=[h["kT"]])

                    def s_dec(h, v):
                        Dg = h["Dg"]
                        P.op("dve", lambda e: e.tensor_scalar(out=Dg[:, 0, :], in0=ident[:], scalar1=v["gc"], scalar2=None, op0=ALU.mult), reads=[ident, gs], writes=[Dg])
                        P.op("pool", lambda e: e.tensor_scalar(out=Dg[:, 1, :], in0=Dg[:, 0, :], scalar1=-1.0, scalar2=None, op0=ALU.mult), reads=[Dg], writes=[Dg])
                        pb = h["pb"]
                        P.op("pe", lambda e: e.matmul(pb[:, 0:128], lhsT=Dg[:, 0, :], rhs=ones[:], start=True, stop=False), reads=[Dg, ones], writes=[pb])
                        P.op("pe", lambda e: e.matmul(pb[:, 0:128], lhsT=ones[:], rhs=Dg[:, 1, :], start=False, stop=False), reads=[Dg, ones], writes=[pb])
                        P.op("pe", lambda e: e.matmul(pb[:, 0:128], lhsT=ident[:], rhs=dcon[:, 4 + d, :], start=False, stop=True), reads=[ident, dcon], writes=[pb])
                        P.op("act", lambda e: e.activation(out=h["Dst"][:], in_=pb[:, 0:128], func=AF.Exp), reads=[pb], writes=[h["Dst"]])
                        P.op("pool", lambda e: e.tensor_add(out=h["Din"][:], in0=h["Dst"][:], in1=ident[:]), reads=[h["Dst"], ident], writes=[h["Din"]])

                    def s_gram(h, v):
                        pa = h["pa"]
                        P.op("pe", lambda e: e.matmul(pa[:, 128:256], lhsT=h["kT"][:], rhs=h["kT"][:], start=True, stop=True), reads=[h["kT"]], writes=[pa])
                        P.op("pe", lambda e: e.matmul(pa[:, 256:384], lhsT=h["qT"][:], rhs=h["kT"][:], start=True, stop=True), reads=[h["qT"], h["kT"]], writes=[pa])
                        Mm = h["Mm"]
                        P.op("dve", lambda e: e.scalar_tensor_tensor(out=Mm[:, 0, 0, :], in0=pa[:, 128:256], scalar=v["beta"], in1=h["Dst"][:], op0=ALU.mult, op1=ALU.mult),
                             reads=[pa, b_, h["Dst"]], writes=[Mm])
                        P.op("pool", lambda e: e.tensor_scalar(out=Mm[:, 0, 0, :], in0=Mm[:, 0, 0, :], scalar1=-1.0, scalar2=None, op0=ALU.mult), reads=[Mm], writes=[Mm])
                        P.op("dve", lambda e: e.tensor_tensor(out=h["QKm"][:], in0=pa[:, 256:384], in1=h["Din"][:], op=ALU.mult), reads=[pa, h["Din"]], writes=[h["QKm"]])

                    def s_tr2(h, v):
                        pb = h["pb"]
                        Mm = h["Mm"]
                        P.op("pe", lambda e: e.transpose(out=pb[:, 128:256], in_=Mm[:, 0, 0, :], identity=ident[:]), reads=[Mm, ident], writes=[pb])
                        P.op("pe", lambda e: e.transpose(out=pb[:, 256:384], in_=h["QKm"][:], identity=ident[:]), reads=[h["QKm"], ident], writes=[pb])
                        P.op("act", lambda e: e.activation(out=Mm[:, 0, 1, :], in_=pb[:, 128:256], func=AF.Copy), reads=[pb], writes=[Mm])
                        P.op("act", lambda e: e.activation(out=h["QKmT"][:], in_=pb[:, 256:384], func=AF.Copy), reads=[pb], writes=[h["QKmT"]])
                        Rr = h["Rr"]
                        P.op("pool", lambda e: e.tensor_add(out=Rr[:, 0, 0, :], in0=Mm[:, 0, 0, :], in1=ident[:]), reads=[Mm, ident], writes=[Rr])
                        P.op("pool", lambda e: e.tensor_add(out=Rr[:, 0, 1, :], in0=Mm[:, 0, 1, :], in1=ident[:]), reads=[Mm, ident], writes=[Rr])

                    def mk_dbl(it2, part):
                        a_, b2 = it2 % 2, (it2 + 1) % 2

                        def f(h, v):
                            Mm, Rr, pa, pb = h["Mm"], h["Rr"], h["pa"], h["pb"]
                            if part == 0:
                                P.op("pe", lambda e: e.matmul(pa[:, 0:128], lhsT=Mm[:, a_, 1, :], rhs=Mm[:, a_, 0, :], start=True, stop=True), reads=[Mm], writes=[pa])
                                P.op("pe", lambda e: e.matmul(pa[:, 128:256], lhsT=Mm[:, a_, 0, :], rhs=Mm[:, a_, 1, :], start=True, stop=True), reads=[Mm], writes=[pa])
                                P.op("dve", lambda e: e.tensor_copy(out=Mm[:, b2, :, :], in_=pa[:, 0:256].rearrange("p (a b) -> p a b", a=2)), reads=[pa], writes=[Mm])
                            else:
                                if it2 < 4:
                                    P.op("pe", lambda e: e.matmul(pb[:, 0:128], lhsT=Rr[:, a_, 1, :], rhs=Mm[:, b2, 0, :], start=True, stop=True), reads=[Rr, Mm], writes=[pb])
                                P.op("pe", lambda e: e.matmul(pb[:, 128:256], lhsT=Mm[:, b2, 0, :], rhs=Rr[:, a_, 1, :], start=True, stop=True), reads=[Rr, Mm], writes=[pb])
                                if it2 < 4:
                                    P.op("dve", lambda e: e.tensor_tensor(out=Rr[:, b2, :, :], in0=Rr[:, a_, :, :], in1=pb[:, 0:256].rearrange("p (a b) -> p a b", a=2), op=ALU.add),
                                         reads=[Rr, pb], writes=[Rr])
                                else:
                                    P.op("dve", lambda e: e.tensor_tensor(out=Rr[:, b2, 1, :], in0=Rr[:, a_, 1, :], in1=pb[:, 128:256], op=ALU.add), reads=[Rr, pb], writes=[Rr])
                        return f

                    def s_uw(h, v):
                        RT = h["Rr"][:, 1, 1, :]
                        pa, pb = h["pa"], h["pb"]
                        P.op("pool", lambda e: e.tensor_scalar(out=h["vb"][:], in0=v["vh"], scalar1=v["beta"], scalar2=None, op0=ALU.mult), reads=[q3, b_], writes=[h["vb"]])
                        P.op("dve", lambda e: e.tensor_scalar(out=h["xk"][:], in0=v["kh"], scalar1=v["beta"], scalar2=v["egc"], op0=ALU.mult, op1=ALU.mult),
                             reads=[q3, b_, gs], writes=[h["xk"]])
                        P.op("pe", lambda e: e.matmul(pa[:, 256:384], lhsT=RT, rhs=h["vb"][:], start=True, stop=True), reads=[h["Rr"], h["vb"]], writes=[pa])
                        P.op("pe", lambda e: e.matmul(pa[:, 384:512], lhsT=h["xk"][:], rhs=RT, start=True, stop=True), reads=[h["Rr"], h["xk"]], writes=[pa])
                        P.op("dve", lambda e: e.tensor_copy(out=h["ut"][:], in_=pa[:, 256:384]), reads=[pa], writes=[h["ut"]])
                        P.op("dve", lambda e: e.tensor_copy(out=h["wT"][:], in_=pa[:, 384:512]), reads=[pa], writes=[h["wT"]])
                        P.op("pool", lambda e: e.tensor_scalar(out=h["dq"][:], in0=identb[:], scalar1=v["egc"], scalar2=None, op0=ALU.mult), reads=[identb, gs], writes=[h["dq"]])
                        P.op("pe", lambda e: e.matmul(pb[:, 384:512], lhsT=v["qh"], rhs=h["dq"][:], start=True, stop=True), reads=[q3, h["dq"]], writes=[pb])
                        P.op("act", lambda e: e.activation(out=h["qgT"][:], in_=pb[:, 384:512], func=AF.Copy), reads=[pb], writes=[h["qgT"]])
                        P.op("pool", lambda e: e.tensor_scalar(out=h["kdec"][:], in0=v["kh"], scalar1=v["edec"], scalar2=None, op0=ALU.mult), reads=[q3, gs], writes=[h["kdec"]])

                    def mk_chunk(ch, part):
                        r0 = ch * 64
                        r = slice(r0, r0 + 64)

                        def f(h, v):
                            hh, pa, pb = h["hh"], h["pa"], h["pb"]
                            if part == 0:
                                P.op("pe", lambda e: e.matmul(pa[r, 0:128], lhsT=h["wT"][:, r], rhs=h["Sb"][:], start=True, stop=True), reads=[h["wT"], h["Sb"]], writes=[pa])
                                P.op("dve", lambda e: e.tensor_sub(out=h["vnew"][r, :], in0=h["ut"][r, :], in1=pa[r, 0:128]), reads=[h["ut"], pa], writes=[h["vnew"]])
                            elif part == 1:
                                P.op("pe", lambda e: e.matmul(pa[r, 128:256], lhsT=h["qgT"][:, r], rhs=h["Sb"][:], start=True, stop=False), reads=[h["qgT"], h["Sb"]], writes=[pa])
                                P.op("pe", lambda e: e.matmul(pa[r, 128:256], lhsT=h["QKmT"][r, r], rhs=h["vnew"][r, :], start=False, stop=True), reads=[h["QKmT"], h["vnew"]], writes=[pa])
                                P.op("pe", lambda e: e.matmul(pb[:, 0:128], lhsT=h["kdec"][r, :], rhs=h["vnew"][r, :], start=True, stop=True), reads=[h["kdec"], h["vnew"]], writes=[pb])
                                P.op("act", lambda e: e.activation(out=os_[r, hh, :], in_=pa[r, 128:256], func=AF.Copy), reads=[pa], writes=[os_])
                            else:
                                P.op("dve", lambda e: e.scalar_tensor_tensor(out=h["Sf"][:], in0=h["Sf"][:], scalar=egb[:, 0, 4 * ch + hh:4 * ch + hh + 1],
                                                                             in1=pb[:, 0:128], op0=ALU.mult, op1=ALU.add), reads=[h["Sf"], egb, pb], writes=[h["Sf"]])
                                P.op("act", lambda e: e.activation(out=h["Sb"][:], in_=h["Sf"][:], func=AF.Copy), reads=[h["Sf"]], writes=[h["Sb"]])
                        return f

                    stages = [s_tr, s_dec, s_gram, s_tr2]
                    for it2 in range(5):
                        stages += [mk_dbl(it2, 0), mk_dbl(it2, 1)]
                    stages.append(s_uw)
                    for ch in ((0, 1) if d == 0 else (1, 0)):
                        stages += [mk_chunk(ch, 0), mk_chunk(ch, 1), mk_chunk(ch, 2)]
                    hvs = [hv(h) for h in H]
                    for stg_ in stages:
                        for hi_ in range(4):
                            stg_(H[hi_], hvs[hi_])
                    if tt >= 2 or with_ctx:
                        P.dma("act", OD[d, tok0:tok0 + 128, :], os_[:].rearrange("p a b -> p (a b)"), reads=[os_], writes=[dOD])
            P.barrier()
            A.reset(b_mark)
            o2 = [A.alloc([2, 512]) for _ in range(2)]
            zt = [A.alloc([512]) for _ in range(2)]
            sqr = A.alloc([512])
            rn = A.alloc([2, 4])
            ybt = A.alloc([512], BF16)
            ybT = [A.alloc([4, 128], BF16) for _ in range(2)]
            for tt in range(NT):
                if tt < 2 and not with_ctx:
                    continue
                tok0 = tt * 128
                o_ = o2[tt % 2]
                z_ = zt[tt % 2]
                for d in range(2):
                    P.dma("sp", o_[:, d, :], OD[d, tok0:tok0 + 128, :], reads=[dOD], writes=[o_])
                P.dma("sp", z_[:], TM[tok0:tok0 + 128, 0:512], reads=[dTM], writes=[z_])
                P.op("pool", lambda e, o_=o_: e.tensor_add(out=o_[:, 0, :], in0=o_[:, 0, :], in1=o_[:, 1, :]), reads=[o_], writes=[o_])
                P.op("act", lambda e, o_=o_: e.activation(out=sqr[:], in_=o_[:, 0, :], func=AF.Square), reads=[o_], writes=[sqr])
                P.op("dve", lambda e: e.reduce_sum(out=rn[:, 0, :], in_=sqr[:].rearrange("p (h d) -> p h d", h=4), axis=AX.X), reads=[sqr], writes=[rn])
                P.op("act", lambda e: e.activation(out=rn[:, 1, :], in_=rn[:, 0, :], func=AF.Sqrt, bias=EPS, scale=1.0 / 128), reads=[rn], writes=[rn])
                P.op("dve", lambda e: e.reciprocal(out=rn[:, 1, :], in_=rn[:, 1, :]), reads=[rn], writes=[rn])
                P.op("dve", lambda e, o_=o_: e.tensor_tensor(out=o_[:, 0, :].rearrange("p (h d) -> p h d", h=4), in0=o_[:, 0, :].rearrange("p (h d) -> p h d", h=4),
                                                            in1=rn[:, 1, :].unsqueeze(2).to_broadcast([128, 4, 128]), op=ALU.mult), reads=[o_, rn], writes=[o_])
                P.op("pool", lambda e, o_=o_: e.tensor_tensor(out=o_[:, 0, :].rearrange("p (h d) -> p h d", h=4), in0=o_[:, 0, :].rearrange("p (h d) -> p h d", h=4),
                                                             in1=ngB[:].unsqueeze(1).to_broadcast([128, 4, 128]), op=ALU.mult), reads=[o_, ngB], writes=[o_])
                P.op("act", lambda e, z_=z_: e.activation(out=z_[:], in_=z_[:], func=AF.Silu), reads=[z_], writes=[z_])
                P.op("dve", lambda e, o_=o_, z_=z_: e.tensor_tensor(out=ybt[:], in0=o_[:, 0, :], in1=z_[:], op=ALU.mult), reads=[o_, z_], writes=[ybt])
                pT = PS[7]
                yo = ybT[tt % 2]
                for k in range(4):
                    P.op("pe", lambda e, k=k: e.transpose(out=psbf(7)[:, k * 128:(k + 1) * 128], in_=ybt[:, k * 128:(k + 1) * 128], identity=identb[:]),
                         reads=[ybt, identb], writes=[pT])
                P.op("dve", lambda e, yo=yo: e.tensor_copy(out=yo[:], in_=psbf(7)[:, 0:512].rearrange("p (a b) -> p a b", a=4)), reads=[pT], writes=[yo])
                P.dma("act", YT[1, :, :, tok0:tok0 + 128], yo[:], reads=[yo], writes=[dYT])
            P.barrier()
            A.reset(l_mark)

            if stop == pre + 'B':
                break
            def load_w(dst, src_v, ncols, kk, stg, k0=0):
                cw_ = stg[0].ap.shape[-1]
                for ci in range((ncols + cw_ - 1) // cw_):
                    c0 = ci * cw_
                    cw = min(cw_, ncols - c0)
                    ws = stg[ci % 2]
                    P.dma("sp", ws[:, 0:kk, 0:cw], src_v[:, :, c0:c0 + cw], writes=[ws])
                    P.op("pool" if ci % 2 else "dve", lambda e, ws=ws, c0=c0, cw=cw: e.tensor_copy(
                        out=dst[:, k0:k0 + kk, c0:c0 + cw], in_=ws[:, 0:kk, 0:cw]), reads=[ws], writes=[dst])

            tok_lo = 0 if with_ctx else LC
            stg = [A.alloc([8, 256]) for _ in range(2)]
            wbr = [A.alloc([4, 1024], BF16) for _ in range(3)]
            for b, nm in enumerate(("w_branch_a", "w_branch_b", "w_branch_c")):
                load_w(wbr[b], W[nm][l].rearrange("(k p) n -> p k n", p=128), 1024, 4, stg)
            wo = A.alloc([8, 1024], BF16)
            load_w(wo, W["w_out"][l].rearrange("(k p) n -> p k n", p=128), 1024, 8, stg)
            yin = [A.alloc([3, 4, 512], BF16) for _ in range(1)]
            gin = A.alloc([24, 512], BF16)
            mT = A.alloc([8, 512], BF16)
            t1 = [A.alloc([512]) for _ in range(3)]
            xts = [A.alloc([1024]) for _ in range(2)]
            t2 = A.alloc([512])
            GTv = GT.rearrange("(c p) t -> p c t", p=128)
            t0 = tok_lo
            while t0 < T:
                ST = min(512, T - t0)
                if t0 < LC:
                    ST = min(ST, LC - t0)
                cnd = 1 if t0 < LC else 0
                yb_ = yin[0]
                for b in range(3):
                    P.dma("sp", yb_[:, b, :, 0:ST], YT[b, :, :, t0:t0 + ST], reads=[dYT], writes=[yb_])
                P.dma("sp", gin[:, :, 0:ST], GTv[:, :, t0:t0 + ST], reads=[dGT], writes=[gin])
                for oc in range(8):
                    for b in range(3):
                        ps = PS[b + 3 * (oc % 2)]
                        for k in range(4):
                            P.op("pe", lambda e, b=b, k=k, oc=oc, ps=ps, ST=ST, yb_=yb_: e.matmul(
                                ps[:, 0:ST], lhsT=wbr[b][:, k, oc * 128:(oc + 1) * 128], rhs=yb_[:, b, k, 0:ST], start=(k == 0), stop=(k == 3)),
                                reads=[wbr[b], yb_], writes=[ps])
                        P.op("dve", lambda e, b=b, oc=oc, ps=ps, ST=ST: e.tensor_tensor(
                            out=t1[b][:, 0:ST], in0=ps[:, 0:ST], in1=gin[:, b * 8 + oc, 0:ST], op=ALU.mult), reads=[ps, gin], writes=[t1[b]])
                    P.op("pool", lambda e, ST=ST: e.tensor_add(out=t1[0][:, 0:ST], in0=t1[0][:, 0:ST], in1=t1[1][:, 0:ST]), reads=[t1[0], t1[1]], writes=[t1[0]])
                    P.op("pool", lambda e, ST=ST, oc=oc: e.tensor_add(out=mT[:, oc, 0:ST], in0=t1[0][:, 0:ST], in1=t1[2][:, 0:ST]), reads=[t1[0], t1[2]], writes=[mT])
                for sub in range(ST // 128):
                    tok0 = t0 + sub * 128
                    xt = xts[sub % 2]
                    P.dma("sp", xt[:], Xr[tok0:tok0 + 128, :], reads=[dX], writes=[xt])
                    for n in range(2):
                        ps = PS[6 + n]
                        for k in range(8):
                            P.op("pe", lambda e, k=k, n=n, ps=ps, sub=sub: e.matmul(
                                ps[:], lhsT=mT[:, k, sub * 128:(sub + 1) * 128], rhs=wo[:, k, n * 512:(n + 1) * 512], start=(k == 0), stop=(k == 7)),
                                reads=[mT, wo], writes=[ps])
                        P.op("dve", lambda e, n=n, ps=ps, cnd=cnd: e.tensor_tensor(out=t2[:], in0=ps[:], in1=Grow[:, 0, cnd, n * 512:(n + 1) * 512], op=ALU.mult),
                             reads=[ps, Grow], writes=[t2])
                        P.op("pool", lambda e, n=n, xt=xt: e.tensor_add(out=xt[:, n * 512:(n + 1) * 512], in0=xt[:, n * 512:(n + 1) * 512], in1=t2[:]),
                             reads=[xt, t2], writes=[xt])
                    P.dma("act", X[tok0:tok0 + 128, :], xt[:], reads=[xt], writes=[dX])
                t0 += ST
            P.barrier()
            A.reset(l_mark)

            if stop == pre + 'M':
                break
            last = (l == depth - 1)
            stg = [A.alloc([8, 128]) for _ in range(2)]
            w1 = A.alloc([8, 4096], BF16)
            load_w(w1, W["w_ff1"][l].rearrange("(k p) n -> p k n", p=128), 4096, 8, stg)
            w2 = A.alloc([32, 1024], BF16)
            w2v = W["w_ff2"][l].rearrange("(k p) n -> p k n", p=128)
            for kq in range(4):
                load_w(w2, w2v[:, kq * 8:(kq + 1) * 8, :], 1024, 8, stg, k0=kq * 8)
            fg = A.alloc([1024])
            if last:
                P.dma("sp", fg[:], W["final_norm_g"].partition_broadcast(128), writes=[fg])
            xts = [A.alloc([1024]) for _ in range(2)]
            junk = A.alloc([1024], BF16)
            xn = A.alloc([1024], BF16)
            ss = A.alloc([4])
            h2T = A.alloc([8, 256], BF16)
            aT = A.alloc([32, 256], BF16)
            rl = [A.alloc([256]) for _ in range(2)]
            t2 = A.alloc([512])
            t0 = tok_lo
            while t0 < T:
                ST = 256
                cnd = 1 if t0 < LC else 0
                for sub in range(2):
                    tok0 = t0 + sub * 128
                    xt = xts[sub]
                    P.dma("sp", xt[:], X[tok0:tok0 + 128, :], reads=[dX], writes=[xt])
                    P.op("act", lambda e, xt=xt: e.activation(out=junk[:], in_=xt[:], func=AF.Square, accum_out=ss[:, 0:1]),
                         reads=[xt], writes=[junk, ss])
                    P.op("act", lambda e: e.activation(out=ss[:, 1:2], in_=ss[:, 0:1], func=AF.Sqrt, scale=1.0 / D, bias=EPS), reads=[ss], writes=[ss])
                    P.op("dve", lambda e: e.reciprocal(out=ss[:, 1:2], in_=ss[:, 1:2]), reads=[ss], writes=[ss])
                    P.op("act", lambda e, xt=xt: e.activation(out=xn[:], in_=xt[:], func=AF.Copy, scale=ss[:, 1:2]), reads=[xt, ss], writes=[xn])
                    pT = PS[7]
                    for k in range(8):
                        P.op("pe", lambda e, k=k: e.transpose(out=psbf(7)[:, k * 128:(k + 1) * 128], in_=xn[:, k * 128:(k + 1) * 128],
                                                               identity=identb[:]), reads=[xn, identb], writes=[pT])
                    for k in range(8):
                        if k % 2:
                            P.op("dve", lambda e, k=k, sub=sub, cnd=cnd: e.tensor_scalar(
                                out=h2T[:, k, sub * 128:(sub + 1) * 128], in0=psbf(7)[:, k * 128:(k + 1) * 128],
                                scalar1=AB[:, 1, k, cnd:cnd + 1], scalar2=modF[:, 24 + k, cnd:cnd + 1], op0=ALU.mult, op1=ALU.add),
                                reads=[pT, AB, modF], writes=[h2T])
                        else:
                            P.op("act", lambda e, k=k, sub=sub, cnd=cnd: e.activation(
                                out=h2T[:, k, sub * 128:(sub + 1) * 128], in_=psbf(7)[:, k * 128:(k + 1) * 128], func=AF.Identity,
                                scale=AB[:, 1, k, cnd:cnd + 1], bias=modF[:, 24 + k, cnd:cnd + 1]), reads=[pT, AB, modF], writes=[h2T])
                for oc in range(32):
                    ps = PS[oc % 4]
                    r_ = rl[oc % 2]
                    for k in range(8):
                        P.op("pe", lambda e, k=k, oc=oc, ps=ps: e.matmul(ps[:, 0:256], lhsT=w1[:, k, oc * 128:(oc + 1) * 128], rhs=h2T[:, k, :],
                                                                        start=(k == 0), stop=(k == 7)), reads=[w1, h2T], writes=[ps])
                    P.op("act", lambda e, ps=ps, r_=r_: e.activation(out=r_[:], in_=ps[:, 0:256], func=AF.Relu), reads=[ps], writes=[r_])
                    P.op("pool", lambda e, oc=oc, r_=r_: e.tensor_tensor(out=aT[:, oc, :], in0=r_[:], in1=r_[:], op=ALU.mult), reads=[r_], writes=[aT])
                for sub in range(2):
                    tok0 = t0 + sub * 128
                    xt = xts[sub]
                    for n in range(2):
                        ps = PS[4 + n]
                        for k in range(32):
                            P.op("pe", lambda e, k=k, n=n, ps=ps, sub=sub: e.matmul(
                                ps[:], lhsT=aT[:, k, sub * 128:(sub + 1) * 128], rhs=w2[:, k, n * 512:(n + 1) * 512], start=(k == 0), stop=(k == 31)),
                                reads=[aT, w2], writes=[ps])
                        P.op("dve", lambda e, n=n, ps=ps, cnd=cnd: e.tensor_tensor(out=t2[:], in0=ps[:], in1=Grow[:, 1, cnd, n * 512:(n + 1) * 512], op=ALU.mult),
                             reads=[ps, Grow], writes=[t2])
                        P.op("pool", lambda e, n=n, xt=xt: e.tensor_add(out=xt[:, n * 512:(n + 1) * 512], in0=xt[:, n * 512:(n + 1) * 512], in1=t2[:]),
                             reads=[xt, t2], writes=[xt])
                    if not last:
                        P.dma("act", X[tok0:tok0 + 128, :], xt[:], reads=[xt], writes=[dX])
                    else:
                        P.op("act", lambda e, xt=xt: e.activation(out=junk[:], in_=xt[:], func=AF.Square, accum_out=ss[:, 2:3]),
                             reads=[xt], writes=[junk, ss])
                        P.op("act", lambda e: e.activation(out=ss[:, 3:4], in_=ss[:, 2:3], func=AF.Sqrt, scale=1.0 / D, bias=EPS), reads=[ss], writes=[ss])
                        P.op("dve", lambda e: e.reciprocal(out=ss[:, 3:4], in_=ss[:, 3:4]), reads=[ss], writes=[ss])
                        P.op("dve", lambda e, xt=xt: e.scalar_tensor_tensor(out=xt[:], in0=xt[:], scalar=ss[:, 3:4], in1=fg[:], op0=ALU.mult, op1=ALU.mult),
                             reads=[xt, ss, fg], writes=[xt])
                        P.dma("act", out_d[tok0 - LC:tok0 - LC + 128, :], xt[:], reads=[xt])
                t0 += ST
            P.barrier()
            A.reset(base_mark)
            if stop == pre + 'F':
                break

        P.emit()
    return nc


WSHAPES = {
    'norm1_g': (DEPTH, D), 'norm2_g': (DEPTH, D), 'w_mod': (DEPTH, D, 6 * D), 'b_mod': (DEPTH, 6 * D),
    'w_in': (DEPTH, D, D_IN),
    'ssm_lam_re': (DEPTH, 2, 32, 64), 'ssm_lam_im': (DEPTH, 2, 32, 64), 'ssm_log_dt': (DEPTH, 2, 32),
    'ssm_b_re': (DEPTH, 2, 32, 64, 16), 'ssm_b_im': (DEPTH, 2, 32, 64, 16),
    'ssm_c_re': (DEPTH, 2, 32, 16, 64), 'ssm_c_im': (DEPTH, 2, 32, 16, 64),
    'ssm_d': (DEPTH, 512), 'ssm_w_glu': (DEPTH, 512, 512),
    'dn_conv_w': (DEPTH, 5, 1536), 'dn_a_log': (DEPTH, 2, 4), 'dn_dt_bias': (DEPTH, 2, 4), 'dn_norm_g': (DEPTH, 128),
    'attn_sink': (DEPTH, 8),
    'w_branch_a': (DEPTH, 512, D), 'w_branch_b': (DEPTH, 512, D), 'w_branch_c': (DEPTH, 512, D),
    'w_out': (DEPTH, D, D), 'w_ff1': (DEPTH, D, 4 * D), 'w_ff2': (DEPTH, 4 * D, D), 'final_norm_g': (D,),
}


def make_in_maps(inputs, nb, L):
    x = np.asarray(inputs['x'], np.float32)
    ctx = np.asarray(inputs['ctx'], np.float32)
    c = np.asarray(inputs['c'], np.float32)
    c_ctx = np.asarray(inputs['c_ctx'], np.float32)
    shared = {nm: np.ascontiguousarray(np.asarray(inputs[nm], np.float32)) for nm in WSHAPES}
    shared["ident"] = np.eye(128, dtype=np.float32)
    q_ = np.arange(128)[:, None]
    j_ = np.arange(128)[None, :]
    mk = np.zeros((128, 3, 128), np.float32)
    mk[:, 0, :] = np.where(j_ >= q_, 0.0, -30000.0)
    mk[:, 1, :] = np.where(j_ <= q_, 0.0, -30000.0)
    mk[:, 2, :] = -30000.0
    shared["masks"] = mk
    same = (q_ // 64) == (j_ // 64)
    dc = np.zeros((128, 7, 128), np.float32)
    dc[:, 0, :] = (same & (q_ <= j_))
    dc[:, 1, :] = (same & (q_ >= j_))
    dc[:, 2, :] = same
    dc[:, 3, :] = (q_ != j_)
    dc[:, 4, :] = np.where(same & (j_ < q_), 0.0, -30000.0)
    dc[:, 5, :] = np.where(same & (j_ > q_), 0.0, -30000.0)
    shared["dncon"] = dc
    sl = np.zeros((128, 2, 128), np.float32)
    sl[0, 0, :] = 1.0
    sl[64, 1, :] = 1.0
    shared["dnsel"] = sl
    NL = L // 128
    rp = np.zeros((128, NL + 1), np.float32)
    for i in range(NL):
        rp[:, i] = 2 * i + (np.arange(128) >= 64)
    rp[:, NL] = np.arange(128) % 64
    shared["ropepos"] = rp
    shared["invfreq"] = np.tile((10000.0 ** (-np.arange(16, dtype=np.float32) / 16))[None, :], (128, 1)).astype(np.float32)
    maps = []
    for b in range(nb):
        m = dict(shared)
        m["xin"] = np.ascontiguousarray(np.concatenate([ctx[b], x[b]], axis=0))
        m["cc"] = np.ascontiguousarray(np.stack([c[b], c_ctx], axis=0))
        maps.append(m)
    return maps


_NC_CACHE = {}


def kernel(**inputs):
    x = np.asarray(inputs['x'])
    nb, L = x.shape[0], x.shape[1]
    if L not in _NC_CACHE:
        _NC_CACHE[L] = build(L)
    nc = _NC_CACHE[L]
    maps = make_in_maps(inputs, nb, L)
    res = run_bass_kernel_spmd(nc, maps, core_ids=list(range(nb)))
    return np.stack([np.asarray(r["out"], np.float32) for r in res.results], axis=0)
```
